# Optimizing a Trainium2 kernel written in Bass

```python
import jax, jax.numpy as jnp
from jax import lax
import numpy as np

D_MODEL = 2048
BATCH = 4
SEQ = 4096
DEPTH = 4

GRID_W = 64
CTX_LEN = 256

RW_WIDTH = D_MODEL // 4
GQ_WIDTH = D_MODEL // 2
NA_WIDTH = D_MODEL // 4

RW_HEAD_DIM = 64
RW_HEADS = RW_WIDTH // RW_HEAD_DIM
DECAY_LORA = max(32, int(round(1.8 * D_MODEL ** 0.5 / 32)) * 32)
ICLR_LORA = max(32, int(round(1.8 * D_MODEL ** 0.5 / 32)) * 32)
GATE_LORA = max(32, int(round(0.6 * D_MODEL ** 0.8 / 32)) * 32)
RW_GN_EPS = 64e-5
RW_SIZES = (RW_WIDTH, RW_WIDTH, RW_WIDTH, DECAY_LORA, DECAY_LORA, ICLR_LORA, ICLR_LORA, GATE_LORA)
RW_COLS = sum(RW_SIZES)

GQ_HEAD_DIM = 128
GQ_HEADS = GQ_WIDTH // GQ_HEAD_DIM
GQ_GROUP = 4
GQ_KV_HEADS = GQ_HEADS // GQ_GROUP
GQ_KV_WIDTH = GQ_KV_HEADS * GQ_HEAD_DIM
GQ_COLS = GQ_WIDTH + 2 * GQ_KV_WIDTH
ROPE_THETA = 10000.0
Q_BLOCK = 128

NA_HEAD_DIM = 64
NA_HEADS = NA_WIDTH // NA_HEAD_DIM
NA_WIN_ROWS = 8
NA_WIN_COLS = 16
NA_COLS = 3 * NA_WIDTH

GATE_COLS = 3 * D_MODEL
IN_SIZES = (RW_COLS, GQ_COLS, NA_COLS, GATE_COLS)
N_IN = sum(IN_SIZES)

D_FF = -(-8 * D_MODEL // (3 * 256)) * 256
NORM_EPS = 1e-6

kernel_name = 'hybrid_rwkv7_gqa_natten_dit_block'


def rms_norm(x, g):
    xf = x.astype(jnp.float32)
    y = xf * lax.rsqrt(jnp.mean(xf * xf, axis=-1, keepdims=True) + NORM_EPS)
    return (y * g.astype(jnp.float32)).astype(x.dtype)


def split_cols(z, sizes):
    offs = np.cumsum((0,) + tuple(sizes))
    return [z[..., int(offs[i]):int(offs[i + 1])] for i in range(len(sizes))]


def heads(t, n):
    b, s, _ = t.shape
    return t.reshape(b, s, n, -1).transpose(0, 2, 1, 3)


def merge_heads(t):
    b, h, s, d = t.shape
    return t.transpose(0, 2, 1, 3).reshape(b, s, h * d)


def axial_rope(n_tok, head_dim):
    n_freq = head_dim // 4
    inv = ROPE_THETA ** (-jnp.arange(n_freq, dtype=jnp.float32) / n_freq)
    t = jnp.arange(n_tok, dtype=jnp.int32)
    row = (t // GRID_W).astype(jnp.float32)
    col = (t % GRID_W).astype(jnp.float32)
    ang = jnp.concatenate([row[:, None] * inv, col[:, None] * inv], axis=-1)
    return jnp.cos(ang), jnp.sin(ang)


def apply_rope(x, cos, sin):
    x1 = x[..., 0::2].astype(jnp.float32)
    x2 = x[..., 1::2].astype(jnp.float32)
    out = jnp.stack([x1 * cos - x2 * sin, x1 * sin + x2 * cos], axis=-1)
    return out.reshape(x.shape).astype(x.dtype)


def centred_shift(z, mu_prev, mu_next):
    zp = jnp.pad(z, ((0, 0), (1, 0), (0, 0)))[:, :-1]
    zn = jnp.pad(z, ((0, 0), (0, 1), (0, 0)))[:, 1:]
    return z + mu_prev * (zp - z) + mu_next * (zn - z)


def rwkv_inputs(z, w0, w_up, a0, a_up, k_k, k_a):
    r, k, v, wf, wb, af, ab, gd = split_cols(z, RW_SIZES)
    b, t, _ = r.shape
    hd = lambda u: u.reshape(b, t, RW_HEADS, RW_HEAD_DIM).astype(jnp.float32)
    kk = hd(k * k_k)
    kk = kk / jnp.maximum(jnp.sqrt(jnp.sum(kk * kk, axis=-1, keepdims=True)), 1e-12)
    dirs = []
    for d, (wl, al) in enumerate(((wf, af), (wb, ab))):
        w_raw = (w0[d] + jnp.tanh(wl) @ w_up[d]).astype(jnp.float32)
        decay = jnp.exp(-jnp.exp(-jax.nn.softplus(-w_raw) - 0.5))
        a = jax.nn.sigmoid(a0[d] + al @ a_up[d])
        k_d = k * (1 + (a - 1) * k_a)
        dirs.append((hd(decay), hd(k_d), -kk, kk * hd(a)))
    return hd(r), hd(v), dirs, gd


def rwkv7_scan(r, w, k, v, a, b, s0, reverse, emit):
    def step(s, inp):
        w_t, k_t, v_t, a_t, b_t = inp[:5]
        sa = jnp.einsum('bhvk,bhk->bhv', s, a_t)
        s = s * w_t[:, :, None, :] + sa[..., None] * b_t[:, :, None, :] + v_t[..., None] * k_t[:, :, None, :]
        if emit:
            return s, jnp.einsum('bhvk,bhk->bhv', s, inp[5])
        return s, None
    seqs = (w, k, v, a, b, r) if emit else (w, k, v, a, b)
    xs = tuple(jnp.swapaxes(u, 0, 1) for u in seqs)
    s, out = lax.scan(step, s0, xs, reverse=reverse)
    return (jnp.swapaxes(out, 0, 1) if emit else None), s


def rwkv_output(r, v, k_f, k_b, wkv, gd, g_up, r_k, ln_w, ln_b, dtype):
    b, t, h, n = wkv.shape
    mu = jnp.mean(wkv, axis=-1, keepdims=True)
    var = jnp.mean(jnp.square(wkv - mu), axis=-1, keepdims=True)
    y = ((wkv - mu) * lax.rsqrt(var + RW_GN_EPS)).reshape(b, t, h * n) * ln_w + ln_b
    bonus = (jnp.sum(r * k_f * r_k, axis=-1, keepdims=True) + jnp.sum(r * k_b * r_k, axis=-1, keepdims=True)) * v
    y = y + bonus.reshape(b, t, h * n)
    g = jax.nn.sigmoid(gd) @ g_up
    return (y * g).astype(dtype)


def gqa_q(zq, gain, rope):
    q = rms_norm(heads(zq, GQ_HEADS), gain)
    return apply_rope(q, *rope) if rope is not None else q


def gqa_kv(zkv, gain, rope):
    k, v = split_cols(zkv, (GQ_KV_WIDTH, GQ_KV_WIDTH))
    k = rms_norm(heads(k, GQ_KV_HEADS), gain)
    if rope is not None:
        k = apply_rope(k, *rope)
    return k, heads(v, GQ_KV_HEADS)


def dense_attend(q, k, v):
    b, hq, s, dh = q.shape
    hk = k.shape[1]
    qg = q.reshape(b, hk, hq // hk, s, dh)
    sc = jnp.einsum('bhgqd,bhkd->bhgqk', qg, k).astype(jnp.float32) * dh ** -0.5
    pr = jax.nn.softmax(sc, axis=-1).astype(v.dtype)
    o = jnp.einsum('bhgqk,bhkd->bhgqd', pr, v).reshape(b, hq, s, dh)
    return merge_heads(o)


def gqa_latent(q, k, v, kc, vc):
    b, hq, t, dh = q.shape
    g = hq // GQ_KV_HEADS
    nb = t // Q_BLOCK
    qb = jnp.moveaxis(q.reshape(b, GQ_KV_HEADS, g, nb, Q_BLOCK, dh), 3, 0)
    scale = dh ** -0.5
    def block(qi):
        s = jnp.concatenate([jnp.einsum('bhgqd,bhkd->bhgqk', qi, k),
                             jnp.einsum('bhgqd,bhkd->bhgqk', qi, kc)], axis=-1).astype(jnp.float32) * scale
        pr = jax.nn.softmax(s, axis=-1).astype(v.dtype)
        return (jnp.einsum('bhgqk,bhkd->bhgqd', pr[..., :t], v)
                + jnp.einsum('bhgqk,bhkd->bhgqd', pr[..., t:], vc))
    o = lax.map(block, qb)
    o = jnp.moveaxis(o, 0, 3).reshape(b, hq, t, dh)
    return merge_heads(o)


def na_latent(q, k, v, kc, vc, rpb):
    b, h, t, dh = q.shape
    rows = t // GRID_W
    wr = min(NA_WIN_ROWS, rows)
    wc = NA_WIN_COLS
    cols = np.arange(GRID_W)
    cstart = np.clip(cols - wc // 2, 0, GRID_W - wc)
    col_idx = cstart[:, None] + np.arange(wc)[None, :]
    col_off = col_idx - cols[:, None] + (NA_WIN_COLS - 1)
    rpb_cols = rpb[:, :, col_off]
    kg = k.reshape(b, h, rows, GRID_W, dh)
    vg = v.reshape(b, h, rows, GRID_W, dh)
    qg = jnp.moveaxis(q.reshape(b, h, rows, GRID_W, dh), 2, 0)
    scale = dh ** -0.5
    nwin = wr * wc
    def one_row(args):
        i, q_row = args
        rs = jnp.clip(i - wr // 2, 0, rows - wr)
        k_win = lax.dynamic_slice_in_dim(kg, rs, wr, axis=2)[:, :, :, col_idx]
        v_win = lax.dynamic_slice_in_dim(vg, rs, wr, axis=2)[:, :, :, col_idx]
        row_off = rs + jnp.arange(wr) - i + (NA_WIN_ROWS - 1)
        bias = jnp.take(rpb_cols, row_off, axis=1).transpose(0, 2, 1, 3)
        s_lat = jnp.einsum('bhqd,bhrqcd->bhqrc', q_row, k_win).astype(jnp.float32) * scale + bias
        s_ctx = jnp.einsum('bhqd,bhcd->bhqc', q_row, kc).astype(jnp.float32) * scale
        s = jnp.concatenate([s_lat.reshape(b, h, GRID_W, nwin), s_ctx], axis=-1)
        pr = jax.nn.softmax(s, axis=-1).astype(v.dtype)
        p_lat = pr[..., :nwin].reshape(b, h, GRID_W, wr, wc)
        return (jnp.einsum('bhqrc,bhrqcd->bhqd', p_lat, v_win)
                + jnp.einsum('bhqc,bhcd->bhqd', pr[..., nwin:], vc))
    o = lax.map(one_row, (jnp.arange(rows, dtype=jnp.int32), qg))
    o = jnp.moveaxis(o, 0, 2).reshape(b, h, t, dh)
    return merge_heads(o)


def merge_branches(ya, yb, yc, zg, w_br_a, w_br_b, w_br_c, w_out):
    ga, gb, gc = split_cols(jax.nn.sigmoid(zg), (D_MODEL, D_MODEL, D_MODEL))
    m = ga * (ya @ w_br_a) + gb * (yb @ w_br_b) + gc * (yc @ w_br_c)
    return m @ w_out


def swiglu(h, w1, w3, w2):
    return (jax.nn.silu(h @ w1) * (h @ w3)) @ w2


def mixer(h, hc, p, rope, need_ctx):
    dtype = h.dtype
    zr, zq, zn, zg = split_cols(h @ p['w_in'], IN_SIZES)
    zrc, zqc, znc, zgc = split_cols(hc @ p['w_in'], IN_SIZES)

    lora = (p['rw_w0'], p['rw_w_up'], p['rw_a0'], p['rw_a_up'], p['rw_k_k'], p['rw_k_a'])
    r, v, dirs, gd = rwkv_inputs(centred_shift(zr, p['rw_mu_prev'], p['rw_mu_next']), *lora)
    r_c, v_c, dirs_c, gd_c = rwkv_inputs(centred_shift(zrc, p['rw_mu_prev'], p['rw_mu_next']), *lora)
    s_zero = jnp.zeros((hc.shape[0], RW_HEADS, RW_HEAD_DIM, RW_HEAD_DIM), jnp.float32)
    outs, outs_c = [], []
    for d, rev in enumerate((False, True)):
        dec, kd, av, bv = dirs_c[d]
        oc, s_ctx = rwkv7_scan(r_c, dec, kd, v_c, av, bv, s_zero, rev, need_ctx)
        outs_c.append(oc)
        dec, kd, av, bv = dirs[d]
        o, _ = rwkv7_scan(r, dec, kd, v, av, bv, s_ctx, rev, True)
        outs.append(o)
    rw_tail = (p['rw_g_up'], p['rw_r_k'], p['rw_ln_w'], p['rw_ln_b'], dtype)
    ya = rwkv_output(r, v, dirs[0][1], dirs[1][1], outs[0] + outs[1], gd, *rw_tail)

    gk, gv = gqa_kv(zq[..., GQ_WIDTH:], p['gq_k_norm'], rope)
    gkc, gvc = gqa_kv(zqc[..., GQ_WIDTH:], p['gq_k_norm'], None)
    yb = gqa_latent(gqa_q(zq[..., :GQ_WIDTH], p['gq_q_norm'], rope), gk, gv, gkc, gvc)

    nq, nk, nv = (heads(u, NA_HEADS) for u in split_cols(zn, (NA_WIDTH, NA_WIDTH, NA_WIDTH)))
    nkc = heads(znc[..., NA_WIDTH:2 * NA_WIDTH], NA_HEADS)
    nvc = heads(znc[..., 2 * NA_WIDTH:], NA_HEADS)
    yc = na_latent(nq, nk, nv, nkc, nvc, p['na_rpb'])

    br = (p['w_br_a'], p['w_br_b'], p['w_br_c'], p['w_out'])
    y = merge_branches(ya, yb, yc, zg, *br)
    if not need_ctx:
        return y, None
    ya_c = rwkv_output(r_c, v_c, dirs_c[0][1], dirs_c[1][1], outs_c[0] + outs_c[1], gd_c, *rw_tail)
    yb_c = dense_attend(gqa_q(zqc[..., :GQ_WIDTH], p['gq_q_norm'], None), gkc, gvc)
    yc_c = dense_attend(heads(znc[..., :NA_WIDTH], NA_HEADS), nkc, nvc)
    return y, merge_branches(ya_c, yb_c, yc_c, zgc, *br)


def setup_inputs(seed: int = 0) -> dict:
    key = jax.random.key(seed)
    ks = iter(jax.random.split(key, 40))
    L, D = DEPTH, D_MODEL
    f32 = jnp.float32
    def normal(shape, scale):
        return jax.random.normal(next(ks), shape, f32) * scale
    def unif(shape, lo, hi):
        return jax.random.uniform(next(ks), shape, f32, lo, hi)
    return {
        'x': normal((BATCH, SEQ, D), 1.0),
        'c': normal((BATCH, D), 1.0),
        'ctx': normal((BATCH, CTX_LEN, D), 1.0),
        'c_ctx': normal((D,), 1.0),
        'ada_w': normal((L, D, 6 * D), 0.5 * D ** -0.5),
        'ada_b': normal((L, 6 * D), 0.02),
        'norm1': 1.0 + normal((L, D), 0.05),
        'norm2': 1.0 + normal((L, D), 0.05),
        'w_in': normal((L, D, N_IN), D ** -0.5),
        'rw_mu_prev': unif((L, RW_COLS), 0.0, 0.5),
        'rw_mu_next': unif((L, RW_COLS), 0.0, 0.5),
        'rw_w0': unif((L, 2, RW_WIDTH), -6.0, 0.0),
        'rw_w_up': normal((L, 2, DECAY_LORA, RW_WIDTH), 0.1),
        'rw_a0': normal((L, 2, RW_WIDTH), 0.1),
        'rw_a_up': normal((L, 2, ICLR_LORA, RW_WIDTH), ICLR_LORA ** -0.5),
        'rw_g_up': normal((L, GATE_LORA, RW_WIDTH), GATE_LORA ** -0.5),
        'rw_k_k': 0.85 + normal((L, RW_WIDTH), 0.05),
        'rw_k_a': 1.0 + normal((L, RW_WIDTH), 0.05),
        'rw_r_k': normal((L, RW_HEADS, RW_HEAD_DIM), 0.1),
        'rw_ln_w': 1.0 + normal((L, RW_WIDTH), 0.05),
        'rw_ln_b': normal((L, RW_WIDTH), 0.02),
        'gq_q_norm': 1.0 + normal((L, GQ_HEAD_DIM), 0.05),
        'gq_k_norm': 1.0 + normal((L, GQ_HEAD_DIM), 0.05),
        'na_rpb': normal((L, NA_HEADS, 2 * NA_WIN_ROWS - 1, 2 * NA_WIN_COLS - 1), 0.5),
        'w_br_a': normal((L, RW_WIDTH, D), RW_WIDTH ** -0.5),
        'w_br_b': normal((L, GQ_WIDTH, D), GQ_WIDTH ** -0.5),
        'w_br_c': normal((L, NA_WIDTH, D), NA_WIDTH ** -0.5),
        'w_out': normal((L, D, D), D ** -0.5),
        'ffn_w1': normal((L, D, D_FF), D ** -0.5),
        'ffn_w3': normal((L, D, D_FF), D ** -0.5),
        'ffn_w2': normal((L, D_FF, D), D_FF ** -0.5),
        'final_norm': 1.0 + normal((D,), 0.05),
    }


def reference(x, c, ctx, c_ctx, ada_w, ada_b, norm1, norm2, w_in, rw_mu_prev, rw_mu_next, rw_w0, rw_w_up,
              rw_a0, rw_a_up, rw_g_up, rw_k_k, rw_k_a, rw_r_k, rw_ln_w, rw_ln_b, gq_q_norm, gq_k_norm, na_rpb,
              w_br_a, w_br_b, w_br_c, w_out, ffn_w1, ffn_w3, ffn_w2, final_norm):
    rope = axial_rope(x.shape[1], GQ_HEAD_DIM)
    xc = ctx
    for l in range(DEPTH):
        need_ctx = l < DEPTH - 1
        p = {'w_in': w_in[l], 'rw_mu_prev': rw_mu_prev[l], 'rw_mu_next': rw_mu_next[l],
             'rw_w0': rw_w0[l], 'rw_w_up': rw_w_up[l], 'rw_a0': rw_a0[l], 'rw_a_up': rw_a_up[l],
             'rw_g_up': rw_g_up[l], 'rw_k_k': rw_k_k[l], 'rw_k_a': rw_k_a[l], 'rw_r_k': rw_r_k[l],
             'rw_ln_w': rw_ln_w[l], 'rw_ln_b': rw_ln_b[l], 'gq_q_norm': gq_q_norm[l],
             'gq_k_norm': gq_k_norm[l], 'na_rpb': na_rpb[l], 'w_br_a': w_br_a[l], 'w_br_b': w_br_b[l],
             'w_br_c': w_br_c[l], 'w_out': w_out[l]}
        mod = jax.nn.silu(c) @ ada_w[l] + ada_b[l]
        mod_c = jax.nn.silu(c_ctx) @ ada_w[l] + ada_b[l]
        sh1, sc1, g1, sh2, sc2, g2 = jnp.split(mod[:, None, :], 6, axis=-1)
        sh1c, sc1c, g1c, sh2c, sc2c, g2c = jnp.split(mod_c, 6, axis=-1)
        h = rms_norm(x, norm1[l]) * (1 + sc1) + sh1
        hc = rms_norm(xc, norm1[l]) * (1 + sc1c) + sh1c
        y, yc = mixer(h, hc, p, rope, need_ctx)
        x = x + g1 * y
        h = rms_norm(x, norm2[l]) * (1 + sc2) + sh2
        x = x + g2 * swiglu(h, ffn_w1[l], ffn_w3[l], ffn_w2[l])
        if need_ctx:
            xc = xc + g1c * yc
            hc = rms_norm(xc, norm2[l]) * (1 + sc2c) + sh2c
            xc = xc + g2c * swiglu(hc, ffn_w1[l], ffn_w3[l], ffn_w2[l])
    return rms_norm(x, final_norm)
```

```python
import numpy as np
from contextlib import ExitStack
import concourse.bass as bass
import concourse.mybir as mybir
from concourse.bass_utils import run_bass_kernel_spmd

F32 = mybir.dt.float32
BF16 = mybir.dt.bfloat16
AF = mybir.ActivationFunctionType
ALU = mybir.AluOpType
AX = mybir.AxisListType

D = 2048
KC = 16
SEQ = 4096
CTX = 256
DEPTH = 4
NB = 4
TPC = 2176
NORM_EPS = 1e-6


class Sched:
    ENG = ('pe', 'dve', 'act', 'pool', 'sp')

    def __init__(self, nc):
        self.nc = nc
        self.ops = {e: [] for e in self.ENG}
        self.cnt = {}
        self.waited = {}
        self.last_w = {}
        self.readers = {}
        self.semnames = []

    def _sem(self, name):
        if name not in self.cnt:
            self.cnt[name] = 0
            self.semnames.append(name)
        return name

    def op(self, eng, fn, reads=(), writes=(), dma=None, pe_sync=False):
        al = getattr(self, 'alias', None)
        if al:
            reads = [x for k in reads for x in al.get(k, [k])]
        if getattr(self, 'psum_excl', False):
            nb = lambda k: k[:3] if (k.startswith('pp') and k[2].isdigit()) else k
            reads = [nb(k) for k in reads]
            writes = [nb(k) for k in writes]
            writes = list(writes) + [k for k in reads if k.startswith('pp') and k[2].isdigit() and k not in writes]
        deps = []
        for k in reads:
            if k in self.last_w:
                deps.append(self.last_w[k])
        for k in writes:
            if k in self.last_w:
                deps.append(self.last_w[k])
            deps.extend(self.readers.get(k, ()))
        waits = {}
        for (s, v) in deps:
            if s == 'E_pe' and eng == 'pe' and dma is None:
                continue
            if self.waited.get((eng, s), 0) < v:
                waits[s] = max(waits.get(s, 0), v)
        if pe_sync and self.cnt.get('E_pe', 0) > 0:
            waits['E_pe'] = self.cnt['E_pe']
        for s, v in waits.items():
            self.waited[(eng, s)] = v
        if dma is None:
            s = self._sem('E_' + eng)
            inc = 1
        else:
            s = self._sem('D_' + dma)
            inc = 16
        self.cnt[s] += inc
        tok = (s, self.cnt[s])
        self.ops[eng].append((sorted(waits.items()), fn, s, inc))
        for k in reads:
            self.readers.setdefault(k, []).append(tok)
        for k in writes:
            self.last_w[k] = tok
            self.readers[k] = []
        return tok

    def wait_all(self, eng, keys):
        waits = {}
        for k in keys:
            if k in self.last_w:
                s, v = self.last_w[k]
                if self.waited.get((eng, s), 0) < v:
                    waits[s] = max(waits.get(s, 0), v)
        for s, v in waits.items():
            self.waited[(eng, s)] = v
        self.ops[eng].append((sorted(waits.items()), None, None, 0))

    def emit(self, stack):
        nc = self.nc
        sems = {}
        for name in self.semnames:
            sems[name] = stack.enter_context(nc.semaphore(name))
        block = stack.enter_context(nc.Block())

        def run(eng_name):
            def body(e):
                for waits, fn, s, inc in self.ops[eng_name]:
                    for ws, wv in waits:
                        e.wait_ge(sems[ws], wv)
                    if fn is not None:
                        ins = fn(e)
                        ins.then_inc(sems[s], inc)
            return body
        if self.ops['sp']:
            block.sync(run('sp'))
        if self.ops['pe']:
            block.tensor(run('pe'))
        if self.ops['dve']:
            block.vector(run('dve'))
        if self.ops['act']:
            block.scalar(run('act'))
        if self.ops['pool']:
            block.gpsimd(run('pool'))


def token_tiles(T, step):
    out = []
    t = 0
    while t < T:
        n = min(step, T - t)
        out.append((t, n))
        t += n
    return out


def emit_mod(S, nc, st, cT, adaw, adab, nfc, x32, ps_mod, consts, prefix):
    c32 = st.enter_context(nc.sbuf_tensor(prefix + "c32", [128, KC, 2], F32))
    sc = st.enter_context(nc.sbuf_tensor(prefix + "sc", [128, KC, 2], F32))
    bsb = st.enter_context(nc.sbuf_tensor(prefix + "adab", [128, nfc], F32))
    mod = st.enter_context(nc.sbuf_tensor(prefix + "mod", [128, nfc, 2], F32))
    S.op('sp', lambda e: e.dma_start(out=c32[:], in_=cT), writes=[prefix + 'c32'], dma=prefix + 'c')
    S.op('sp', lambda e: e.dma_start(out=bsb[:], in_=adab), writes=[prefix + 'adab'], dma=prefix + 'b')
    S.op('act', lambda e: e.activation(out=sc[:], in_=c32[:], func=AF.Silu), reads=[prefix + 'c32'], writes=[prefix + 'sc'])
    ngrp = nfc // 2
    for g in range(ngrp):
        slot = g % 2
        S.op('sp', lambda e, g=g, slot=slot: e.dma_start(out=x32[:, slot, :, :], in_=adaw[:, g * 256:(g + 1) * 256].rearrange("(k p) n -> p k n", p=128)),
             writes=[f'x32_{slot}'], dma=f'x32_{slot}')
        for j in range(2):
            fc = g * 2 + j
            def mm(e, slot=slot, j=j):
                for k in range(KC):
                    ins = e.matmul(ps_mod[:, 0:2], lhsT=x32[:, slot, k, j * 128:(j + 1) * 128], rhs=sc[:, k, :], start=(k == 0), stop=(k == KC - 1))
                return ins
            S.op('pe', mm, reads=[f'x32_{slot}', prefix + 'sc'], writes=['ps_mod'])
            S.op('dve', lambda e, fc=fc: e.tensor_scalar(out=mod[:, fc, :], in0=ps_mod[:, 0:2], scalar1=bsb[:, fc:fc + 1], scalar2=None, op0=ALU.add),
                 reads=['ps_mod', prefix + 'adab'], writes=[prefix + 'mod'])
    return mod


def emit_norm_tile(S, nc, x32, slot, n, gm, sh, col, hT, t0, onesD, eps_t, sq, ps_ss, rstd, tmp, hkey='hT'):
    for k in range(KC):
        qs = k % 4
        S.op('act', lambda e, k=k, qs=qs: e.activation(out=sq[:, qs, :n], in_=x32[:, slot, k, :n], func=AF.Square),
             reads=[f'x32_{slot}'], writes=[f'sq{qs}'])
        S.op('pe', lambda e, k=k, qs=qs: e.matmul(ps_ss[:, :n], lhsT=onesD[:], rhs=sq[:, qs, :n], start=(k == 0), stop=(k == KC - 1)),
             reads=[f'sq{qs}'], writes=['ps_ss'])
    S.op('act', lambda e: e.activation(out=rstd[:, :n], in_=ps_ss[:, :n], func=AF.Sqrt, bias=eps_t[:], scale=1.0), reads=['ps_ss'], writes=['rstd'])
    S.op('dve', lambda e: e.reciprocal(out=rstd[:, :n], in_=rstd[:, :n]), reads=['rstd'], writes=['rstd'])
    for k in range(KC):
        ts = k % 2
        S.op('dve', lambda e, k=k, ts=ts: e.tensor_tensor(out=tmp[:, ts, :n], in0=x32[:, slot, k, :n], in1=rstd[:, :n], op=ALU.mult),
             reads=[f'x32_{slot}', 'rstd'], writes=[f'tmp{ts}'])
        S.op('act', lambda e, k=k, ts=ts: e.activation(out=hT[:, k, t0:t0 + n], in_=tmp[:, ts, :n], func=AF.Identity, bias=sh[:, k, col:col + 1], scale=gm[:, k, col:col + 1]),
             reads=[f'tmp{ts}', 'gmsh'], writes=[hkey + str(k)])


def emit_gm(S, nc, st, mod, n_dram, sc_off, prefix):
    nw = st.enter_context(nc.sbuf_tensor(prefix + "nw", [128, KC], F32))
    gm = st.enter_context(nc.sbuf_tensor(prefix + "gm", [128, KC, 2], F32))
    S.op('sp', lambda e: e.dma_start(out=nw[:], in_=n_dram), writes=[prefix + 'nw'], dma=prefix + 'nw')
    for c in range(2):
        S.op('dve', lambda e, c=c: e.scalar_tensor_tensor(out=gm[:, :, c], in0=mod[:, sc_off:sc_off + KC, c], scalar=1.0, in1=nw[:], op0=ALU.add, op1=ALU.mult),
             reads=[prefix + 'nw', 'mod'], writes=['gmsh'])
    return gm


NCH_A = 27
NCH_B = 14
NQSCALE = (27 + 2, 27 + 6)


def build_P():
    nc = bass.Bass("TRN2", target_bir_lowering=False)
    T = TPC
    xT = nc.dram_tensor("xT", [D, T], F32, kind="ExternalInput").ap()
    cT = nc.dram_tensor("cT", [128, KC, 2], F32, kind="ExternalInput").ap()
    adaw = nc.dram_tensor("adaw", [D, 2 * D], F32, kind="ExternalInput").ap()
    adab = nc.dram_tensor("adab", [128, 32], F32, kind="ExternalInput").ap()
    n1 = nc.dram_tensor("n1", [128, KC], F32, kind="ExternalInput").ap()
    win = nc.dram_tensor("win", [D, (NCH_A + NCH_B) * 128], F32, kind="ExternalInput").ap()
    zA = nc.dram_tensor("zA", [NCH_A * 128, T], F32, kind="ExternalOutput").ap()
    zB = nc.dram_tensor("zB", [NCH_B * 128, T], BF16, kind="ExternalOutput").ap()
    with ExitStack() as st:
        sb = lambda name, shape, dt: st.enter_context(nc.sbuf_tensor(name, shape, dt))
        x32 = sb("x32", [128, 2, KC, 256], F32)
        hT = sb("hT", [128, KC, T], BF16)
        sq = sb("sq", [128, 4, 256], F32)
        tmp = sb("tmp", [128, 2, 256], F32)
        rstd = sb("rstd", [128, 256], F32)
        onesD = sb("onesD", [128, 128], F32)
        eps_t = sb("eps", [128, 1], F32)
        wb = sb("wb", [128, 2, KC, 512], BF16)
        zo32 = sb("zo32", [128, 2, T], F32)
        zo16 = sb("zo16", [128, 2, T], BF16)
        ps_mod = st.enter_context(nc.psum_tensor("ps_mod", [128, 512], F32))
        ps_ss = st.enter_context(nc.psum_tensor("ps_ss", [128, 512], F32))
        ps = [st.enter_context(nc.psum_tensor(f"ps{i}", [128, 512], F32)) for i in range(4)]
        S = Sched(nc)
        S.op('dve', lambda e: e.memset(onesD[:], 1.0 / D), writes=['onesD'])
        S.op('dve', lambda e: e.memset(eps_t[:], NORM_EPS), writes=['eps'])
        mod = emit_mod(S, nc, st, cT, adaw, adab, 32, x32, ps_mod, None, "m_")
        S.last_w['mod'] = S.last_w['m_mod']
        gm = emit_gm(S, nc, st, mod, n1, KC, "g_")
        tiles = token_tiles(2048, 256) + [(2048, 128)]
        for i, (t0, n) in enumerate(tiles):
            slot = i % 2
            S.op('sp', lambda e, t0=t0, n=n, slot=slot: e.dma_start(out=x32[:, slot, :, :n], in_=xT[:, t0:t0 + n].rearrange("(k p) t -> p k t", p=128)),
                 writes=[f'x32_{slot}'], dma=f'x32_{slot}')
            col = 0 if t0 < 2048 else 1
            emit_norm_tile(S, nc, x32, slot, n, gm, mod, col, hT, t0, onesD, eps_t, sq, ps_ss, rstd, tmp, hkey=f'hT{i}_')
        hkeys = [f'hT{i}_{k}' for i in range(len(tiles)) for k in range(KC)]
        nch = NCH_A + NCH_B
        mt = token_tiles(T, 512)
        pi = 0
        for g0 in range(0, nch, 4):
            gn = min(4, nch - g0)
            ws = (g0 // 4) % 2
            S.op('pool', lambda e, g0=g0, gn=gn, ws=ws: e.dma_start(out=wb[:, ws, :, :gn * 128], in_=win[:, g0 * 128:(g0 + gn) * 128].rearrange("(k p) n -> p k n", p=128)),
                 writes=[f'wb{ws}'], dma=f'wb{ws}')
            for j in range(gn):
                ch = g0 + j
                isA = ch < NCH_A
                os_ = ch % 2
                okey = (f'zo32_{os_}' if isA else f'zo16_{os_}')
                for (t0, n) in mt:
                    p = ps[pi % 4]
                    pk = f'ps{pi % 4}'
                    pi += 1
                    def mm(e, p=p, ws=ws, j=j, t0=t0, n=n):
                        for k in range(KC):
                            ins = e.matmul(p[:, :n], lhsT=wb[:, ws, k, j * 128:(j + 1) * 128], rhs=hT[:, k, t0:t0 + n], start=(k == 0), stop=(k == KC - 1))
                        return ins
                    S.op('pe', mm, reads=[f'wb{ws}'] + hkeys, writes=[pk])
                    dst = (zo32 if isA else zo16)
                    scale = 0.125 if (NQSCALE[0] <= ch < NQSCALE[1]) else 1.0
                    if (pi % 2) == 0:
                        S.op('act', lambda e, p=p, dst=dst, os_=os_, t0=t0, n=n, scale=scale: e.activation(out=dst[:, os_, t0:t0 + n], in_=p[:, :n], func=AF.Copy, scale=scale),
                             reads=[pk], writes=[okey])
                    else:
                        S.op('dve', lambda e, p=p, dst=dst, os_=os_, t0=t0, n=n, scale=scale: e.tensor_scalar(out=dst[:, os_, t0:t0 + n], in0=p[:, :n], scalar1=scale, scalar2=None, op0=ALU.mult),
                             reads=[pk], writes=[okey])
                if isA:
                    S.op('sp', lambda e, ch=ch, os_=os_: e.dma_start(out=zA[ch * 128:(ch + 1) * 128, :], in_=zo32[:, os_, :]), reads=[okey], dma=f'zo32_{os_}')
                else:
                    S.op('sp', lambda e, ch=ch, os_=os_: e.dma_start(out=zB[(ch - NCH_A) * 128:(ch - NCH_A + 1) * 128, :], in_=zo16[:, os_, :]), reads=[okey], dma=f'zo16_{os_}')
        for s in ('D_zo32_0', 'D_zo32_1', 'D_zo16_0', 'D_zo16_1'):
            S.ops['sp'].append(([(s, S.cnt[s])], None, None, 0))
        S.emit(st)
    return nc


NFF = 44


def build_MF(final=False):
    nc = bass.Bass("TRN2", target_bir_lowering=False)
    T = TPC
    xT = nc.dram_tensor("xT", [D, T], F32, kind="ExternalInput").ap()
    yT = nc.dram_tensor("yT", [D, T], BF16, kind="ExternalInput").ap()
    cT = nc.dram_tensor("cT", [128, KC, 2], F32, kind="ExternalInput").ap()
    adaw = nc.dram_tensor("adaw", [D, 6 * D], F32, kind="ExternalInput").ap()
    adab = nc.dram_tensor("adab", [128, 96], F32, kind="ExternalInput").ap()
    n1 = nc.dram_tensor("n1", [128, KC], F32, kind="ExternalInput").ap()
    n2 = nc.dram_tensor("n2", [128, KC], F32, kind="ExternalInput").ap()
    wgb = nc.dram_tensor("wgb", [KC, 128, KC * 4 * 128], F32, kind="ExternalInput").ap()
    wout = nc.dram_tensor("wout", [KC, 128, KC * 128], F32, kind="ExternalInput").ap()
    w13 = nc.dram_tensor("w13", [NFF, 128, KC * 2 * 128], F32, kind="ExternalInput").ap()
    w2 = nc.dram_tensor("w2", [KC, 128, NFF * 128], F32, kind="ExternalInput").ap()
    if final:
        fn = nc.dram_tensor("fn", [128, KC], F32, kind="ExternalInput").ap()
    xo = nc.dram_tensor("xo", [D, T], F32, kind="ExternalOutput").ap()
    with ExitStack() as st:
        sb = lambda name, shape, dt: st.enter_context(nc.sbuf_tensor(name, shape, dt))
        x32 = sb("x32", [128, KC, 512], F32)
        x32v = x32[:].rearrange("p k (s t) -> p s k t", s=2)
        hT = sb("hT", [128, KC, 512], BF16)
        bufA = sb("bufA", [128, NFF, 512], BF16)
        wbuf = sb("wbuf", [128, 2, 8192], BF16)
        gsb = sb("gsb", [128, 3, 512], F32)
        asb = sb("asb", [128, 2, 512], F32)
        sq = sb("sq", [128, 4, 512], F32)
        tmp = sb("tmp", [128, 2, 512], F32)
        rstd = sb("rstd", [128, 512], F32)
        onesD = sb("onesD", [128, 128], F32)
        eps_t = sb("eps", [128, 1], F32)
        if final:
            fnsb = sb("fnsb", [128, KC], F32)
        ps_mod = st.enter_context(nc.psum_tensor("ps_mod", [128, 512], F32))
        ps_ss = st.enter_context(nc.psum_tensor("ps_ss", [128, 512], F32))
        ps = [st.enter_context(nc.psum_tensor(f"ps{i}", [128, 512], F32)) for i in range(6)]
        S = Sched(nc)
        S.op('dve', lambda e: e.memset(onesD[:], 1.0 / D), writes=['onesD'])
        S.op('dve', lambda e: e.memset(eps_t[:], NORM_EPS), writes=['eps'])
        if final:
            S.op('sp', lambda e: e.dma_start(out=fnsb[:], in_=fn), writes=['fnsb'], dma='fn')

        class X32V:
            def __getitem__(self, idx):
                return x32v[idx]
        mod = emit_mod(S, nc, st, cT, adaw, adab, 96, X32V(), ps_mod, None, "m_")
        S.last_w['mod'] = S.last_w['m_mod']
        gm1 = emit_gm(S, nc, st, mod, n1, 16, "g1_")
        gm2 = emit_gm(S, nc, st, mod, n2, 64, "g2_")

        class X32S:
            def __getitem__(self, idx):
                return x32[(idx[0],) + tuple(idx[2:])]
        xs = X32S()
        wslot = [0]

        def wload(src_ap, nelem):
            ws = wslot[0] % 2
            wslot[0] += 1
            S.op('pool', lambda e, ws=ws: e.dma_start(out=wbuf[:, ws, 0:nelem], in_=src_ap), writes=[f'wbuf{ws}'], dma=f'wbuf{ws}')
            return ws

        pcount = [0]

        def nextps():
            i = pcount[0] % 6
            pcount[0] += 1
            return ps[i], f'ps{i}'

        mt = token_tiles(T, 512)

        def do_tile(ti, t0, n):
            col = 0 if t0 < 2048 else 1
            S.op('sp', lambda e, t0=t0, n=n: e.dma_start(out=x32[:, :, :n], in_=xT[:, t0:t0 + n].rearrange("(k p) t -> p k t", p=128)),
                 writes=['x32_0', 'x32_1'] + [f'xc{k}' for k in range(KC)], dma='x32')
            S.op('sp', lambda e, t0=t0, n=n: e.dma_start(out=bufA[:, 0:KC, :n], in_=yT[:, t0:t0 + n].rearrange("(k p) t -> p k t", p=128)),
                 writes=[f'ba{k}' for k in range(KC)], dma='yT')
            emit_norm_tile(S, nc, xs, 0, n, gm1, mod[:, 0:16, :], col, hT, 0, onesD, eps_t, sq, ps_ss, rstd, tmp, hkey='h')
            hk = [f'h{k}' for k in range(KC)]
            for dc in range(KC):
                ws = wload(wgb[dc], KC * 4 * 128)
                wv = wbuf[:, ws, :].rearrange("p (k g n) -> p k g n", k=KC, g=4)
                pg = []
                for a in range(3):
                    p, pk = nextps()
                    def mmg(e, p=p, wv=wv, a=a):
                        for k in range(KC):
                            ins = e.matmul(p[:, :n], lhsT=wv[:, k, a, :], rhs=hT[:, k, :n], start=(k == 0), stop=(k == KC - 1))
                        return ins
                    S.op('pe', mmg, reads=[f'wbuf{ws}'] + hk, writes=[pk])
                    S.op('act', lambda e, p=p, a=a: e.activation(out=gsb[:, a, :n], in_=p[:, :n], func=AF.Sigmoid), reads=[pk], writes=[f'gsb{a}'])
                    pg.append((p, pk))
                pp = []
                for a, (k0, k1) in enumerate(((0, 4), (4, 12), (12, 16))):
                    p, pk = nextps()
                    def mmp(e, p=p, wv=wv, k0=k0, k1=k1):
                        for k in range(k0, k1):
                            ins = e.matmul(p[:, :n], lhsT=wv[:, k, 3, :], rhs=bufA[:, k, :n], start=(k == k0), stop=(k == k1 - 1))
                        return ins
                    S.op('pe', mmp, reads=[f'wbuf{ws}'] + [f'ba{k}' for k in range(k0, k1)], writes=[pk])
                    pp.append((p, pk))
                S.op('dve', lambda e, p=pp[0][0]: e.tensor_tensor(out=tmp[:, 0, :n], in0=gsb[:, 0, :n], in1=p[:, :n], op=ALU.mult), reads=['gsb0', pp[0][1]], writes=['tmp0'])
                S.op('dve', lambda e, p=pp[1][0]: e.tensor_tensor(out=tmp[:, 1, :n], in0=gsb[:, 1, :n], in1=p[:, :n], op=ALU.mult), reads=['gsb1', pp[1][1]], writes=['tmp1'])
                S.op('dve', lambda e: e.tensor_tensor(out=tmp[:, 0, :n], in0=tmp[:, 0, :n], in1=tmp[:, 1, :n], op=ALU.add), reads=['tmp0', 'tmp1'], writes=['tmp0'])
                S.op('dve', lambda e, p=pp[2][0]: e.tensor_tensor(out=tmp[:, 1, :n], in0=gsb[:, 2, :n], in1=p[:, :n], op=ALU.mult), reads=['gsb2', pp[2][1]], writes=['tmp1'])
                S.op('dve', lambda e, dc=dc: e.tensor_tensor(out=bufA[:, KC + dc, :n], in0=tmp[:, 0, :n], in1=tmp[:, 1, :n], op=ALU.add), reads=['tmp0', 'tmp1'], writes=[f'ba{KC + dc}'])
            mk = [f'ba{KC + k}' for k in range(KC)]
            for dc in range(KC):
                ws = wload(wout[dc], KC * 128)
                wv = wbuf[:, ws, 0:KC * 128].rearrange("p (k n) -> p k n", k=KC)
                p, pk = nextps()
                def mmo(e, p=p, wv=wv):
                    for k in range(KC):
                        ins = e.matmul(p[:, :n], lhsT=wv[:, k, :], rhs=bufA[:, KC + k, :n], start=(k == 0), stop=(k == KC - 1))
                    return ins
                S.op('pe', mmo, reads=[f'wbuf{ws}'] + mk, writes=[pk])
                S.op('dve', lambda e, p=p, dc=dc: e.scalar_tensor_tensor(out=x32[:, dc, :n], in0=p[:, :n], scalar=mod[:, 32 + dc, col:col + 1], in1=x32[:, dc, :n], op0=ALU.mult, op1=ALU.add),
                     reads=[pk, f'xc{dc}', 'x32_0', 'gmsh'], writes=[f'xc{dc}'])
            S.op('dve', lambda e: e.memset(eps_t[:], NORM_EPS), reads=[f'xc{k}' for k in range(KC)], writes=['x32_0', 'x32_1'])
            emit_norm_tile(S, nc, xs, 0, n, gm2, mod[:, 48:64, :], col, hT, 0, onesD, eps_t, sq, ps_ss, rstd, tmp, hkey='h')
            for f in range(NFF):
                ws = wload(w13[f], KC * 2 * 128)
                wv = wbuf[:, ws, 0:KC * 256].rearrange("p (k g n) -> p k g n", k=KC, g=2)
                p1, pk1 = nextps()
                p3, pk3 = nextps()
                for (p, pk, g) in ((p1, pk1, 0), (p3, pk3, 1)):
                    def mmf(e, p=p, wv=wv, g=g):
                        for k in range(KC):
                            ins = e.matmul(p[:, :n], lhsT=wv[:, k, g, :], rhs=hT[:, k, :n], start=(k == 0), stop=(k == KC - 1))
                        return ins
                    S.op('pe', mmf, reads=[f'wbuf{ws}'] + hk, writes=[pk])
                a_s = f % 2
                S.op('act', lambda e, p1=p1, a_s=a_s: e.activation(out=asb[:, a_s, :n], in_=p1[:, :n], func=AF.Silu), reads=[pk1], writes=[f'asb{a_s}'])
                S.op('dve', lambda e, p3=p3, a_s=a_s, f=f: e.tensor_tensor(out=bufA[:, f, :n], in0=asb[:, a_s, :n], in1=p3[:, :n], op=ALU.mult), reads=[pk3, f'asb{a_s}'], writes=[f'ba{f}'])
            ak = [f'ba{f}' for f in range(NFF)]
            for dc in range(KC):
                ws = wload(w2[dc], NFF * 128)
                wv = wbuf[:, ws, 0:NFF * 128].rearrange("p (k n) -> p k n", k=NFF)
                p, pk = nextps()
                def mmd(e, p=p, wv=wv):
                    for f in range(NFF):
                        ins = e.matmul(p[:, :n], lhsT=wv[:, f, :], rhs=bufA[:, f, :n], start=(f == 0), stop=(f == NFF - 1))
                    return ins
                S.op('pe', mmd, reads=[f'wbuf{ws}'] + ak, writes=[pk])
                S.op('dve', lambda e, p=p, dc=dc: e.scalar_tensor_tensor(out=x32[:, dc, :n], in0=p[:, :n], scalar=mod[:, 80 + dc, col:col + 1], in1=x32[:, dc, :n], op0=ALU.mult, op1=ALU.add),
                     reads=[pk, f'xc{dc}', 'x32_0'], writes=[f'xc{dc}'])
            S.op('dve', lambda e: e.memset(eps_t[:], NORM_EPS), reads=[f'xc{k}' for k in range(KC)], writes=['x32_0', 'x32_1'])
            if final:
                for k in range(KC):
                    qs = k % 4
                    S.op('act', lambda e, k=k, qs=qs: e.activation(out=sq[:, qs, :n], in_=x32[:, k, :n], func=AF.Square), reads=['x32_0'], writes=[f'sq{qs}'])
                    S.op('pe', lambda e, k=k, qs=qs: e.matmul(ps_ss[:, :n], lhsT=onesD[:], rhs=sq[:, qs, :n], start=(k == 0), stop=(k == KC - 1)), reads=[f'sq{qs}'], writes=['ps_ss'])
                S.op('act', lambda e: e.activation(out=rstd[:, :n], in_=ps_ss[:, :n], func=AF.Sqrt, bias=eps_t[:], scale=1.0), reads=['ps_ss'], writes=['rstd'])
                S.op('dve', lambda e: e.reciprocal(out=rstd[:, :n], in_=rstd[:, :n]), reads=['rstd'], writes=['rstd'])
                for k in range(KC):
                    S.op('dve', lambda e, k=k: e.scalar_tensor_tensor(out=x32[:, k, :n], in0=x32[:, k, :n], scalar=fnsb[:, k:k + 1], in1=rstd[:, :n], op0=ALU.mult, op1=ALU.mult),
                         reads=['x32_0', 'rstd', 'fnsb'], writes=[f'xc{k}'])
                S.op('dve', lambda e: e.memset(eps_t[:], NORM_EPS), reads=[f'xc{k}' for k in range(KC)], writes=['x32_0', 'x32_1'])
            S.op('sp', lambda e, t0=t0, n=n: e.dma_start(out=xo[:, t0:t0 + n].rearrange("(k p) t -> p k t", p=128), in_=x32[:, :, :n]),
                 reads=['x32_0', 'x32_1'], dma='xo')
        for ti, (t0, n) in enumerate(mt):
            do_tile(ti, t0, n)
        S.ops['sp'].append(([('D_xo', S.cnt['D_xo'])], None, None, 0))
        S.emit(st)
    return nc


RWC = 2176
GQ_OFF = RWC
NA_OFF = RWC + 1536
GATE_OFF = RWC + 1536 + 1536


def _perm_heads(cols, hd):
    cols = np.asarray(cols).reshape(-1, hd)
    return np.concatenate([cols[:, 0::2], cols[:, 1::2]], axis=1).reshape(-1)


def p_cols():
    colsA = np.concatenate([np.arange(0, RWC), _perm_heads(np.arange(GQ_OFF, GQ_OFF + 1024), 128), _perm_heads(np.arange(GQ_OFF + 1024, GQ_OFF + 1280), 128)])
    colsB = np.concatenate([np.arange(GQ_OFF + 1280, GQ_OFF + 1536), np.arange(NA_OFF, NA_OFF + 1536)])
    return np.concatenate([colsA, colsB])


def pk(v):
    return np.ascontiguousarray(v.reshape(-1, 128).T)


def prep_P_shared(l, I):
    return {"adaw": np.ascontiguousarray(I['ada_w'][l][:, :4096]), "adab": pk(I['ada_b'][l][:4096]),
            "n1": pk(I['norm1'][l]), "win": np.ascontiguousarray(I['w_in'][l][:, p_cols()])}


def cT_of(I, b):
    cc = np.stack([I['c'][b], I['c_ctx']], axis=1)
    return np.ascontiguousarray(cc.reshape(16, 128, 2).transpose(1, 0, 2))


def prep_MF_shared(l, I, final):
    w_in = I['w_in'][l]
    wg = w_in[:, GATE_OFF:GATE_OFF + 3 * D].reshape(16, 128, 3, 16, 128)
    wbr = np.concatenate([I['w_br_a'][l], I['w_br_b'][l], I['w_br_c'][l]], axis=0).reshape(16, 128, 1, 16, 128)
    wgb = np.concatenate([wg, wbr], axis=2).transpose(3, 1, 0, 2, 4)
    wgb = np.ascontiguousarray(wgb).reshape(16, 128, 16 * 4 * 128)
    wout = np.ascontiguousarray(I['w_out'][l].reshape(16, 128, 16, 128).transpose(2, 1, 0, 3)).reshape(16, 128, 16 * 128)
    w1 = I['ffn_w1'][l].reshape(16, 128, 1, NFF, 128)
    w3 = I['ffn_w3'][l].reshape(16, 128, 1, NFF, 128)
    w13 = np.ascontiguousarray(np.concatenate([w1, w3], axis=2).transpose(3, 1, 0, 2, 4)).reshape(NFF, 128, 16 * 2 * 128)
    w2 = np.ascontiguousarray(I['ffn_w2'][l].reshape(NFF, 128, 16, 128).transpose(2, 1, 0, 3)).reshape(16, 128, NFF * 128)
    d = {"adaw": np.ascontiguousarray(I['ada_w'][l]), "adab": pk(I['ada_b'][l]), "n1": pk(I['norm1'][l]), "n2": pk(I['norm2'][l]),
         "wgb": wgb, "wout": wout, "w13": w13, "w2": w2}
    if final:
        d["fn"] = pk(I['final_norm'])
    return d


TK = SEQ + CTX


def build_GQ():
    nc = bass.Bass("TRN2", target_bir_lowering=False)
    qT = nc.dram_tensor("qT", [4, 128, TK], F32, kind="ExternalInput").ap()
    kT = nc.dram_tensor("kT", [128, TK], F32, kind="ExternalInput").ap()
    V = nc.dram_tensor("V", [TK, 128], BF16, kind="ExternalInput").ap()
    ropeC = nc.dram_tensor("ropeC", [128, SEQ], F32, kind="ExternalInput").ap()
    ropeS = nc.dram_tensor("ropeS", [128, SEQ], F32, kind="ExternalInput").ap()
    gains = nc.dram_tensor("gains", [128, 2], F32, kind="ExternalInput").ap()
    pswap = nc.dram_tensor("pswap", [128, 128], F32, kind="ExternalInput").ap()
    yb = nc.dram_tensor("yb", [4, 128, TK], BF16, kind="ExternalOutput").ap()
    NKC = TK // 128
    with ExitStack() as st:
        sb = lambda name, shape, dt: st.enter_context(nc.sbuf_tensor(name, shape, dt))
        qr = sb("qr", [128, 4, TK], BF16)
        kr = sb("kr", [128, TK], BF16)
        Vs = sb("Vs", [128, NKC, 128], BF16)
        Ct = sb("Ct", [128, SEQ], F32)
        St = sb("St", [128, SEQ], F32)
        gsb = sb("gsb", [128, 2], F32)
        psw = sb("psw", [128, 128], F32)
        ones32 = sb("ones32", [128, 128], F32)
        ones16 = sb("ones16", [128, 128], BF16)
        eps_t = sb("eps", [128, 1], F32)
        raw = sb("raw", [128, 2, 512], F32)
        sqv = sb("sqv", [128, 2, 512], F32)
        rstd = sb("rstd", [128, 2, 512], F32)
        qn = sb("qn", [128, 2, 512], F32)
        t1 = sb("t1", [128, 2, 512], F32)
        t2 = sb("t2", [128, 2, 512], F32)
        pT = sb("pT", [128, 2, 512], BF16)
        rden = sb("rden", [128, 512], F32)
        yo = sb("yo", [128, 2, 512], BF16)
        ps_ms = st.enter_context(nc.psum_tensor("ps_ms", [128, 512], F32))
        ps_sw = st.enter_context(nc.psum_tensor("ps_sw", [128, 512], F32))
        ps_s = [st.enter_context(nc.psum_tensor(f"ps_s{i}", [128, 512], F32)) for i in range(2)]
        ps_o = [st.enter_context(nc.psum_tensor(f"ps_o{i}", [128, 512], F32)) for i in range(2)]
        ps_d = [st.enter_context(nc.psum_tensor(f"ps_d{i}", [128, 512], F32)) for i in range(2)]
        S = Sched(nc)
        S.op('dve', lambda e: e.memset(ones32[:], 1.0 / 128), writes=['ones32'])
        S.op('dve', lambda e: e.memset(ones16[:], 1.0), writes=['ones16'])
        S.op('dve', lambda e: e.memset(eps_t[:], NORM_EPS), writes=['eps'])
        S.op('sp', lambda e: e.dma_start(out=Ct[:], in_=ropeC), writes=['Ct'], dma='Ct')
        S.op('sp', lambda e: e.dma_start(out=St[:], in_=ropeS), writes=['St'], dma='St')
        S.op('sp', lambda e: e.dma_start(out=gsb[:], in_=gains), writes=['gsb'], dma='gsb')
        S.op('sp', lambda e: e.dma_start(out=psw[:], in_=pswap), writes=['psw'], dma='psw')
        S.op('sp', lambda e: e.dma_start(out=Vs[:], in_=V.rearrange("(c p) d -> p c d", p=128)), writes=['Vs'], dma='Vs')

        cnt = [0]

        def prep(src_ap, dst_ap, gcol, t0, n, rope, dkey):
            i = cnt[0] % 2
            cnt[0] += 1
            S.op('sp', lambda e: e.dma_start(out=raw[:, i, :n], in_=src_ap), writes=[f'raw{i}'], dma=f'raw{i}')
            S.op('act', lambda e: e.activation(out=sqv[:, i, :n], in_=raw[:, i, :n], func=AF.Square), reads=[f'raw{i}'], writes=[f'sqv{i}'])
            S.op('pe', lambda e: e.matmul(ps_ms[:, :n], lhsT=ones32[:], rhs=sqv[:, i, :n], start=True, stop=True), reads=[f'sqv{i}', 'ones32'], writes=['ps_ms'])
            S.op('act', lambda e: e.activation(out=rstd[:, i, :n], in_=ps_ms[:, :n], func=AF.Sqrt, bias=eps_t[:], scale=1.0), reads=['ps_ms', 'eps'], writes=[f'rstd{i}'])
            S.op('dve', lambda e: e.reciprocal(out=rstd[:, i, :n], in_=rstd[:, i, :n]), reads=[f'rstd{i}'], writes=[f'rstd{i}'])
            S.op('dve', lambda e: e.scalar_tensor_tensor(out=qn[:, i, :n], in0=raw[:, i, :n], scalar=gsb[:, gcol:gcol + 1], in1=rstd[:, i, :n], op0=ALU.mult, op1=ALU.mult),
                 reads=[f'raw{i}', f'rstd{i}', 'gsb'], writes=[f'qn{i}'])
            if rope:
                S.op('pe', lambda e: e.matmul(ps_sw[:, :n], lhsT=psw[:], rhs=qn[:, i, :n], start=True, stop=True), reads=[f'qn{i}', 'psw'], writes=['ps_sw'])
                S.op('pool', lambda e: e.tensor_tensor(out=t1[:, i, :n], in0=qn[:, i, :n], in1=Ct[:, t0:t0 + n], op=ALU.mult), reads=[f'qn{i}', 'Ct'], writes=[f't1{i}'])
                S.op('dve', lambda e: e.tensor_tensor(out=t2[:, i, :n], in0=ps_sw[:, :n], in1=St[:, t0:t0 + n], op=ALU.mult), reads=['ps_sw', 'St'], writes=[f't2{i}'])
                S.op('pool', lambda e: e.tensor_tensor(out=dst_ap, in0=t1[:, i, :n], in1=t2[:, i, :n], op=ALU.add), reads=[f't1{i}', f't2{i}'], writes=[dkey])
            else:
                S.op('pool', lambda e: e.tensor_copy(out=dst_ap, in_=qn[:, i, :n]), reads=[f'qn{i}'], writes=[dkey])

        tl = token_tiles(SEQ, 512) + [(SEQ, CTX)]
        kkeys = []
        for (t0, n) in tl:
            prep(kT[:, t0:t0 + n], kr[:, t0:t0 + n], 1, t0, n, t0 < SEQ, f'kr{t0}')
            kkeys.append(f'kr{t0}')
        for h in range(4):
            for (t0, n) in tl:
                prep(qT[h, :, t0:t0 + n], qr[:, h, t0:t0 + n], 0, t0, n, t0 < SEQ, f'qr{h}_{t0}')

        scale = 128 ** -0.5
        acnt = [0]

        def attend(h, t0, n, kcs):
            a = acnt[0] % 2
            acnt[0] += 1
            qk = f'qr{h}_{t0}'
            po, pd = ps_o[a], ps_d[a]

            def smm(j):
                kc = kcs[j]
                b = j % 2
                S.op('pe', lambda e: e.matmul(ps_s[b][:, :n], lhsT=kr[:, kc * 128:(kc + 1) * 128], rhs=qr[:, h, t0:t0 + n], start=True, stop=True),
                     reads=[qk] + kkeys, writes=[f'ps_s{b}'])
            smm(0)
            for j, kc in enumerate(kcs):
                b = j % 2
                if j + 1 < len(kcs):
                    smm(j + 1)
                S.op('act', lambda e, b=b: e.activation(out=pT[:, b, :n], in_=ps_s[b][:, :n], func=AF.Exp, scale=scale), reads=[f'ps_s{b}'], writes=[f'pT{b}'])
                first, last = (j == 0), (j == len(kcs) - 1)
                S.op('pe', lambda e, b=b, kc=kc, first=first, last=last: e.matmul(po[:, :n], lhsT=Vs[:, kc, :], rhs=pT[:, b, :n], start=first, stop=last),
                     reads=[f'pT{b}', 'Vs'], writes=[f'ps_o{a}'])
                S.op('pe', lambda e, b=b, first=first, last=last: e.matmul(pd[:, :n], lhsT=ones16[:], rhs=pT[:, b, :n], start=first, stop=last),
                     reads=[f'pT{b}', 'ones16'], writes=[f'ps_d{a}'])
            S.op('dve', lambda e: e.reciprocal(out=rden[:, :n], in_=pd[:, :n]), reads=[f'ps_d{a}'], writes=['rden'])
            S.op('dve', lambda e: e.tensor_tensor(out=yo[:, a, :n], in0=po[:, :n], in1=rden[:, :n], op=ALU.mult), reads=[f'ps_o{a}', 'rden'], writes=[f'yo{a}'])
            S.op('sp', lambda e: e.dma_start(out=yb[h, :, t0:t0 + n], in_=yo[:, a, :n]), reads=[f'yo{a}'], dma=f'yo{a}')

        for h in range(4):
            for (t0, n) in token_tiles(SEQ, 512):
                attend(h, t0, n, list(range(NKC)))
            attend(h, SEQ, CTX, [32, 33])
        for a in range(2):
            S.ops['sp'].append(([(f'D_yo{a}', S.cnt[f'D_yo{a}'])], None, None, 0))
        S.emit(st)
    return nc


def rope_tables():
    n_freq = 32
    inv = (10000.0 ** (-np.arange(n_freq, dtype=np.float32) / n_freq)).astype(np.float32)
    t = np.arange(SEQ)
    row = (t // 64).astype(np.float32)
    colp = (t % 64).astype(np.float32)
    ang = np.concatenate([row[:, None] * inv, colp[:, None] * inv], axis=-1).astype(np.float32)
    cos = np.cos(ang).astype(np.float32).T
    sin = np.sin(ang).astype(np.float32).T
    C = np.ascontiguousarray(np.concatenate([cos, cos], axis=0))
    Sg = np.ascontiguousarray(np.concatenate([-sin, sin], axis=0))
    P = np.zeros((128, 128), np.float32)
    for p in range(128):
        P[p, (p + 64) % 128] = 1.0
    return C, Sg, P


NEG = -30000.0


def build_NA():
    nc = bass.Bass("TRN2", target_bir_lowering=False)
    qT = nc.dram_tensor("qT", [4, 64, TK], BF16, kind="ExternalInput").ap()
    kT = nc.dram_tensor("kT", [4, 64, TK], BF16, kind="ExternalInput").ap()
    V = nc.dram_tensor("V", [4, TK, 64], BF16, kind="ExternalInput").ap()
    bias = nc.dram_tensor("bias", [128, 4, 8, 256], F32, kind="ExternalInput").ap()
    yc = nc.dram_tensor("yc", [4, 64, TK], BF16, kind="ExternalOutput").ap()
    with ExitStack() as st:
        sb = lambda name, shape, dt: st.enter_context(nc.sbuf_tensor(name, shape, dt))
        qs = sb("qs", [64, 4, TK], BF16)
        ks = sb("ks", [64, 4, TK], BF16)
        Ve = sb("Ve", [128, 4, 34, 64], BF16)
        Vo = sb("Vo", [128, 4, 31, 64], BF16)
        bs = sb("bs", [128, 4, 8, 256], F32)
        ones16 = sb("ones16", [128, 64], BF16)
        pT = sb("pT", [128, 2, 384], BF16)
        rden = sb("rden", [64, 512], F32)
        yo = sb("yo", [64, 2, 512], BF16)
        ps_s = [st.enter_context(nc.psum_tensor(f"ps_s{i}", [128, 512], F32)) for i in range(2)]
        ps_o = [st.enter_context(nc.psum_tensor(f"ps_o{i}", [64, 512], F32)) for i in range(2)]
        ps_d = [st.enter_context(nc.psum_tensor(f"ps_d{i}", [64, 512], F32)) for i in range(2)]
        S = Sched(nc)
        S.op('dve', lambda e: e.memset(ones16[:], 1.0), writes=['ones16'])
        S.op('sp', lambda e: e.dma_start(out=bs[:], in_=bias), writes=['bs'], dma='bs')
        for h in range(4):
            S.op('sp', lambda e, h=h: e.dma_start(out=qs[:, h, :], in_=qT[h]), writes=['qs'], dma='qs')
            S.op('sp', lambda e, h=h: e.dma_start(out=ks[:, h, :], in_=kT[h]), writes=['ks'], dma='ks')
            S.op('sp', lambda e, h=h: e.dma_start(out=Ve[:, h, :, :], in_=V[h].rearrange("(c p) d -> p c d", p=128)), writes=['Ve'], dma='Ve')
            S.op('sp', lambda e, h=h: e.dma_start(out=Vo[:, h, :, :], in_=V[h, 64:64 + 31 * 128, :].rearrange("(c p) d -> p c d", p=128)), writes=['Vo'], dma='Vo')

        rcnt = [0]

        def row(h, q0, chunks, variant, a, slot, first_in_grp):
            b = rcnt[0] % 2
            rcnt[0] += 1
            ncx = len(chunks)

            def mms(e):
                for c, (k0, _) in enumerate(chunks):
                    ins = e.matmul(ps_s[b][:, c * 64:(c + 1) * 64], lhsT=ks[:, h, k0:k0 + 128], rhs=qs[:, h, q0:q0 + 64], start=True, stop=True)
                return ins
            S.op('pe', mms, reads=['qs', 'ks'], writes=[f'ps_s{b}'])
            if variant is not None:
                S.op('dve', lambda e: e.tensor_tensor(out=ps_s[b][:, 0:256], in0=ps_s[b][:, 0:256], in1=bs[:, h, variant, :], op=ALU.add), reads=[f'ps_s{b}', 'bs'], writes=[f'ps_s{b}'])
            S.op('act', lambda e: e.activation(out=pT[:, b, :ncx * 64], in_=ps_s[b][:, :ncx * 64], func=AF.Exp), reads=[f'ps_s{b}'], writes=[f'pT{b}'])

            def mmo(e):
                for c, (_, vap) in enumerate(chunks):
                    ins = e.matmul(ps_o[a][:, slot * 64:(slot + 1) * 64], lhsT=vap, rhs=pT[:, b, c * 64:(c + 1) * 64], start=(c == 0), stop=(c == ncx - 1))
                for c in range(ncx):
                    ins = e.matmul(ps_d[a][:, slot * 64:(slot + 1) * 64], lhsT=ones16[:], rhs=pT[:, b, c * 64:(c + 1) * 64], start=(c == 0), stop=(c == ncx - 1))
                return ins
            S.op('pe', mmo, reads=[f'pT{b}', 'Ve', 'Vo', 'ones16'], writes=[f'ps_o{a}', f'ps_d{a}'])

        gcnt = [0]

        def finish(h, q0, nq, a):
            S.op('dve', lambda e: e.reciprocal(out=rden[:, :nq], in_=ps_d[a][:, :nq]), reads=[f'ps_d{a}'], writes=['rden'])
            S.op('dve', lambda e: e.tensor_tensor(out=yo[:, a, :nq], in0=ps_o[a][:, :nq], in1=rden[:, :nq], op=ALU.mult), reads=[f'ps_o{a}', 'rden'], writes=[f'yo{a}'])
            S.op('sp', lambda e: e.dma_start(out=yc[h, :, q0:q0 + nq], in_=yo[:, a, :nq]), reads=[f'yo{a}'], dma=f'yo{a}')

        def vtile(h, tok0):
            blk = tok0 // 64
            if blk % 2 == 0:
                return Ve[:, h, blk // 2, :]
            return Vo[:, h, (blk - 1) // 2, :]

        for h in range(4):
            for g in range(8):
                a = gcnt[0] % 2
                gcnt[0] += 1
                for r in range(8):
                    i = g * 8 + r
                    rs = min(max(i - 4, 0), 56)
                    variant = i if i < 4 else (4 if i <= 60 else i - 56)
                    chunks = [((rs + 2 * c) * 64, vtile(h, (rs + 2 * c) * 64)) for c in range(4)]
                    chunks += [(SEQ + cc * 128, Ve[:, h, 32 + cc, :]) for cc in range(2)]
                    row(h, i * 64, chunks, variant, a, r, r == 0)
                finish(h, g * 512, 512, a)
            a = gcnt[0] % 2
            gcnt[0] += 1
            for r in range(4):
                chunks = [(SEQ + cc * 128, Ve[:, h, 32 + cc, :]) for cc in range(2)]
                row(h, SEQ + r * 64, chunks, None, a, r, r == 0)
            finish(h, SEQ, 256, a)
        for a in range(2):
            S.ops['sp'].append(([(f'D_yo{a}', S.cnt[f'D_yo{a}'])], None, None, 0))
        S.emit(st)
    return nc


def na_bias(rpb4):
    out = np.full((128, 4, 8, 4, 64), NEG, np.float32)
    p = np.arange(128)
    kr_off = p // 64
    kc = p % 64
    j = np.arange(64)
    cs = np.clip(j - 8, 0, 48)
    inwin = (kc[:, None] >= cs[None, :]) & (kc[:, None] < cs[None, :] + 16)
    coff = kc[:, None] - j[None, :] + 15
    coff_c = np.clip(coff, 0, 30)
    for v in range(8):
        d = -v
        for c in range(4):
            roff = d + 2 * c + kr_off + 7
            for h in range(4):
                vals = rpb4[h][roff[:, None], coff_c]
                out[:, h, v, c, :] = np.where(inwin, vals, NEG)
    return np.ascontiguousarray(out.reshape(128, 4, 8, 256))


NCHK = TK // 64
LWC = -0.6065306597126334


def build_RS():
    TT = 256
    NCT = TT // 64
    nc = bass.Bass("TRN2", target_bir_lowering=False)
    z0 = nc.dram_tensor("z0", [12, 128, TK], F32, kind="ExternalInput").ap()
    zL = nc.dram_tensor("zL", [12, 128, TK], F32, kind="ExternalInput").ap()
    zR = nc.dram_tensor("zR", [12, 128, TK], F32, kind="ExternalInput").ap()
    l0 = nc.dram_tensor("l0", [2, 96, TK], F32, kind="ExternalInput").ap()
    lL = nc.dram_tensor("lL", [2, 96, TK], F32, kind="ExternalInput").ap()
    lR = nc.dram_tensor("lR", [2, 96, TK], F32, kind="ExternalInput").ap()
    mu = nc.dram_tensor("mu", [128, 12, 2], F32, kind="ExternalInput").ap()
    mul = nc.dram_tensor("mul", [96, 2, 2], F32, kind="ExternalInput").ap()
    pvec = nc.dram_tensor("pvec", [128, 4, 5], F32, kind="ExternalInput").ap()
    wup = nc.dram_tensor("wup", [96, 512], F32, kind="ExternalInput").ap()
    aup = nc.dram_tensor("aup", [96, 512], F32, kind="ExternalInput").ap()
    cst = nc.dram_tensor("cst", [128, 4, 128], F32, kind="ExternalInput").ap()
    o = nc.dram_tensor("o", [TK, 512], F32, kind="ExternalOutput").ap()
    bv = nc.dram_tensor("bv", [4, 128, TK], F32, kind="ExternalOutput").ap()
    with ExitStack() as st:
        sb = lambda name, shape, dt: st.enter_context(nc.sbuf_tensor(name, shape, dt))
        zin = sb("zin", [128, 2, 3, TT], F32)
        zs = sb("zs", [128, 12, TT], F32)
        ls = sb("ls", [96, 2, TT], F32)
        mus = sb("mus", [128, 12, 2], F32)
        muls = sb("muls", [96, 2, 2], F32)
        pv = sb("pv", [128, 4, 5], F32)
        wups = sb("wups", [96, 512], F32)
        aups = sb("aups", [96, 512], F32)
        cs = sb("cs", [128, 4, 128], F32)
        zer = sb("zer", [128, 64], F32)
        d1 = sb("d1", [128, 2, TT], F32)
        sg = sb("sg", [128, TT], F32)
        av = sb("av", [128, TT], F32)
        kk = sb("kk", [128, TT], F32)
        kd = sb("kd", [128, TT], F32)
        t1 = sb("t1", [128, TT], F32)
        t2 = sb("t2", [128, TT], F32)
        lcw = sb("lcw", [128, TT], F32)
        cw = sb("cw", [128, 2, TT], F32)
        icw = sb("icw", [128, TT], F32)
        cwp = sb("cwp", [128, TT], F32)
        AR = sb("AR", [128, 2, NCT, 128], F32)
        BK = sb("BK", [128, 2, NCT, 128], F32)
        VV = sb("VV", [128, 2, NCT, 128], F32)
        bvs = sb("bvs", [128, 2, TT], F32)
        Sst = sb("Sst", [128, 4, 64], F32)
        BKt = sb("BKt", [128, 128], F32)
        UV = sb("UV", [128, 2, 64], F32)
        A4s = sb("A4s", [128, 2, 128], F32)
        Mb = sb("Mb", [64, 2, 2, 64], F32)
        Nb = sb("Nb", [64, 2, 2, 64], F32)
        Pb = sb("Pb", [64, 2, 2, 64], F32)
        Rs = sb("Rs", [64, 2, 64], F32)
        St = sb("St", [128, 64], F32)
        osb = sb("osb", [64, 2, NCT, 512], F32)
        pp = [st.enter_context(nc.psum_tensor(f"pp{i}", [128, TT], F32)) for i in range(8)]
        S = Sched(nc)
        S.alias = {'par': [f'par{i}' for i in range(6)]}
        S.psum_excl = True
        S.op('sp', lambda e: e.dma_start(out=mus[:], in_=mu), writes=['par0'], dma='par0')
        S.op('sp', lambda e: e.dma_start(out=muls[:], in_=mul), writes=['par1'], dma='par1')
        S.op('sp', lambda e: e.dma_start(out=pv[:], in_=pvec), writes=['par2'], dma='par2')
        S.op('sp', lambda e: e.dma_start(out=wups[:], in_=wup), writes=['par3'], dma='par3')
        S.op('sp', lambda e: e.dma_start(out=aups[:], in_=aup), writes=['par4'], dma='par4')
        S.op('sp', lambda e: e.dma_start(out=cs[:], in_=cst), writes=['par5'], dma='par5')
        S.op('dve', lambda e: e.memset(zer[:], 0.0), writes=['zer'])
        S.op('dve', lambda e: e.memset(Sst[:], 0.0), reads=['par', 'zer'], writes=[f'S{p}' for p in range(4)])
        S.op('dve', lambda e: e.memset(VV[:], 0.0), writes=['VV0', 'VV1'])
        blk1 = cs[:, 0, :]
        mask4 = cs[:, 1, :]
        ident = cs[:, 2, :]
        zcnt = [0]

        def shift(dst, n_part, src0, srcL, srcR, muL, muR, dkey):
            i = zcnt[0] % 2
            zcnt[0] += 1
            P = n_part
            S.op('sp', lambda e: e.dma_start(out=zin[:P, i, 0, :], in_=src0), writes=[f'zin{i}_0'], dma=f'zin{i}_0')
            S.op('sp', lambda e: e.dma_start(out=zin[:P, i, 1, :], in_=srcL), writes=[f'zin{i}_1'], dma=f'zin{i}_1')
            S.op('sp', lambda e: e.dma_start(out=zin[:P, i, 2, :], in_=srcR), writes=[f'zin{i}_2'], dma=f'zin{i}_2')
            S.op('pool', lambda e: e.tensor_tensor(out=d1[:P, 0, :], in0=zin[:P, i, 1, :], in1=zin[:P, i, 0, :], op=ALU.subtract), reads=[f'zin{i}_0', f'zin{i}_1'], writes=['d1a'])
            S.op('pool', lambda e: e.tensor_tensor(out=d1[:P, 1, :], in0=zin[:P, i, 2, :], in1=zin[:P, i, 0, :], op=ALU.subtract), reads=[f'zin{i}_0', f'zin{i}_2'], writes=['d1b'])
            S.op('dve', lambda e: e.scalar_tensor_tensor(out=d1[:P, 0, :], in0=d1[:P, 0, :], scalar=muL, in1=zin[:P, i, 0, :], op0=ALU.mult, op1=ALU.add), reads=['d1a', f'zin{i}_0', 'par'], writes=['d1a'])
            S.op('dve', lambda e: e.scalar_tensor_tensor(out=dst, in0=d1[:P, 1, :], scalar=muR, in1=d1[:P, 0, :], op0=ALU.mult, op1=ALU.add), reads=['d1a', 'd1b', 'par'], writes=[dkey])

        def tile_prep(ti):
            c0 = ti * TT
            for c in range(12):
                shift(zs[:, c, :], 128, z0[c, :, c0:c0 + TT], zL[c, :, c0:c0 + TT], zR[c, :, c0:c0 + TT], mus[:, c, 0:1], mus[:, c, 1:2], f'zs{c}')
            for j in range(2):
                shift(ls[:, j, :], 96, l0[j, :, c0:c0 + TT], lL[j, :, c0:c0 + TT], lR[j, :, c0:c0 + TT], muls[:, j, 0:1], muls[:, j, 1:2], f'ls{j}')
            S.op('act', lambda e: e.activation(out=ls[:, 0, :], in_=ls[:, 0, :], func=AF.Tanh), reads=['ls0'], writes=['ls0'])

        def pair_prep(ti, pr):
            q = pr % 2
            r_ = zs[:, pr, :]
            k_ = zs[:, 4 + pr, :]
            v_ = zs[:, 8 + pr, :]
            rk = f'zs{pr}'; kkey = f'zs{4 + pr}'; vkey = f'zs{8 + pr}'
            S.op('pe', lambda e: e.matmul(pp[0][:, :TT], lhsT=wups[:, pr * 128:(pr + 1) * 128], rhs=ls[:, 0, :], start=True, stop=True), reads=['ls0', 'par'], writes=['pp0'])
            S.op('act', lambda e: e.activation(out=sg[:], in_=pp[0][:, :TT], func=AF.Sigmoid, bias=pv[:, pr, 0:1], scale=1.0), reads=['pp0', 'par'], writes=['sg'])
            S.op('pe', lambda e: e.matmul(pp[1][:, :TT], lhsT=aups[:, pr * 128:(pr + 1) * 128], rhs=ls[:, 1, :], start=True, stop=True), reads=['ls1', 'par'], writes=['pp1'])
            S.op('act', lambda e: e.activation(out=av[:], in_=pp[1][:, :TT], func=AF.Sigmoid, bias=pv[:, pr, 1:2], scale=1.0), reads=['pp1', 'par'], writes=['av'])
            S.op('dve', lambda e: e.tensor_scalar(out=kk[:], in0=k_, scalar1=pv[:, pr, 2:3], scalar2=None, op0=ALU.mult), reads=[kkey, 'par'], writes=['kk'])
            S.op('act', lambda e: e.activation(out=t1[:], in_=kk[:], func=AF.Square), reads=['kk'], writes=['t1'])
            S.op('pe', lambda e: e.matmul(pp[0][:, :TT], lhsT=blk1, rhs=t1[:], start=True, stop=True), reads=['t1', 'par'], writes=['pp0'])
            S.op('act', lambda e: e.activation(out=t2[:], in_=pp[0][:, :TT], func=AF.Sqrt), reads=['pp0'], writes=['t2'])
            S.op('dve', lambda e: e.tensor_scalar(out=t2[:], in0=t2[:], scalar1=1e-12, scalar2=None, op0=ALU.max), reads=['t2'], writes=['t2'])
            S.op('dve', lambda e: e.reciprocal(out=t2[:], in_=t2[:]), reads=['t2'], writes=['t2'])
            S.op('dve', lambda e: e.tensor_tensor(out=kk[:], in0=kk[:], in1=t2[:], op=ALU.mult), reads=['kk', 't2'], writes=['kk'])
            S.op('dve', lambda e: e.tensor_scalar(out=t1[:], in0=av[:], scalar1=-1.0, scalar2=pv[:, pr, 3:4], op0=ALU.add, op1=ALU.mult), reads=['av', 'par', 't1'], writes=['t1'])
            S.op('dve', lambda e: e.scalar_tensor_tensor(out=kd[:], in0=t1[:], scalar=1.0, in1=k_, op0=ALU.add, op1=ALU.mult), reads=['t1', kkey], writes=['kd'])
            S.op('dve', lambda e: e.tensor_scalar(out=sg[:], in0=sg[:], scalar1=LWC, scalar2=None, op0=ALU.mult), reads=['sg'], writes=['sg'])
            for ch in range(NCT):
                S.op('dve', lambda e, ch=ch: e.tensor_tensor_scan(out=lcw[:, ch * 64:(ch + 1) * 64], data0=sg[:, ch * 64:(ch + 1) * 64], data1=zer[:], initial=0.0, op0=ALU.add, op1=ALU.add),
                     reads=['sg', 'zer'], writes=['lcw'])
            S.op('act', lambda e: e.activation(out=cw[:, q, :], in_=lcw[:], func=AF.Exp), reads=['lcw'], writes=[f'cw{q}'])
            S.op('act', lambda e: e.activation(out=icw[:], in_=lcw[:], func=AF.Exp, scale=-1.0), reads=['lcw'], writes=['icw'])
            S.op('dve', lambda e: e.tensor_tensor(out=cwp[:], in0=lcw[:], in1=sg[:], op=ALU.subtract), reads=['lcw', 'sg'], writes=['cwp'])
            S.op('act', lambda e: e.activation(out=cwp[:], in_=cwp[:], func=AF.Exp), reads=['cwp'], writes=['cwp'])
            v3 = lambda ap: ap.rearrange("p (c t) -> p c t", t=64)
            S.op('dve', lambda e: e.scalar_tensor_tensor(out=AR[:, q, :, 0:64], in0=v3(kk[:]), scalar=-1.0, in1=v3(cwp[:]), op0=ALU.mult, op1=ALU.mult), reads=['kk', 'cwp'], writes=[f'AR{q}'])
            S.op('dve', lambda e: e.tensor_tensor(out=AR[:, q, :, 64:128], in0=v3(r_), in1=v3(cw[:, q, :]), op=ALU.mult), reads=[rk, f'cw{q}'], writes=[f'AR{q}'])
            S.op('dve', lambda e: e.tensor_tensor(out=t1[:], in0=kk[:], in1=av[:], op=ALU.mult), reads=['kk', 'av', 't1'], writes=['t1'])
            S.op('dve', lambda e: e.tensor_tensor(out=BK[:, q, :, 0:64], in0=v3(t1[:]), in1=v3(icw[:]), op=ALU.mult), reads=['t1', 'icw'], writes=[f'BK{q}'])
            S.op('dve', lambda e: e.tensor_tensor(out=BK[:, q, :, 64:128], in0=v3(kd[:]), in1=v3(icw[:]), op=ALU.mult), reads=['kd', 'icw'], writes=[f'BK{q}'])
            S.op('pool', lambda e: e.tensor_copy(out=VV[:, q, :, 64:128], in_=v3(v_)), reads=[vkey], writes=[f'VV{q}'])
            S.op('dve', lambda e: e.scalar_tensor_tensor(out=t2[:], in0=r_, scalar=pv[:, pr, 4:5], in1=kd[:], op0=ALU.mult, op1=ALU.mult), reads=[rk, 'kd', 'par', 't2'], writes=['t2'])
            S.op('pe', lambda e: e.matmul(pp[1][:, :TT], lhsT=blk1, rhs=t2[:], start=True, stop=True), reads=['t2', 'par'], writes=['pp1'])
            S.op('dve', lambda e: e.tensor_tensor(out=bvs[:, q, :], in0=pp[1][:, :TT], in1=v_, op=ALU.mult), reads=['pp1', vkey], writes=[f'bvs{q}'])
            S.op('sp', lambda e: e.dma_start(out=bv[pr, :, ti * TT:(ti + 1) * TT], in_=bvs[:, q, :]), reads=[f'bvs{q}'], dma=f'bvs{q}')

        ecnt = [0]

        def evac(dst, src, rd, wr):
            ecnt[0] += 1
            if ecnt[0] % 2:
                S.op('act', lambda e: e.activation(out=dst, in_=src, func=AF.Copy), reads=rd, writes=wr)
            else:
                S.op('dve', lambda e: e.tensor_copy(out=dst, in_=src), reads=rd, writes=wr)

        def chunk(ti, pr, ch):
            q = pr % 2
            os_ = ti % 2
            S.op('pe', lambda e: e.transpose(pp[2][:, 0:128], BK[:, q, ch, :], ident), reads=[f'BK{q}', 'par'], writes=['pp2a'])
            evac(BKt[:], pp[2][:, 0:128], ['pp2a'], ['BKt'])
            S.op('pe', lambda e: e.transpose(pp[2][:, 128:256], VV[:, q, ch, :], ident), reads=[f'VV{q}', 'par'], writes=['pp2b'])
            evac(UV[64:128, :, :], pp[2][64:128, 128:256].rearrange("p (h v) -> p h v", h=2), ['pp2b'], ['UVv'])
            def head(h):
                hp = h * 64
                sl = slice(hp, hp + 64)
                S.op('pe', lambda e: e.matmul(pp[3][:, 0:128], lhsT=BK[sl, q, ch, :], rhs=AR[sl, q, ch, :], start=True, stop=True), reads=[f'BK{q}', f'AR{q}'], writes=['pp3a'])
                S.op('dve', lambda e: e.tensor_tensor(out=A4s[:, h, :], in0=pp[3][:, 0:128], in1=mask4, op=ALU.mult), reads=['pp3a', 'par'], writes=[f'A4s{h}'])
                S.op('pe', lambda e: e.matmul(pp[3][0:64, 128:192], lhsT=AR[sl, q, ch, 0:64], rhs=BK[sl, q, ch, 0:64], start=True, stop=True), reads=[f'BK{q}', f'AR{q}'], writes=['pp3b'])
                S.op('dve', lambda e: e.tensor_tensor(out=Mb[:, h, 0, :], in0=pp[3][0:64, 128:192], in1=cs[0:64, 3, 0:64], op=ALU.mult), reads=['pp3b', 'par'], writes=[f'M{h}_0'])
                S.op('pool', lambda e: e.tensor_copy(out=Nb[:, h, 0, :], in_=A4s[0:64, h, 0:64]), reads=[f'A4s{h}'], writes=[f'N{h}_0'])
                S.op('pool', lambda e: e.tensor_tensor(out=Pb[:, h, 0, :], in0=A4s[0:64, h, 0:64], in1=cs[0:64, 2, 0:64], op=ALU.add), reads=[f'A4s{h}', 'par'], writes=[f'P{h}_0'])
                cur = 0
                for lv in range(1, 6):
                    nx = 1 - cur
                    S.op('pe', lambda e, cur=cur: e.matmul(pp[4][0:64, 0:64], lhsT=Nb[:, h, cur, :], rhs=Mb[:, h, cur, :], start=True, stop=True), reads=[f'N{h}_{cur}', f'M{h}_{cur}'], writes=['pp4a'])
                    evac(Mb[:, h, nx, :], pp[4][0:64, 0:64], ['pp4a'], [f'M{h}_{nx}'])
                    if lv < 5:
                        S.op('pe', lambda e, cur=cur: e.matmul(pp[4][0:64, 64:128], lhsT=Mb[:, h, cur, :], rhs=Nb[:, h, cur, :], start=True, stop=True), reads=[f'N{h}_{cur}', f'M{h}_{cur}'], writes=['pp4b'])
                        evac(Nb[:, h, nx, :], pp[4][0:64, 64:128], ['pp4b'], [f'N{h}_{nx}'])
                    S.op('pe', lambda e, cur=cur, nx=nx: e.matmul(pp[5][0:64, 0:64], lhsT=Mb[:, h, nx, :], rhs=Pb[:, h, cur, :], start=True, stop=True), reads=[f'M{h}_{nx}', f'P{h}_{cur}'], writes=['pp5'])
                    S.op('dve', lambda e, cur=cur, nx=nx: e.tensor_tensor(out=Pb[:, h, nx, :], in0=pp[5][0:64, 0:64], in1=Pb[:, h, cur, :], op=ALU.add), reads=['pp5', f'P{h}_{cur}'], writes=[f'P{h}_{nx}'])
                    cur = nx
                Pf = Pb[:, h, cur, :]
                pkey = f'P{h}_{cur}'
                skey = f'S{pr}'

                S.op('pe', lambda e: e.matmul(pp[6][0:64, 0:64], lhsT=AR[sl, q, ch, 0:64], rhs=Sst[sl, pr, :], start=True, stop=False), reads=[f'AR{q}', skey], writes=['pp6a'])
                S.op('pe', lambda e: e.matmul(pp[6][0:64, 0:64], lhsT=A4s[64:128, h, 0:64], rhs=UV[64:128, h, :], start=False, stop=True), reads=[f'A4s{h}', 'UVv'], writes=['pp6a'], pe_sync=True)
                evac(Rs[:, h, :], pp[6][0:64, 0:64], ['pp6a'], [f'Rs{h}'])
                S.op('pe', lambda e: e.matmul(pp[6][0:64, 64:128], lhsT=Pf, rhs=Rs[:, h, :], start=True, stop=True), reads=[pkey, f'Rs{h}'], writes=['pp6b'])
                evac(UV[0:64, h, :], pp[6][0:64, 64:128], ['pp6b'], [f'UVu{h}'])

                def mm_o(e):
                    e.matmul(pp[6][0:64, 128:192], lhsT=AR[sl, q, ch, 64:128], rhs=Sst[sl, pr, :], start=True, stop=False)
                    return e.matmul(pp[6][0:64, 128:192], lhsT=A4s[:, h, 64:128], rhs=UV[:, h, :], start=False, stop=True)
                S.op('pe', mm_o, reads=[f'AR{q}', skey, f'A4s{h}', 'UVv', f'UVu{h}'], writes=['pp6c'])
                evac(osb[:, os_, ch, (2 * pr + h) * 64:(2 * pr + h + 1) * 64], pp[6][0:64, 128:192], ['pp6c'], [f'osb{os_}'])
                S.op('pe', lambda e: e.matmul(pp[7][:, 0:64], lhsT=BKt[:], rhs=UV[:, h, :], start=True, stop=True), reads=['BKt', 'UVv', f'UVu{h}'], writes=['pp7'])
                S.op('dve', lambda e: e.tensor_tensor(out=St[sl, :], in0=pp[7][sl, 0:64], in1=Sst[sl, pr, :], op=ALU.add), reads=['pp7', skey], writes=['St'])
                S.op('dve', lambda e: e.tensor_scalar(out=Sst[sl, pr, :], in0=St[sl, :], scalar1=cw[sl, q, ch * 64 + 63:ch * 64 + 64], scalar2=None, op0=ALU.mult), reads=['St', f'cw{q}'], writes=[skey])

            for h in range(2):
                head(h)

        for ti in range(TK // TT):
            c0 = ti * TT
            os_ = ti % 2
            tile_prep(ti)
            for pr in range(4):
                pair_prep(ti, pr)
                for ch in range(NCT):
                    chunk(ti, pr, ch)
            S.op('sp', lambda e, c0=c0, os_=os_: e.dma_start(out=o[c0:c0 + TT, :].rearrange("(c t) f -> t c f", t=64), in_=osb[:, os_, :, :]), reads=[f'osb{os_}'], dma=f'osb{os_}')
        for s in list(S.cnt):
            if s.startswith('D_osb') or s.startswith('D_bvs'):
                S.ops['sp'].append(([(s, S.cnt[s])], None, None, 0))
        S.emit(st)
    return nc


def rs_consts():
    c = np.zeros((128, 4, 128), np.float32)
    p = np.arange(128)[:, None]
    q = np.arange(128)[None, :]
    c[:, 0, :] = (p // 64 == q // 64)
    j = p % 64
    t = q % 64
    c[:, 1, :] = np.where(q < 64, j < t, j <= t)
    c[:, 2, :] = (p == q)
    c[:, 3, :] = (j > t)
    return c


def prep_RS(l, I, zlat, zctx, d):
    def seg(a):
        a = a[::-1] if d == 1 else a
        zl = np.concatenate([np.zeros_like(a[:1]), a[:-1]], axis=0)
        zr = np.concatenate([a[1:], np.zeros_like(a[:1])], axis=0)
        return a, zl, zr
    c0, cl, cr = seg(zctx)
    a0, al, ar = seg(zlat)
    s0 = np.concatenate([c0, a0], axis=0)
    sL = np.concatenate([cl, al], axis=0)
    sR = np.concatenate([cr, ar], axis=0)
    mp, mn = I['rw_mu_prev'][l], I['rw_mu_next'][l]
    muL, muR = (mp, mn) if d == 0 else (mn, mp)
    wl0, al0 = 1536 + d * 96, 1536 + 192 + d * 96

    def rk(s):
        return np.ascontiguousarray(s[:, 0:1536].T).reshape(12, 128, TK)

    def lo(s):
        return np.ascontiguousarray(np.stack([s[:, wl0:wl0 + 96].T, s[:, al0:al0 + 96].T]))
    mu = np.ascontiguousarray(np.stack([muL[0:1536].reshape(12, 128).T, muR[0:1536].reshape(12, 128).T], axis=2))
    mul = np.ascontiguousarray(np.stack([np.stack([muL[wl0:wl0 + 96], muL[al0:al0 + 96]], axis=1), np.stack([muR[wl0:wl0 + 96], muR[al0:al0 + 96]], axis=1)], axis=2))
    pv = np.stack([I['rw_w0'][l][d], I['rw_a0'][l][d], I['rw_k_k'][l], I['rw_k_a'][l], I['rw_r_k'][l].reshape(512)], axis=1)
    pvec = np.ascontiguousarray(pv.reshape(4, 128, 5).transpose(1, 0, 2))
    return {"z0": rk(s0), "zL": rk(sL), "zR": rk(sR), "l0": lo(s0), "lL": lo(sL), "lR": lo(sR), "mu": mu, "mul": mul, "pvec": pvec,
            "wup": np.ascontiguousarray(I['rw_w_up'][l][d]), "aup": np.ascontiguousarray(I['rw_a_up'][l][d]), "cst": rs_consts()}


def unrev(a, d, axis):
    if d == 0:
        return a
    c, t = np.split(a, [CTX], axis=axis)
    return np.concatenate([np.flip(c, axis), np.flip(t, axis)], axis=axis)


RW_GN_EPS = 64e-5


def build_RO():
    nc = bass.Bass("TRN2", target_bir_lowering=False)
    T = TPC
    of = nc.dram_tensor("of", [4, 128, T], F32, kind="ExternalInput").ap()
    ob = nc.dram_tensor("ob", [4, 128, T], F32, kind="ExternalInput").ap()
    bf_ = nc.dram_tensor("bvf", [4, 128, T], F32, kind="ExternalInput").ap()
    bb_ = nc.dram_tensor("bvb", [4, 128, T], F32, kind="ExternalInput").ap()
    g0 = nc.dram_tensor("g0", [2, 128, T], F32, kind="ExternalInput").ap()
    gL = nc.dram_tensor("gL", [2, 128, T], F32, kind="ExternalInput").ap()
    gR = nc.dram_tensor("gR", [2, 128, T], F32, kind="ExternalInput").ap()
    gmu = nc.dram_tensor("gmu", [128, 2, 2], F32, kind="ExternalInput").ap()
    gup = nc.dram_tensor("gup", [128, 2, 512], F32, kind="ExternalInput").ap()
    lnp = nc.dram_tensor("lnp", [128, 4, 2], F32, kind="ExternalInput").ap()
    blk = nc.dram_tensor("blk", [128, 128], F32, kind="ExternalInput").ap()
    ya = nc.dram_tensor("ya", [4, 128, T], BF16, kind="ExternalOutput").ap()
    with ExitStack() as st:
        sb = lambda name, shape, dt: st.enter_context(nc.sbuf_tensor(name, shape, dt))
        a_ = sb("a_", [128, 2, 512], F32)
        b_ = sb("b_", [128, 2, 512], F32)
        c_ = sb("c_", [128, 2, 512], F32)
        d_ = sb("d_", [128, 2, 512], F32)
        gz = sb("gz", [128, 3, 2, 512], F32)
        sgd = sb("sgd", [128, 2, 512], F32)
        xc = sb("xc", [128, 512], F32)
        sq = sb("sq", [128, 512], F32)
        rs_ = sb("rs_", [128, 512], F32)
        yo = sb("yo", [128, 2, 512], BF16)
        gmus = sb("gmus", [128, 2, 2], F32)
        gups = sb("gups", [128, 2, 512], F32)
        lns = sb("lns", [128, 4, 2], F32)
        blks = sb("blks", [128, 128], F32)
        eps_t = sb("eps", [128, 1], F32)
        p0 = st.enter_context(nc.psum_tensor("p0", [128, 512], F32))
        p1 = st.enter_context(nc.psum_tensor("p1", [128, 512], F32))
        p2 = st.enter_context(nc.psum_tensor("p2", [128, 512], F32))
        S = Sched(nc)
        S.alias = {'par': [f'par{i}' for i in range(4)]}
        S.op('sp', lambda e: e.dma_start(out=gmus[:], in_=gmu), writes=['par0'], dma='par0')
        S.op('sp', lambda e: e.dma_start(out=gups[:], in_=gup), writes=['par1'], dma='par1')
        S.op('sp', lambda e: e.dma_start(out=lns[:], in_=lnp), writes=['par2'], dma='par2')
        S.op('sp', lambda e: e.dma_start(out=blks[:], in_=blk), writes=['par3'], dma='par3')
        S.op('dve', lambda e: e.memset(eps_t[:], RW_GN_EPS), writes=['eps'])
        cnt = [0]

        def tile(t0, n):
            for j in range(2):
                for w, src in enumerate((g0, gL, gR)):
                    S.op('sp', lambda e, j=j, w=w, src=src: e.dma_start(out=gz[:, w, j, :n], in_=src[j, :, t0:t0 + n]), writes=[f'gz{w}{j}'], dma=f'gz{w}{j}')
                S.op('pool', lambda e, j=j: e.tensor_tensor(out=gz[:, 1, j, :n], in0=gz[:, 1, j, :n], in1=gz[:, 0, j, :n], op=ALU.subtract), reads=[f'gz0{j}', f'gz1{j}'], writes=[f'gz1{j}'])
                S.op('pool', lambda e, j=j: e.tensor_tensor(out=gz[:, 2, j, :n], in0=gz[:, 2, j, :n], in1=gz[:, 0, j, :n], op=ALU.subtract), reads=[f'gz0{j}', f'gz2{j}'], writes=[f'gz2{j}'])
                S.op('dve', lambda e, j=j: e.scalar_tensor_tensor(out=gz[:, 1, j, :n], in0=gz[:, 1, j, :n], scalar=gmus[:, j, 0:1], in1=gz[:, 0, j, :n], op0=ALU.mult, op1=ALU.add), reads=[f'gz0{j}', f'gz1{j}', 'par'], writes=[f'gz1{j}'])
                S.op('dve', lambda e, j=j: e.scalar_tensor_tensor(out=gz[:, 2, j, :n], in0=gz[:, 2, j, :n], scalar=gmus[:, j, 1:2], in1=gz[:, 1, j, :n], op0=ALU.mult, op1=ALU.add), reads=[f'gz1{j}', f'gz2{j}', 'par'], writes=[f'gz2{j}'])
                S.op('act', lambda e, j=j: e.activation(out=sgd[:, j, :n], in_=gz[:, 2, j, :n], func=AF.Sigmoid), reads=[f'gz2{j}'], writes=[f'sgd{j}'])

            def pair(pr):
                i = cnt[0] % 2
                cnt[0] += 1
                for buf, src, nm in ((a_, of, 'a'), (b_, ob, 'b'), (c_, bf_, 'c'), (d_, bb_, 'd')):
                    S.op('sp', lambda e, buf=buf, src=src: e.dma_start(out=buf[:, i, :n], in_=src[pr, :, t0:t0 + n]), writes=[f'{nm}{i}'], dma=f'{nm}{i}')
                S.op('dve', lambda e: e.tensor_tensor(out=a_[:, i, :n], in0=a_[:, i, :n], in1=b_[:, i, :n], op=ALU.add), reads=[f'a{i}', f'b{i}'], writes=[f'a{i}'])
                S.op('pe', lambda e: e.matmul(p0[:, :n], lhsT=blks[:], rhs=a_[:, i, :n], start=True, stop=True), reads=[f'a{i}', 'par'], writes=['p0'])
                S.op('dve', lambda e: e.tensor_tensor(out=xc[:, :n], in0=a_[:, i, :n], in1=p0[:, :n], op=ALU.subtract), reads=[f'a{i}', 'p0'], writes=['xc'])
                S.op('act', lambda e: e.activation(out=sq[:, :n], in_=xc[:, :n], func=AF.Square), reads=['xc'], writes=['sq'])
                S.op('pe', lambda e: e.matmul(p1[:, :n], lhsT=blks[:], rhs=sq[:, :n], start=True, stop=True), reads=['sq', 'par'], writes=['p1'])
                S.op('act', lambda e: e.activation(out=rs_[:, :n], in_=p1[:, :n], func=AF.Sqrt, bias=eps_t[:], scale=1.0), reads=['p1', 'eps'], writes=['rs'])
                S.op('dve', lambda e: e.reciprocal(out=rs_[:, :n], in_=rs_[:, :n]), reads=['rs'], writes=['rs'])
                S.op('dve', lambda e: e.tensor_tensor(out=xc[:, :n], in0=xc[:, :n], in1=rs_[:, :n], op=ALU.mult), reads=['xc', 'rs'], writes=['xc'])
                S.op('act', lambda e: e.activation(out=xc[:, :n], in_=xc[:, :n], func=AF.Identity, bias=lns[:, pr, 1:2], scale=lns[:, pr, 0:1]), reads=['xc', 'par'], writes=['xc'])
                S.op('pool', lambda e: e.tensor_tensor(out=c_[:, i, :n], in0=c_[:, i, :n], in1=d_[:, i, :n], op=ALU.add), reads=[f'c{i}', f'd{i}'], writes=[f'c{i}'])
                S.op('dve', lambda e: e.tensor_tensor(out=xc[:, :n], in0=xc[:, :n], in1=c_[:, i, :n], op=ALU.add), reads=['xc', f'c{i}'], writes=['xc'])

                def mmg(e):
                    e.matmul(p2[:, :n], lhsT=gups[:, 0, pr * 128:(pr + 1) * 128], rhs=sgd[:, 0, :n], start=True, stop=False)
                    return e.matmul(p2[:, :n], lhsT=gups[:, 1, pr * 128:(pr + 1) * 128], rhs=sgd[:, 1, :n], start=False, stop=True)
                S.op('pe', mmg, reads=['sgd0', 'sgd1', 'par'], writes=['p2'])
                S.op('dve', lambda e: e.tensor_tensor(out=yo[:, i, :n], in0=xc[:, :n], in1=p2[:, :n], op=ALU.mult), reads=['xc', 'p2'], writes=[f'yo{i}'])
                S.op('sp', lambda e: e.dma_start(out=ya[pr, :, t0:t0 + n], in_=yo[:, i, :n]), reads=[f'yo{i}'], dma=f'yo{i}')
            for pr in range(4):
                pair(pr)
        for (t0, n) in token_tiles(T, 512):
            tile(t0, n)
        for i in range(2):
            S.ops['sp'].append(([(f'D_yo{i}', S.cnt[f'D_yo{i}'])], None, None, 0))
        S.emit(st)
    return nc


_PROGS = {}


def _prog(name, fn):
    if name not in _PROGS:
        _PROGS[name] = fn()
    return _PROGS[name]


def _run(nc, ins):
    res = run_bass_kernel_spmd(nc, ins, core_ids=list(range(len(ins))))
    return res.results


def _core_tok(s):
    return np.concatenate([np.arange(s * 2048, (s + 1) * 2048), SEQ + np.arange(s * 128, (s + 1) * 128)])


def _nbrs(a):
    def seg(x):
        z = np.zeros_like(x[:, :1])
        return np.concatenate([z, x[:, :-1]], axis=1), np.concatenate([x[:, 1:], z], axis=1)
    ll, lr = seg(a[:, :SEQ])
    cl, cr = seg(a[:, SEQ:])
    return np.concatenate([ll, cl], axis=1), np.concatenate([lr, cr], axis=1)


def kernel(**I):
    I = {k: np.asarray(v) for k, v in I.items()}
    x, ctx = I['x'], I['ctx']
    xT = []
    for c in range(8):
        b, s = c // 2, c % 2
        xT.append(np.ascontiguousarray(np.concatenate([x[b, s * 2048:(s + 1) * 2048].T, ctx[b, s * 128:(s + 1) * 128].T], axis=1)))
    C, Sg, Pm = rope_tables()
    pm = _perm_heads(np.arange(128), 128)
    blk64 = np.ascontiguousarray(rs_consts()[:, 0, :] / 64.0).astype(np.float32)
    for l in range(DEPTH):
        shP = prep_P_shared(l, I)
        res = _run(_prog('P', build_P), [dict(shP, xT=xT[c], cT=cT_of(I, c // 2)) for c in range(8)])
        ZA, ZB = [], []
        for b in range(NB):
            for nm, lst in (('zA', ZA), ('zB', ZB)):
                r0, r1 = res[2 * b][nm], res[2 * b + 1][nm]
                lst.append(np.concatenate([r0[:, :2048], r1[:, :2048], r0[:, 2048:], r1[:, 2048:]], axis=1))
        del res
        gains = np.ascontiguousarray(np.stack([I['gq_q_norm'][l][pm], I['gq_k_norm'][l][pm]], axis=1))
        ins = []
        for c in range(8):
            b, g = c // 2, c % 2
            q0 = RWC + 4 * g * 128
            ins.append({"qT": np.ascontiguousarray(ZA[b][q0:q0 + 512].reshape(4, 128, TK)),
                        "kT": np.ascontiguousarray(ZA[b][RWC + 1024 + g * 128:RWC + 1024 + (g + 1) * 128]),
                        "V": np.ascontiguousarray(ZB[b][g * 128:(g + 1) * 128].T),
                        "ropeC": C, "ropeS": Sg, "gains": gains, "pswap": Pm})
        rGQ = _run(_prog('GQ', build_GQ), ins)
        ins = []
        for c in range(8):
            b, hg = c // 2, c % 2
            o_ = 256 + hg * 256
            ins.append({"qT": np.ascontiguousarray(ZB[b][o_:o_ + 256].reshape(4, 64, TK)),
                        "kT": np.ascontiguousarray(ZB[b][o_ + 512:o_ + 768].reshape(4, 64, TK)),
                        "V": np.ascontiguousarray(ZB[b][o_ + 1024:o_ + 1280].reshape(4, 64, TK).transpose(0, 2, 1)),
                        "bias": na_bias(I['na_rpb'][l][4 * hg:4 * hg + 4])})
        rNA = _run(_prog('NA', build_NA), ins)
        ins = []
        for c in range(8):
            b, d = c // 2, c % 2
            zt = np.ascontiguousarray(ZA[b][0:RWC].T)
            ins.append(prep_RS(l, I, zt[:SEQ], zt[SEQ:], d))
        rRS = _run(_prog('RS', build_RS), ins)
        del ins
        mp, mn = I['rw_mu_prev'][l], I['rw_mu_next'][l]
        gmu = np.ascontiguousarray(np.stack([mp[1920:2176].reshape(2, 128).T, mn[1920:2176].reshape(2, 128).T], axis=2))
        gup = np.ascontiguousarray(I['rw_g_up'][l].reshape(2, 128, 512).transpose(1, 0, 2))
        lnp = np.ascontiguousarray(np.stack([I['rw_ln_w'][l].reshape(4, 128).T, I['rw_ln_b'][l].reshape(4, 128).T], axis=2))
        per_b = []
        for b in range(NB):
            d_ = {}
            for d, nm in ((0, 'f'), (1, 'b')):
                o = unrev(rRS[2 * b + d]["o"], d, 0)
                o = np.concatenate([o[CTX:], o[:CTX]], axis=0)
                d_['o' + nm] = np.ascontiguousarray(o.T).reshape(4, 128, TK)
                v_ = unrev(rRS[2 * b + d]["bv"], d, 2)
                d_['bv' + nm] = np.concatenate([v_[:, :, CTX:], v_[:, :, :CTX]], axis=2)
            g0 = ZA[b][1920:2176]
            gl, gr = _nbrs(g0)
            d_['g0'], d_['gL'], d_['gR'] = g0, gl, gr
            per_b.append(d_)
        del rRS
        ins = []
        for c in range(8):
            b, s = c // 2, c % 2
            tk = _core_tok(s)
            pb = per_b[b]
            ins.append({"of": np.ascontiguousarray(pb['of'][:, :, tk]), "ob": np.ascontiguousarray(pb['ob'][:, :, tk]),
                        "bvf": np.ascontiguousarray(pb['bvf'][:, :, tk]), "bvb": np.ascontiguousarray(pb['bvb'][:, :, tk]),
                        "g0": np.ascontiguousarray(pb['g0'][:, tk]).reshape(2, 128, TPC), "gL": np.ascontiguousarray(pb['gL'][:, tk]).reshape(2, 128, TPC),
                        "gR": np.ascontiguousarray(pb['gR'][:, tk]).reshape(2, 128, TPC), "gmu": gmu, "gup": gup, "lnp": lnp, "blk": blk64})
        rRO = _run(_prog('RO', build_RO), ins)
        del per_b, ins
        final = (l == DEPTH - 1)
        shM = prep_MF_shared(l, I, final)
        ins = []
        for c in range(8):
            b, s = c // 2, c % 2
            tk = _core_tok(s)
            ybT = np.concatenate([rGQ[2 * b]["yb"].reshape(512, TK), rGQ[2 * b + 1]["yb"].reshape(512, TK)], axis=0)[:, tk]
            ycT = np.concatenate([rNA[2 * b]["yc"].reshape(256, TK), rNA[2 * b + 1]["yc"].reshape(256, TK)], axis=0)[:, tk]
            yT = np.ascontiguousarray(np.concatenate([rRO[c]["ya"].reshape(512, TPC), ybT, ycT], axis=0))
            ins.append(dict(shM, xT=xT[c], yT=yT, cT=cT_of(I, b)))
        res = _run(_prog('MF1' if final else 'MF0', lambda: build_MF(final)), ins)
        xT = [r["xo"] for r in res]
        del res, ins, shM, shP, ZA, ZB, rGQ, rNA, rRO
    out = np.empty((NB, SEQ, D), np.float32)
    for c in range(8):
        b, s = c // 2, c % 2
        out[b, s * 2048:(s + 1) * 2048] = xT[c][:, :2048].T
    return out
```

```python
import numpy as np
from contextlib import ExitStack
import concourse.bass as bass
import concourse.mybir as mybir
from concourse.bass_utils import run_bass_kernel_spmd

F32 = mybir.dt.float32
BF16 = mybir.dt.bfloat16
AF = mybir.ActivationFunctionType
ALU = mybir.AluOpType
AX = mybir.AxisListType

D = 2048
KC = 16
SEQ = 4096
CTX = 256
DEPTH = 4
NB = 4
TPC = 2176
NORM_EPS = 1e-6


class Sched:
    ENG = ('pe', 'dve', 'act', 'pool', 'sp')

    def __init__(self, nc):
        self.nc = nc
        self.ops = {e: [] for e in self.ENG}
        self.cnt = {}
        self.waited = {}
        self.last_w = {}
        self.readers = {}
        self.semnames = []

    def _sem(self, name):
        if name not in self.cnt:
            self.cnt[name] = 0
            self.semnames.append(name)
        return name

    def op(self, eng, fn, reads=(), writes=(), dma=None, pe_sync=False):
        al = getattr(self, 'alias', None)
        if al:
            reads = [x for k in reads for x in al.get(k, [k])]
        if getattr(self, 'psum_excl', False):
            nb = lambda k: k[:3] if (k.startswith('pp') and k[2].isdigit()) else k
            reads = [nb(k) for k in reads]
            writes = [nb(k) for k in writes]
            writes = list(writes) + [k for k in reads if k.startswith('pp') and k[2].isdigit() and k not in writes]
        deps = []
        for k in reads:
            if k in self.last_w:
                deps.append(self.last_w[k])
        for k in writes:
            if k in self.last_w:
                deps.append(self.last_w[k])
            deps.extend(self.readers.get(k, ()))
        waits = {}
        for (s, v) in deps:
            if s == 'E_pe' and eng == 'pe' and dma is None:
                continue
            if self.waited.get((eng, s), 0) < v:
                waits[s] = max(waits.get(s, 0), v)
        if pe_sync and self.cnt.get('E_pe', 0) > 0:
            waits['E_pe'] = self.cnt['E_pe']
        for s, v in waits.items():
            self.waited[(eng, s)] = v
        if dma is None:
            s = self._sem('E_' + eng)
            inc = 1
        else:
            s = self._sem('D_' + dma)
            inc = 16
        self.cnt[s] += inc
        tok = (s, self.cnt[s])
        self.ops[eng].append((sorted(waits.items()), fn, s, inc))
        for k in reads:
            self.readers.setdefault(k, []).append(tok)
        for k in writes:
            self.last_w[k] = tok
            self.readers[k] = []
        return tok

    def wait_all(self, eng, keys):
        waits = {}
        for k in keys:
            if k in self.last_w:
                s, v = self.last_w[k]
                if self.waited.get((eng, s), 0) < v:
                    waits[s] = max(waits.get(s, 0), v)
        for s, v in waits.items():
            self.waited[(eng, s)] = v
        self.ops[eng].append((sorted(waits.items()), None, None, 0))

    def emit(self, stack):
        nc = self.nc
        sems = {}
        for name in self.semnames:
            sems[name] = stack.enter_context(nc.semaphore(name))
        block = stack.enter_context(nc.Block())

        def run(eng_name):
            def body(e):
                for waits, fn, s, inc in self.ops[eng_name]:
                    for ws, wv in waits:
                        e.wait_ge(sems[ws], wv)
                    if fn is not None:
                        ins = fn(e)
                        ins.then_inc(sems[s], inc)
            return body
        if self.ops['sp']:
            block.sync(run('sp'))
        if self.ops['pe']:
            block.tensor(run('pe'))
        if self.ops['dve']:
            block.vector(run('dve'))
        if self.ops['act']:
            block.scalar(run('act'))
        if self.ops['pool']:
            block.gpsimd(run('pool'))


def token_tiles(T, step):
    out = []
    t = 0
    while t < T:
        n = min(step, T - t)
        out.append((t, n))
        t += n
    return out


def emit_mod(S, nc, st, cT, adaw, adab, nfc, x32, ps_mod, consts, prefix):
    c32 = st.enter_context(nc.sbuf_tensor(prefix + "c32", [128, KC, 2], F32))
    sc = st.enter_context(nc.sbuf_tensor(prefix + "sc", [128, KC, 2], F32))
    bsb = st.enter_context(nc.sbuf_tensor(prefix + "adab", [128, nfc], F32))
    mod = st.enter_context(nc.sbuf_tensor(prefix + "mod", [128, nfc, 2], F32))
    S.op('sp', lambda e: e.dma_start(out=c32[:], in_=cT), writes=[prefix + 'c32'], dma=prefix + 'c')
    S.op('sp', lambda e: e.dma_start(out=bsb[:], in_=adab), writes=[prefix + 'adab'], dma=prefix + 'b')
    S.op('act', lambda e: e.activation(out=sc[:], in_=c32[:], func=AF.Silu), reads=[prefix + 'c32'], writes=[prefix + 'sc'])
    ngrp = nfc // 2
    for g in range(ngrp):
        slot = g % 2
        S.op('sp', lambda e, g=g, slot=slot: e.dma_start(out=x32[:, slot, :, :], in_=adaw[:, g * 256:(g + 1) * 256].rearrange("(k p) n -> p k n", p=128)),
             writes=[f'x32_{slot}'], dma=f'x32_{slot}')
        for j in range(2):
            fc = g * 2 + j
            def mm(e, slot=slot, j=j):
                for k in range(KC):
                    ins = e.matmul(ps_mod[:, 0:2], lhsT=x32[:, slot, k, j * 128:(j + 1) * 128], rhs=sc[:, k, :], start=(k == 0), stop=(k == KC - 1))
                return ins
            S.op('pe', mm, reads=[f'x32_{slot}', prefix + 'sc'], writes=['ps_mod'])
            S.op('dve', lambda e, fc=fc: e.tensor_scalar(out=mod[:, fc, :], in0=ps_mod[:, 0:2], scalar1=bsb[:, fc:fc + 1], scalar2=None, op0=ALU.add),
                 reads=['ps_mod', prefix + 'adab'], writes=[prefix + 'mod'])
    return mod


def emit_norm_tile(S, nc, x32, slot, n, gm, sh, col, hT, t0, onesD, eps_t, sq, ps_ss, rstd, tmp, hkey='hT'):
    for k in range(KC):
        qs = k % 4
        S.op('act', lambda e, k=k, qs=qs: e.activation(out=sq[:, qs, :n], in_=x32[:, slot, k, :n], func=AF.Square),
             reads=[f'x32_{slot}'], writes=[f'sq{qs}'])
        S.op('pe', lambda e, k=k, qs=qs: e.matmul(ps_ss[:, :n], lhsT=onesD[:], rhs=sq[:, qs, :n], start=(k == 0), stop=(k == KC - 1)),
             reads=[f'sq{qs}'], writes=['ps_ss'])
    S.op('act', lambda e: e.activation(out=rstd[:, :n], in_=ps_ss[:, :n], func=AF.Sqrt, bias=eps_t[:], scale=1.0), reads=['ps_ss'], writes=['rstd'])
    S.op('dve', lambda e: e.reciprocal(out=rstd[:, :n], in_=rstd[:, :n]), reads=['rstd'], writes=['rstd'])
    for k in range(KC):
        ts = k % 2
        S.op('dve', lambda e, k=k, ts=ts: e.tensor_tensor(out=tmp[:, ts, :n], in0=x32[:, slot, k, :n], in1=rstd[:, :n], op=ALU.mult),
             reads=[f'x32_{slot}', 'rstd'], writes=[f'tmp{ts}'])
        S.op('act', lambda e, k=k, ts=ts: e.activation(out=hT[:, k, t0:t0 + n], in_=tmp[:, ts, :n], func=AF.Identity, bias=sh[:, k, col:col + 1], scale=gm[:, k, col:col + 1]),
             reads=[f'tmp{ts}', 'gmsh'], writes=[hkey + str(k)])


def emit_gm(S, nc, st, mod, n_dram, sc_off, prefix):
    nw = st.enter_context(nc.sbuf_tensor(prefix + "nw", [128, KC], F32))
    gm = st.enter_context(nc.sbuf_tensor(prefix + "gm", [128, KC, 2], F32))
    S.op('sp', lambda e: e.dma_start(out=nw[:], in_=n_dram), writes=[prefix + 'nw'], dma=prefix + 'nw')
    for c in range(2):
        S.op('dve', lambda e, c=c: e.scalar_tensor_tensor(out=gm[:, :, c], in0=mod[:, sc_off:sc_off + KC, c], scalar=1.0, in1=nw[:], op0=ALU.add, op1=ALU.mult),
             reads=[prefix + 'nw', 'mod'], writes=['gmsh'])
    return gm


NCH_A = 27
NCH_B = 14
NQSCALE = (27 + 2, 27 + 6)


def build_P():
    nc = bass.Bass("TRN2", target_bir_lowering=False)
    T = TPC
    xT = nc.dram_tensor("xT", [D, T], F32, kind="ExternalInput").ap()
    cT = nc.dram_tensor("cT", [128, KC, 2], F32, kind="ExternalInput").ap()
    adaw = nc.dram_tensor("adaw", [D, 2 * D], F32, kind="ExternalInput").ap()
    adab = nc.dram_tensor("adab", [128, 32], F32, kind="ExternalInput").ap()
    n1 = nc.dram_tensor("n1", [128, KC], F32, kind="ExternalInput").ap()
    win = nc.dram_tensor("win", [D, (NCH_A + NCH_B) * 128], F32, kind="ExternalInput").ap()
    zA = nc.dram_tensor("zA", [NCH_A * 128, T], F32, kind="ExternalOutput").ap()
    zB = nc.dram_tensor("zB", [NCH_B * 128, T], BF16, kind="ExternalOutput").ap()
    with ExitStack() as st:
        sb = lambda name, shape, dt: st.enter_context(nc.sbuf_tensor(name, shape, dt))
        x32 = sb("x32", [128, 2, KC, 256], F32)
        hT = sb("hT", [128, KC, T], BF16)
        sq = sb("sq", [128, 4, 256], F32)
        tmp = sb("tmp", [128, 2, 256], F32)
        rstd = sb("rstd", [128, 256], F32)
        onesD = sb("onesD", [128, 128], F32)
        eps_t = sb("eps", [128, 1], F32)
        wb = sb("wb", [128, 2, KC, 512], BF16)
        zo32 = sb("zo32", [128, 2, T], F32)
        zo16 = sb("zo16", [128, 2, T], BF16)
        ps_mod = st.enter_context(nc.psum_tensor("ps_mod", [128, 512], F32))
        ps_ss = st.enter_context(nc.psum_tensor("ps_ss", [128, 512], F32))
        ps = [st.enter_context(nc.psum_tensor(f"ps{i}", [128, 512], F32)) for i in range(4)]
        S = Sched(nc)
        S.op('dve', lambda e: e.memset(onesD[:], 1.0 / D), writes=['onesD'])
        S.op('dve', lambda e: e.memset(eps_t[:], NORM_EPS), writes=['eps'])
        mod = emit_mod(S, nc, st, cT, adaw, adab, 32, x32, ps_mod, None, "m_")
        S.last_w['mod'] = S.last_w['m_mod']
        gm = emit_gm(S, nc, st, mod, n1, KC, "g_")
        tiles = token_tiles(2048, 256) + [(2048, 128)]
        for i, (t0, n) in enumerate(tiles):
            slot = i % 2
            S.op('sp', lambda e, t0=t0, n=n, slot=slot: e.dma_start(out=x32[:, slot, :, :n], in_=xT[:, t0:t0 + n].rearrange("(k p) t -> p k t", p=128)),
                 writes=[f'x32_{slot}'], dma=f'x32_{slot}')
            col = 0 if t0 < 2048 else 1
            emit_norm_tile(S, nc, x32, slot, n, gm, mod, col, hT, t0, onesD, eps_t, sq, ps_ss, rstd, tmp, hkey=f'hT{i}_')
        hkeys = [f'hT{i}_{k}' for i in range(len(tiles)) for k in range(KC)]
        nch = NCH_A + NCH_B
        mt = token_tiles(T, 512)
        pi = 0
        for g0 in range(0, nch, 4):
            gn = min(4, nch - g0)
            ws = (g0 // 4) % 2
            S.op('pool', lambda e, g0=g0, gn=gn, ws=ws: e.dma_start(out=wb[:, ws, :, :gn * 128], in_=win[:, g0 * 128:(g0 + gn) * 128].rearrange("(k p) n -> p k n", p=128)),
                 writes=[f'wb{ws}'], dma=f'wb{ws}')
            for j in range(gn):
                ch = g0 + j
                isA = ch < NCH_A
                os_ = ch % 2
                okey = (f'zo32_{os_}' if isA else f'zo16_{os_}')
                for (t0, n) in mt:
                    p = ps[pi % 4]
                    pk = f'ps{pi % 4}'
                    pi += 1
                    def mm(e, p=p, ws=ws, j=j, t0=t0, n=n):
                        for k in range(KC):
                            ins = e.matmul(p[:, :n], lhsT=wb[:, ws, k, j * 128:(j + 1) * 128], rhs=hT[:, k, t0:t0 + n], start=(k == 0), stop=(k == KC - 1))
                        return ins
                    S.op('pe', mm, reads=[f'wb{ws}'] + hkeys, writes=[pk])
                    dst = (zo32 if isA else zo16)
                    scale = 0.125 if (NQSCALE[0] <= ch < NQSCALE[1]) else 1.0
                    if (pi % 2) == 0:
                        S.op('act', lambda e, p=p, dst=dst, os_=os_, t0=t0, n=n, scale=scale: e.activation(out=dst[:, os_, t0:t0 + n], in_=p[:, :n], func=AF.Copy, scale=scale),
                             reads=[pk], writes=[okey])
                    else:
                        S.op('dve', lambda e, p=p, dst=dst, os_=os_, t0=t0, n=n, scale=scale: e.tensor_scalar(out=dst[:, os_, t0:t0 + n], in0=p[:, :n], scalar1=scale, scalar2=None, op0=ALU.mult),
                             reads=[pk], writes=[okey])
                if isA:
                    S.op('sp', lambda e, ch=ch, os_=os_: e.dma_start(out=zA[ch * 128:(ch + 1) * 128, :], in_=zo32[:, os_, :]), reads=[okey], dma=f'zo32_{os_}')
                else:
                    S.op('sp', lambda e, ch=ch, os_=os_: e.dma_start(out=zB[(ch - NCH_A) * 128:(ch - NCH_A + 1) * 128, :], in_=zo16[:, os_, :]), reads=[okey], dma=f'zo16_{os_}')
        for s in ('D_zo32_0', 'D_zo32_1', 'D_zo16_0', 'D_zo16_1'):
            S.ops['sp'].append(([(s, S.cnt[s])], None, None, 0))
        S.emit(st)
    return nc


NFF = 44


def build_MF(final=False):
    nc = bass.Bass("TRN2", target_bir_lowering=False)
    T = TPC
    xT = nc.dram_tensor("xT", [D, T], F32, kind="ExternalInput").ap()
    yT = nc.dram_tensor("yT", [D, T], BF16, kind="ExternalInput").ap()
    cT = nc.dram_tensor("cT", [128, KC, 2], F32, kind="ExternalInput").ap()
    adaw = nc.dram_tensor("adaw", [D, 6 * D], F32, kind="ExternalInput").ap()
    adab = nc.dram_tensor("adab", [128, 96], F32, kind="ExternalInput").ap()
    n1 = nc.dram_tensor("n1", [128, KC], F32, kind="ExternalInput").ap()
    n2 = nc.dram_tensor("n2", [128, KC], F32, kind="ExternalInput").ap()
    wgb = nc.dram_tensor("wgb", [KC, 128, KC * 4 * 128], F32, kind="ExternalInput").ap()
    wout = nc.dram_tensor("wout", [KC, 128, KC * 128], F32, kind="ExternalInput").ap()
    w13 = nc.dram_tensor("w13", [NFF, 128, KC * 2 * 128], F32, kind="ExternalInput").ap()
    w2 = nc.dram_tensor("w2", [KC, 128, NFF * 128], F32, kind="ExternalInput").ap()
    if final:
        fn = nc.dram_tensor("fn", [128, KC], F32, kind="ExternalInput").ap()
    xo = nc.dram_tensor("xo", [D, T], F32, kind="ExternalOutput").ap()
    with ExitStack() as st:
        sb = lambda name, shape, dt: st.enter_context(nc.sbuf_tensor(name, shape, dt))
        x32 = sb("x32", [128, KC, 512], F32)
        x32v = x32[:].rearrange("p k (s t) -> p s k t", s=2)
        hT = sb("hT", [128, KC, 512], BF16)
        bufA = sb("bufA", [128, NFF, 512], BF16)
        wbuf = sb("wbuf", [128, 2, 8192], BF16)
        gsb = sb("gsb", [128, 3, 512], F32)
        asb = sb("asb", [128, 2, 512], F32)
        sq = sb("sq", [128, 4, 512], F32)
        tmp = sb("tmp", [128, 2, 512], F32)
        rstd = sb("rstd", [128, 512], F32)
        onesD = sb("onesD", [128, 128], F32)
        eps_t = sb("eps", [128, 1], F32)
        if final:
            fnsb = sb("fnsb", [128, KC], F32)
        ps_mod = st.enter_context(nc.psum_tensor("ps_mod", [128, 512], F32))
        ps_ss = st.enter_context(nc.psum_tensor("ps_ss", [128, 512], F32))
        ps = [st.enter_context(nc.psum_tensor(f"ps{i}", [128, 512], F32)) for i in range(6)]
        S = Sched(nc)
        S.op('dve', lambda e: e.memset(onesD[:], 1.0 / D), writes=['onesD'])
        S.op('dve', lambda e: e.memset(eps_t[:], NORM_EPS), writes=['eps'])
        if final:
            S.op('sp', lambda e: e.dma_start(out=fnsb[:], in_=fn), writes=['fnsb'], dma='fn')

        class X32V:
            def __getitem__(self, idx):
                return x32v[idx]
        mod = emit_mod(S, nc, st, cT, adaw, adab, 96, X32V(), ps_mod, None, "m_")
        S.last_w['mod'] = S.last_w['m_mod']
        gm1 = emit_gm(S, nc, st, mod, n1, 16, "g1_")
        gm2 = emit_gm(S, nc, st, mod, n2, 64, "g2_")

        class X32S:
            def __getitem__(self, idx):
                return x32[(idx[0],) + tuple(idx[2:])]
        xs = X32S()
        wslot = [0]

        def wload(src_ap, nelem):
            ws = wslot[0] % 2
            wslot[0] += 1
            S.op('pool', lambda e, ws=ws: e.dma_start(out=wbuf[:, ws, 0:nelem], in_=src_ap), writes=[f'wbuf{ws}'], dma=f'wbuf{ws}')
            return ws

        pcount = [0]

        def nextps():
            i = pcount[0] % 6
            pcount[0] += 1
            return ps[i], f'ps{i}'

        mt = token_tiles(T, 512)

        def do_tile(ti, t0, n):
            col = 0 if t0 < 2048 else 1
            S.op('sp', lambda e, t0=t0, n=n: e.dma_start(out=x32[:, :, :n], in_=xT[:, t0:t0 + n].rearrange("(k p) t -> p k t", p=128)),
                 writes=['x32_0', 'x32_1'] + [f'xc{k}' for k in range(KC)], dma='x32')
            S.op('sp', lambda e, t0=t0, n=n: e.dma_start(out=bufA[:, 0:KC, :n], in_=yT[:, t0:t0 + n].rearrange("(k p) t -> p k t", p=128)),
                 writes=[f'ba{k}' for k in range(KC)], dma='yT')
            emit_norm_tile(S, nc, xs, 0, n, gm1, mod[:, 0:16, :], col, hT, 0, onesD, eps_t, sq, ps_ss, rstd, tmp, hkey='h')
            hk = [f'h{k}' for k in range(KC)]
            for dc in range(KC):
                ws = wload(wgb[dc], KC * 4 * 128)
                wv = wbuf[:, ws, :].rearrange("p (k g n) -> p k g n", k=KC, g=4)
                pg = []
                for a in range(3):
                    p, pk = nextps()
                    def mmg(e, p=p, wv=wv, a=a):
                        for k in range(KC):
                            ins = e.matmul(p[:, :n], lhsT=wv[:, k, a, :], rhs=hT[:, k, :n], start=(k == 0), stop=(k == KC - 1))
                        return ins
                    S.op('pe', mmg, reads=[f'wbuf{ws}'] + hk, writes=[pk])
                    S.op('act', lambda e, p=p, a=a: e.activation(out=gsb[:, a, :n], in_=p[:, :n], func=AF.Sigmoid), reads=[pk], writes=[f'gsb{a}'])
                    pg.append((p, pk))
                pp = []
                for a, (k0, k1) in enumerate(((0, 4), (4, 12), (12, 16))):
                    p, pk = nextps()
                    def mmp(e, p=p, wv=wv, k0=k0, k1=k1):
                        for k in range(k0, k1):
                            ins = e.matmul(p[:, :n], lhsT=wv[:, k, 3, :], rhs=bufA[:, k, :n], start=(k == k0), stop=(k == k1 - 1))
                        return ins
                    S.op('pe', mmp, reads=[f'wbuf{ws}'] + [f'ba{k}' for k in range(k0, k1)], writes=[pk])
                    pp.append((p, pk))
                S.op('dve', lambda e, p=pp[0][0]: e.tensor_tensor(out=tmp[:, 0, :n], in0=gsb[:, 0, :n], in1=p[:, :n], op=ALU.mult), reads=['gsb0', pp[0][1]], writes=['tmp0'])
                S.op('dve', lambda e, p=pp[1][0]: e.tensor_tensor(out=tmp[:, 1, :n], in0=gsb[:, 1, :n], in1=p[:, :n], op=ALU.mult), reads=['gsb1', pp[1][1]], writes=['tmp1'])
                S.op('dve', lambda e: e.tensor_tensor(out=tmp[:, 0, :n], in0=tmp[:, 0, :n], in1=tmp[:, 1, :n], op=ALU.add), reads=['tmp0', 'tmp1'], writes=['tmp0'])
                S.op('dve', lambda e, p=pp[2][0]: e.tensor_tensor(out=tmp[:, 1, :n], in0=gsb[:, 2, :n], in1=p[:, :n], op=ALU.mult), reads=['gsb2', pp[2][1]], writes=['tmp1'])
                S.op('dve', lambda e, dc=dc: e.tensor_tensor(out=bufA[:, KC + dc, :n], in0=tmp[:, 0, :n], in1=tmp[:, 1, :n], op=ALU.add), reads=['tmp0', 'tmp1'], writes=[f'ba{KC + dc}'])
            mk = [f'ba{KC + k}' for k in range(KC)]
            for dc in range(KC):
                ws = wload(wout[dc], KC * 128)
                wv = wbuf[:, ws, 0:KC * 128].rearrange("p (k n) -> p k n", k=KC)
                p, pk = nextps()
                def mmo(e, p=p, wv=wv):
                    for k in range(KC):
                        ins = e.matmul(p[:, :n], lhsT=wv[:, k, :], rhs=bufA[:, KC + k, :n], start=(k == 0), stop=(k == KC - 1))
                    return ins
                S.op('pe', mmo, reads=[f'wbuf{ws}'] + mk, writes=[pk])
                S.op('dve', lambda e, p=p, dc=dc: e.scalar_tensor_tensor(out=x32[:, dc, :n], in0=p[:, :n], scalar=mod[:, 32 + dc, col:col + 1], in1=x32[:, dc, :n], op0=ALU.mult, op1=ALU.add),
                     reads=[pk, f'xc{dc}', 'x32_0', 'gmsh'], writes=[f'xc{dc}'])
            S.op('dve', lambda e: e.memset(eps_t[:], NORM_EPS), reads=[f'xc{k}' for k in range(KC)], writes=['x32_0', 'x32_1'])
            emit_norm_tile(S, nc, xs, 0, n, gm2, mod[:, 48:64, :], col, hT, 0, onesD, eps_t, sq, ps_ss, rstd, tmp, hkey='h')
            for f in range(NFF):
                ws = wload(w13[f], KC * 2 * 128)
                wv = wbuf[:, ws, 0:KC * 256].rearrange("p (k g n) -> p k g n", k=KC, g=2)
                p1, pk1 = nextps()
                p3, pk3 = nextps()
                for (p, pk, g) in ((p1, pk1, 0), (p3, pk3, 1)):
                    def mmf(e, p=p, wv=wv, g=g):
                        for k in range(KC):
                            ins = e.matmul(p[:, :n], lhsT=wv[:, k, g, :], rhs=hT[:, k, :n], start=(k == 0), stop=(k == KC - 1))
                        return ins
                    S.op('pe', mmf, reads=[f'wbuf{ws}'] + hk, writes=[pk])
                a_s = f % 2
                S.op('act', lambda e, p1=p1, a_s=a_s: e.activation(out=asb[:, a_s, :n], in_=p1[:, :n], func=AF.Silu), reads=[pk1], writes=[f'asb{a_s}'])
                S.op('dve', lambda e, p3=p3, a_s=a_s, f=f: e.tensor_tensor(out=bufA[:, f, :n], in0=asb[:, a_s, :n], in1=p3[:, :n], op=ALU.mult), reads=[pk3, f'asb{a_s}'], writes=[f'ba{f}'])
            ak = [f'ba{f}' for f in range(NFF)]
            for dc in range(KC):
                ws = wload(w2[dc], NFF * 128)
                wv = wbuf[:, ws, 0:NFF * 128].rearrange("p (k n) -> p k n", k=NFF)
                p, pk = nextps()
                def mmd(e, p=p, wv=wv):
                    for f in range(NFF):
                        ins = e.matmul(p[:, :n], lhsT=wv[:, f, :], rhs=bufA[:, f, :n], start=(f == 0), stop=(f == NFF - 1))
                    return ins
                S.op('pe', mmd, reads=[f'wbuf{ws}'] + ak, writes=[pk])
                S.op('dve', lambda e, p=p, dc=dc: e.scalar_tensor_tensor(out=x32[:, dc, :n], in0=p[:, :n], scalar=mod[:, 80 + dc, col:col + 1], in1=x32[:, dc, :n], op0=ALU.mult, op1=ALU.add),
                     reads=[pk, f'xc{dc}', 'x32_0'], writes=[f'xc{dc}'])
            S.op('dve', lambda e: e.memset(eps_t[:], NORM_EPS), reads=[f'xc{k}' for k in range(KC)], writes=['x32_0', 'x32_1'])
            if final:
                for k in range(KC):
                    qs = k % 4
                    S.op('act', lambda e, k=k, qs=qs: e.activation(out=sq[:, qs, :n], in_=x32[:, k, :n], func=AF.Square), reads=['x32_0'], writes=[f'sq{qs}'])
                    S.op('pe', lambda e, k=k, qs=qs: e.matmul(ps_ss[:, :n], lhsT=onesD[:], rhs=sq[:, qs, :n], start=(k == 0), stop=(k == KC - 1)), reads=[f'sq{qs}'], writes=['ps_ss'])
                S.op('act', lambda e: e.activation(out=rstd[:, :n], in_=ps_ss[:, :n], func=AF.Sqrt, bias=eps_t[:], scale=1.0), reads=['ps_ss'], writes=['rstd'])
                S.op('dve', lambda e: e.reciprocal(out=rstd[:, :n], in_=rstd[:, :n]), reads=['rstd'], writes=['rstd'])
                for k in range(KC):
                    S.op('dve', lambda e, k=k: e.scalar_tensor_tensor(out=x32[:, k, :n], in0=x32[:, k, :n], scalar=fnsb[:, k:k + 1], in1=rstd[:, :n], op0=ALU.mult, op1=ALU.mult),
                         reads=['x32_0', 'rstd', 'fnsb'], writes=[f'xc{k}'])
                S.op('dve', lambda e: e.memset(eps_t[:], NORM_EPS), reads=[f'xc{k}' for k in range(KC)], writes=['x32_0', 'x32_1'])
            S.op('sp', lambda e, t0=t0, n=n: e.dma_start(out=xo[:, t0:t0 + n].rearrange("(k p) t -> p k t", p=128), in_=x32[:, :, :n]),
                 reads=['x32_0', 'x32_1'], dma='xo')
        for ti, (t0, n) in enumerate(mt):
            do_tile(ti, t0, n)
        S.ops['sp'].append(([('D_xo', S.cnt['D_xo'])], None, None, 0))
        S.emit(st)
    return nc


def build_MF2(final=False):
    nc = bass.Bass("TRN2", target_bir_lowering=False)
    T = TPC
    WMAX = 640
    xT = nc.dram_tensor("xT", [D, T], F32, kind="ExternalInput").ap()
    yT = nc.dram_tensor("yT", [D, T], BF16, kind="ExternalInput").ap()
    cT = nc.dram_tensor("cT", [128, KC, 2], F32, kind="ExternalInput").ap()
    adaw = nc.dram_tensor("adaw", [D, 6 * D], F32, kind="ExternalInput").ap()
    adab = nc.dram_tensor("adab", [128, 96], F32, kind="ExternalInput").ap()
    n1 = nc.dram_tensor("n1", [128, KC], F32, kind="ExternalInput").ap()
    n2 = nc.dram_tensor("n2", [128, KC], F32, kind="ExternalInput").ap()
    wgb = nc.dram_tensor("wgb", [KC, 128, KC * 4 * 128], F32, kind="ExternalInput").ap()
    wout = nc.dram_tensor("wout", [KC, 128, KC * 128], F32, kind="ExternalInput").ap()
    w13 = nc.dram_tensor("w13", [NFF, 128, KC * 2 * 128], F32, kind="ExternalInput").ap()
    w2 = nc.dram_tensor("w2", [KC, 128, NFF * 128], F32, kind="ExternalInput").ap()
    if final:
        fn = nc.dram_tensor("fn", [128, KC], F32, kind="ExternalInput").ap()
    xo = nc.dram_tensor("xo", [D, T], F32, kind="ExternalOutput").ap()
    with ExitStack() as st:
        sb = lambda name, shape, dt: st.enter_context(nc.sbuf_tensor(name, shape, dt))
        x32 = sb("x32", [128, KC, WMAX], F32)
        x32v = x32[:, :, 0:512].rearrange("p k (s t) -> p s k t", s=2)
        hT = sb("hT", [128, KC, WMAX], BF16)
        bufA = sb("bufA", [128, NFF, WMAX], BF16)
        wbuf = sb("wbuf", [128, 2, 8192], BF16)
        gsb = sb("gsb", [128, 3, 512], F32)
        asb = sb("asb", [128, 2, 512], F32)
        sq = sb("sq", [128, 4, 512], F32)
        tmp = sb("tmp", [128, 2, 512], F32)
        rstd = sb("rstd", [128, 512], F32)
        onesD = sb("onesD", [128, 128], F32)
        eps_t = sb("eps", [128, 1], F32)
        if final:
            fnsb = sb("fnsb", [128, KC], F32)
        ps_mod = st.enter_context(nc.psum_tensor("ps_mod", [128, 512], F32))
        ps_ss = st.enter_context(nc.psum_tensor("ps_ss", [128, 512], F32))
        ps = [st.enter_context(nc.psum_tensor(f"ps{i}", [128, 512], F32)) for i in range(6)]
        S = Sched(nc)
        S.op('dve', lambda e: e.memset(onesD[:], 1.0 / D), writes=['onesD'])
        S.op('dve', lambda e: e.memset(eps_t[:], NORM_EPS), writes=['eps'])
        if final:
            S.op('sp', lambda e: e.dma_start(out=fnsb[:], in_=fn), writes=['fnsb'], dma='fn')

        class X32V:
            def __getitem__(self, idx):
                return x32v[idx]
        mod = emit_mod(S, nc, st, cT, adaw, adab, 96, X32V(), ps_mod, None, "m_")
        S.last_w['mod'] = S.last_w['m_mod']
        gm1 = emit_gm(S, nc, st, mod, n1, 16, "g1_")
        gm2 = emit_gm(S, nc, st, mod, n2, 64, "g2_")

        class X32S:
            def __init__(self, off):
                self.off = off

            def __getitem__(self, idx):
                sl = idx[3]
                return x32[idx[0], idx[2], self.off:self.off + sl.stop]
        wslot = [0]

        def wload(src_ap, nelem):
            ws = wslot[0] % 2
            wslot[0] += 1
            S.op('pool', lambda e, ws=ws: e.dma_start(out=wbuf[:, ws, 0:nelem], in_=src_ap), writes=[f'wbuf{ws}'], dma=f'wbuf{ws}')
            return ws

        pcount = [0]

        def nextps():
            i = pcount[0] % 6
            pcount[0] += 1
            return ps[i], f'ps{i}'

        mt = [(0, [(0, 512, 0)]), (512, [(0, 512, 0)]), (1024, [(0, 512, 0)]), (1536, [(0, 512, 0), (512, 128, 1)])]

        def do_tile(ti, t0, subs):
            W = sum(n for _, n, _ in subs)
            S.op('sp', lambda e: e.dma_start(out=x32[:, :, :W], in_=xT[:, t0:t0 + W].rearrange("(k p) t -> p k t", p=128)),
                 writes=['x32_0', 'x32_1'] + [f'xc{k}' for k in range(KC)], dma='x32')
            S.op('sp', lambda e: e.dma_start(out=bufA[:, 0:KC, :W], in_=yT[:, t0:t0 + W].rearrange("(k p) t -> p k t", p=128)),
                 writes=[f'ba{k}' for k in range(KC)], dma='yT')
            for (off, n, col) in subs:
                emit_norm_tile(S, nc, X32S(off), 0, n, gm1, mod[:, 0:16, :], col, hT, off, onesD, eps_t, sq, ps_ss, rstd, tmp, hkey='h')
            hk = [f'h{k}' for k in range(KC)]

            def merge_sub(dc, ws, wv, off, n, col):
                pp_ = []
                for a in range(3):
                    p, pk = nextps()

                    def mmg(e, p=p, a=a):
                        for k in range(KC):
                            ins = e.matmul(p[:, :n], lhsT=wv[:, k, a, :], rhs=hT[:, k, off:off + n], start=(k == 0), stop=(k == KC - 1))
                        return ins
                    S.op('pe', mmg, reads=[f'wbuf{ws}'] + hk, writes=[pk])
                    S.op('act', lambda e, p=p, a=a: e.activation(out=gsb[:, a, :n], in_=p[:, :n], func=AF.Sigmoid), reads=[pk], writes=[f'gsb{a}'])
                for a, (k0, k1) in enumerate(((0, 4), (4, 12), (12, 16))):
                    p, pk = nextps()

                    def mmp(e, p=p, k0=k0, k1=k1):
                        for k in range(k0, k1):
                            ins = e.matmul(p[:, :n], lhsT=wv[:, k, 3, :], rhs=bufA[:, k, off:off + n], start=(k == k0), stop=(k == k1 - 1))
                        return ins
                    S.op('pe', mmp, reads=[f'wbuf{ws}'] + [f'ba{k}' for k in range(k0, k1)], writes=[pk])
                    pp_.append((p, pk))
                S.op('dve', lambda e, p=pp_[0][0]: e.tensor_tensor(out=tmp[:, 0, :n], in0=gsb[:, 0, :n], in1=p[:, :n], op=ALU.mult), reads=['gsb0', pp_[0][1]], writes=['tmp0'])
                S.op('dve', lambda e, p=pp_[1][0]: e.tensor_tensor(out=tmp[:, 1, :n], in0=gsb[:, 1, :n], in1=p[:, :n], op=ALU.mult), reads=['gsb1', pp_[1][1]], writes=['tmp1'])
                S.op('dve', lambda e: e.tensor_tensor(out=tmp[:, 0, :n], in0=tmp[:, 0, :n], in1=tmp[:, 1, :n], op=ALU.add), reads=['tmp0', 'tmp1'], writes=['tmp0'])
                S.op('dve', lambda e, p=pp_[2][0]: e.tensor_tensor(out=tmp[:, 1, :n], in0=gsb[:, 2, :n], in1=p[:, :n], op=ALU.mult), reads=['gsb2', pp_[2][1]], writes=['tmp1'])
                S.op('dve', lambda e: e.tensor_tensor(out=bufA[:, KC + dc, off:off + n], in0=tmp[:, 0, :n], in1=tmp[:, 1, :n], op=ALU.add), reads=['tmp0', 'tmp1'], writes=[f'ba{KC + dc}'])

            for dc in range(KC):
                ws = wload(wgb[dc], KC * 4 * 128)
                wv = wbuf[:, ws, :].rearrange("p (k g n) -> p k g n", k=KC, g=4)
                for (off, n, col) in subs:
                    merge_sub(dc, ws, wv, off, n, col)
            mk = [f'ba{KC + k}' for k in range(KC)]

            def out_sub(dc, ws, wv, off, n, col, nk, base, keys, gmod):
                p, pk = nextps()

                def mmo(e):
                    for k in range(nk):
                        ins = e.matmul(p[:, :n], lhsT=wv[:, k, :], rhs=bufA[:, base + k, off:off + n], start=(k == 0), stop=(k == nk - 1))
                    return ins
                S.op('pe', mmo, reads=[f'wbuf{ws}'] + keys, writes=[pk])
                S.op('dve', lambda e: e.scalar_tensor_tensor(out=x32[:, dc, off:off + n], in0=p[:, :n], scalar=mod[:, gmod + dc, col:col + 1], in1=x32[:, dc, off:off + n], op0=ALU.mult, op1=ALU.add),
                     reads=[pk, f'xc{dc}', 'x32_0', 'gmsh'], writes=[f'xc{dc}'])

            for dc in range(KC):
                ws = wload(wout[dc], KC * 128)
                wv = wbuf[:, ws, 0:KC * 128].rearrange("p (k n) -> p k n", k=KC)
                for (off, n, col) in subs:
                    out_sub(dc, ws, wv, off, n, col, KC, KC, mk, 32)
            S.op('dve', lambda e: e.memset(eps_t[:], NORM_EPS), reads=[f'xc{k}' for k in range(KC)], writes=['x32_0', 'x32_1'])
            for (off, n, col) in subs:
                emit_norm_tile(S, nc, X32S(off), 0, n, gm2, mod[:, 48:64, :], col, hT, off, onesD, eps_t, sq, ps_ss, rstd, tmp, hkey='h')

            def up_sub(f, ws, wv, off, n):
                p1, pk1 = nextps()
                p3, pk3 = nextps()
                for (p, pk, g) in ((p1, pk1, 0), (p3, pk3, 1)):
                    def mmf(e, p=p, g=g):
                        for k in range(KC):
                            ins = e.matmul(p[:, :n], lhsT=wv[:, k, g, :], rhs=hT[:, k, off:off + n], start=(k == 0), stop=(k == KC - 1))
                        return ins
                    S.op('pe', mmf, reads=[f'wbuf{ws}'] + hk, writes=[pk])
                a_s = acnt[0] % 2
                acnt[0] += 1
                S.op('act', lambda e: e.activation(out=asb[:, a_s, :n], in_=p1[:, :n], func=AF.Silu), reads=[pk1], writes=[f'asb{a_s}'])
                S.op('dve', lambda e: e.tensor_tensor(out=bufA[:, f, off:off + n], in0=asb[:, a_s, :n], in1=p3[:, :n], op=ALU.mult), reads=[pk3, f'asb{a_s}'], writes=[f'ba{f}'])

            for f in range(NFF):
                ws = wload(w13[f], KC * 2 * 128)
                wv = wbuf[:, ws, 0:KC * 256].rearrange("p (k g n) -> p k g n", k=KC, g=2)
                for (off, n, col) in subs:
                    up_sub(f, ws, wv, off, n)
            ak = [f'ba{f}' for f in range(NFF)]
            for dc in range(KC):
                ws = wload(w2[dc], NFF * 128)
                wv = wbuf[:, ws, 0:NFF * 128].rearrange("p (k n) -> p k n", k=NFF)
                for (off, n, col) in subs:
                    out_sub(dc, ws, wv, off, n, col, NFF, 0, ak, 80)
            S.op('dve', lambda e: e.memset(eps_t[:], NORM_EPS), reads=[f'xc{k}' for k in range(KC)], writes=['x32_0', 'x32_1'])
            if final:
                for (off, n, col) in subs:
                    fin_sub(off, n)
                S.op('dve', lambda e: e.memset(eps_t[:], NORM_EPS), reads=[f'xc{k}' for k in range(KC)], writes=['x32_0', 'x32_1'])
            S.op('sp', lambda e: e.dma_start(out=xo[:, t0:t0 + W].rearrange("(k p) t -> p k t", p=128), in_=x32[:, :, :W]),
                 reads=['x32_0', 'x32_1'], dma='xo')

        def fin_sub(off, n):
            for k in range(KC):
                qs = k % 4
                S.op('act', lambda e, k=k, qs=qs: e.activation(out=sq[:, qs, :n], in_=x32[:, k, off:off + n], func=AF.Square), reads=['x32_0'], writes=[f'sq{qs}'])
                S.op('pe', lambda e, k=k, qs=qs: e.matmul(ps_ss[:, :n], lhsT=onesD[:], rhs=sq[:, qs, :n], start=(k == 0), stop=(k == KC - 1)), reads=[f'sq{qs}'], writes=['ps_ss'])
            S.op('act', lambda e: e.activation(out=rstd[:, :n], in_=ps_ss[:, :n], func=AF.Sqrt, bias=eps_t[:], scale=1.0), reads=['ps_ss'], writes=['rstd'])
            S.op('dve', lambda e: e.reciprocal(out=rstd[:, :n], in_=rstd[:, :n]), reads=['rstd'], writes=['rstd'])
            for k in range(KC):
                S.op('dve', lambda e, k=k: e.scalar_tensor_tensor(out=x32[:, k, off:off + n], in0=x32[:, k, off:off + n], scalar=fnsb[:, k:k + 1], in1=rstd[:, :n], op0=ALU.mult, op1=ALU.mult),
                     reads=['x32_0', 'rstd', 'fnsb'], writes=[f'xc{k}'])

        acnt = [0]
        for ti, (t0, subs) in enumerate(mt):
            do_tile(ti, t0, subs)
        S.ops['sp'].append(([('D_xo', S.cnt['D_xo'])], None, None, 0))
        S.emit(st)
    return nc


RWC = 2176
GQ_OFF = RWC
NA_OFF = RWC + 1536
GATE_OFF = RWC + 1536 + 1536


def _perm_heads(cols, hd):
    cols = np.asarray(cols).reshape(-1, hd)
    return np.concatenate([cols[:, 0::2], cols[:, 1::2]], axis=1).reshape(-1)


def p_cols():
    colsA = np.concatenate([np.arange(0, RWC), _perm_heads(np.arange(GQ_OFF, GQ_OFF + 1024), 128), _perm_heads(np.arange(GQ_OFF + 1024, GQ_OFF + 1280), 128)])
    colsB = np.concatenate([np.arange(GQ_OFF + 1280, GQ_OFF + 1536), np.arange(NA_OFF, NA_OFF + 1536)])
    return np.concatenate([colsA, colsB])


def pk(v):
    return np.ascontiguousarray(v.reshape(-1, 128).T)


def prep_P_shared(l, I):
    return {"adaw": np.ascontiguousarray(I['ada_w'][l][:, :4096]), "adab": pk(I['ada_b'][l][:4096]),
            "n1": pk(I['norm1'][l]), "win": np.ascontiguousarray(I['w_in'][l][:, p_cols()])}


def cT_of(I, b):
    cc = np.stack([I['c'][b], I['c_ctx']], axis=1)
    return np.ascontiguousarray(cc.reshape(16, 128, 2).transpose(1, 0, 2))


def prep_MF_shared(l, I, final):
    w_in = I['w_in'][l]
    wg = w_in[:, GATE_OFF:GATE_OFF + 3 * D].reshape(16, 128, 3, 16, 128)
    wbr = np.concatenate([I['w_br_a'][l], I['w_br_b'][l], I['w_br_c'][l]], axis=0).reshape(16, 128, 1, 16, 128)
    wgb = np.concatenate([wg, wbr], axis=2).transpose(3, 1, 0, 2, 4)
    wgb = np.ascontiguousarray(wgb).reshape(16, 128, 16 * 4 * 128)
    wout = np.ascontiguousarray(I['w_out'][l].reshape(16, 128, 16, 128).transpose(2, 1, 0, 3)).reshape(16, 128, 16 * 128)
    w1 = I['ffn_w1'][l].reshape(16, 128, 1, NFF, 128)
    w3 = I['ffn_w3'][l].reshape(16, 128, 1, NFF, 128)
    w13 = np.ascontiguousarray(np.concatenate([w1, w3], axis=2).transpose(3, 1, 0, 2, 4)).reshape(NFF, 128, 16 * 2 * 128)
    w2 = np.ascontiguousarray(I['ffn_w2'][l].reshape(NFF, 128, 16, 128).transpose(2, 1, 0, 3)).reshape(16, 128, NFF * 128)
    d = {"adaw": np.ascontiguousarray(I['ada_w'][l]), "adab": pk(I['ada_b'][l]), "n1": pk(I['norm1'][l]), "n2": pk(I['norm2'][l]),
         "wgb": wgb, "wout": wout, "w13": w13, "w2": w2}
    if final:
        d["fn"] = pk(I['final_norm'])
    return d


TK = SEQ + CTX


def build_GQ():
    nc = bass.Bass("TRN2", target_bir_lowering=False)
    qT = nc.dram_tensor("qT", [4, 128, TK], F32, kind="ExternalInput").ap()
    kT = nc.dram_tensor("kT", [128, TK], F32, kind="ExternalInput").ap()
    V = nc.dram_tensor("V", [TK, 128], BF16, kind="ExternalInput").ap()
    ropeC = nc.dram_tensor("ropeC", [128, SEQ], F32, kind="ExternalInput").ap()
    ropeS = nc.dram_tensor("ropeS", [128, SEQ], F32, kind="ExternalInput").ap()
    gains = nc.dram_tensor("gains", [128, 2], F32, kind="ExternalInput").ap()
    pswap = nc.dram_tensor("pswap", [128, 128], F32, kind="ExternalInput").ap()
    yb = nc.dram_tensor("yb", [4, 128, TK], BF16, kind="ExternalOutput").ap()
    NKC = TK // 128
    with ExitStack() as st:
        sb = lambda name, shape, dt: st.enter_context(nc.sbuf_tensor(name, shape, dt))
        qr = sb("qr", [128, 4, TK], BF16)
        kr = sb("kr", [128, TK], BF16)
        Vs = sb("Vs", [128, NKC, 128], BF16)
        Ct = sb("Ct", [128, SEQ], F32)
        St = sb("St", [128, SEQ], F32)
        gsb = sb("gsb", [128, 2], F32)
        psw = sb("psw", [128, 128], F32)
        ones32 = sb("ones32", [128, 128], F32)
        ones16 = sb("ones16", [128, 128], BF16)
        eps_t = sb("eps", [128, 1], F32)
        raw = sb("raw", [128, 2, 512], F32)
        sqv = sb("sqv", [128, 2, 512], F32)
        rstd = sb("rstd", [128, 2, 512], F32)
        qn = sb("qn", [128, 2, 512], F32)
        t1 = sb("t1", [128, 2, 512], F32)
        t2 = sb("t2", [128, 2, 512], F32)
        pT = sb("pT", [128, 2, 512], BF16)
        rden = sb("rden", [128, 512], F32)
        yo = sb("yo", [128, 2, 512], BF16)
        ps_ms = st.enter_context(nc.psum_tensor("ps_ms", [128, 512], F32))
        ps_sw = st.enter_context(nc.psum_tensor("ps_sw", [128, 512], F32))
        ps_s = [st.enter_context(nc.psum_tensor(f"ps_s{i}", [128, 512], F32)) for i in range(2)]
        ps_o = [st.enter_context(nc.psum_tensor(f"ps_o{i}", [128, 512], F32)) for i in range(2)]
        ps_d = [st.enter_context(nc.psum_tensor(f"ps_d{i}", [128, 512], F32)) for i in range(2)]
        S = Sched(nc)
        S.op('dve', lambda e: e.memset(ones32[:], 1.0 / 128), writes=['ones32'])
        S.op('dve', lambda e: e.memset(ones16[:], 1.0), writes=['ones16'])
        S.op('dve', lambda e: e.memset(eps_t[:], NORM_EPS), writes=['eps'])
        S.op('sp', lambda e: e.dma_start(out=Ct[:], in_=ropeC), writes=['Ct'], dma='Ct')
        S.op('sp', lambda e: e.dma_start(out=St[:], in_=ropeS), writes=['St'], dma='St')
        S.op('sp', lambda e: e.dma_start(out=gsb[:], in_=gains), writes=['gsb'], dma='gsb')
        S.op('sp', lambda e: e.dma_start(out=psw[:], in_=pswap), writes=['psw'], dma='psw')
        S.op('sp', lambda e: e.dma_start(out=Vs[:], in_=V.rearrange("(c p) d -> p c d", p=128)), writes=['Vs'], dma='Vs')

        cnt = [0]

        def prep(src_ap, dst_ap, gcol, t0, n, rope, dkey):
            i = cnt[0] % 2
            cnt[0] += 1
            S.op('sp', lambda e: e.dma_start(out=raw[:, i, :n], in_=src_ap), writes=[f'raw{i}'], dma=f'raw{i}')
            S.op('act', lambda e: e.activation(out=sqv[:, i, :n], in_=raw[:, i, :n], func=AF.Square), reads=[f'raw{i}'], writes=[f'sqv{i}'])
            S.op('pe', lambda e: e.matmul(ps_ms[:, :n], lhsT=ones32[:], rhs=sqv[:, i, :n], start=True, stop=True), reads=[f'sqv{i}', 'ones32'], writes=['ps_ms'])
            S.op('act', lambda e: e.activation(out=rstd[:, i, :n], in_=ps_ms[:, :n], func=AF.Sqrt, bias=eps_t[:], scale=1.0), reads=['ps_ms', 'eps'], writes=[f'rstd{i}'])
            S.op('dve', lambda e: e.reciprocal(out=rstd[:, i, :n], in_=rstd[:, i, :n]), reads=[f'rstd{i}'], writes=[f'rstd{i}'])
            S.op('dve', lambda e: e.scalar_tensor_tensor(out=qn[:, i, :n], in0=raw[:, i, :n], scalar=gsb[:, gcol:gcol + 1], in1=rstd[:, i, :n], op0=ALU.mult, op1=ALU.mult),
                 reads=[f'raw{i}', f'rstd{i}', 'gsb'], writes=[f'qn{i}'])
            if rope:
                S.op('pe', lambda e: e.matmul(ps_sw[:, :n], lhsT=psw[:], rhs=qn[:, i, :n], start=True, stop=True), reads=[f'qn{i}', 'psw'], writes=['ps_sw'])
                S.op('pool', lambda e: e.tensor_tensor(out=t1[:, i, :n], in0=qn[:, i, :n], in1=Ct[:, t0:t0 + n], op=ALU.mult), reads=[f'qn{i}', 'Ct'], writes=[f't1{i}'])
                S.op('dve', lambda e: e.tensor_tensor(out=t2[:, i, :n], in0=ps_sw[:, :n], in1=St[:, t0:t0 + n], op=ALU.mult), reads=['ps_sw', 'St'], writes=[f't2{i}'])
                S.op('pool', lambda e: e.tensor_tensor(out=dst_ap, in0=t1[:, i, :n], in1=t2[:, i, :n], op=ALU.add), reads=[f't1{i}', f't2{i}'], writes=[dkey])
            else:
                S.op('pool', lambda e: e.tensor_copy(out=dst_ap, in_=qn[:, i, :n]), reads=[f'qn{i}'], writes=[dkey])

        tl = token_tiles(SEQ, 512) + [(SEQ, CTX)]
        kkeys = []
        for (t0, n) in tl:
            prep(kT[:, t0:t0 + n], kr[:, t0:t0 + n], 1, t0, n, t0 < SEQ, f'kr{t0}')
            kkeys.append(f'kr{t0}')
        for h in range(4):
            for (t0, n) in tl:
                prep(qT[h, :, t0:t0 + n], qr[:, h, t0:t0 + n], 0, t0, n, t0 < SEQ, f'qr{h}_{t0}')

        scale = 128 ** -0.5
        acnt = [0]

        def attend(h, t0, n, kcs):
            a = acnt[0] % 2
            acnt[0] += 1
            qk = f'qr{h}_{t0}'
            po, pd = ps_o[a], ps_d[a]

            def smm(j):
                kc = kcs[j]
                b = j % 2
                S.op('pe', lambda e: e.matmul(ps_s[b][:, :n], lhsT=kr[:, kc * 128:(kc + 1) * 128], rhs=qr[:, h, t0:t0 + n], start=True, stop=True),
                     reads=[qk] + kkeys, writes=[f'ps_s{b}'])
            smm(0)
            for j, kc in enumerate(kcs):
                b = j % 2
                if j + 1 < len(kcs):
                    smm(j + 1)
                S.op('act', lambda e, b=b: e.activation(out=pT[:, b, :n], in_=ps_s[b][:, :n], func=AF.Exp, scale=scale), reads=[f'ps_s{b}'], writes=[f'pT{b}'])
                first, last = (j == 0), (j == len(kcs) - 1)
                S.op('pe', lambda e, b=b, kc=kc, first=first, last=last: e.matmul(po[:, :n], lhsT=Vs[:, kc, :], rhs=pT[:, b, :n], start=first, stop=last),
                     reads=[f'pT{b}', 'Vs'], writes=[f'ps_o{a}'])
                S.op('pe', lambda e, b=b, first=first, last=last: e.matmul(pd[:, :n], lhsT=ones16[:], rhs=pT[:, b, :n], start=first, stop=last),
                     reads=[f'pT{b}', 'ones16'], writes=[f'ps_d{a}'])
            S.op('dve', lambda e: e.reciprocal(out=rden[:, :n], in_=pd[:, :n]), reads=[f'ps_d{a}'], writes=['rden'])
            S.op('dve', lambda e: e.tensor_tensor(out=yo[:, a, :n], in0=po[:, :n], in1=rden[:, :n], op=ALU.mult), reads=[f'ps_o{a}', 'rden'], writes=[f'yo{a}'])
            S.op('sp', lambda e: e.dma_start(out=yb[h, :, t0:t0 + n], in_=yo[:, a, :n]), reads=[f'yo{a}'], dma=f'yo{a}')

        for h in range(4):
            for (t0, n) in token_tiles(SEQ, 512):
                attend(h, t0, n, list(range(NKC)))
            attend(h, SEQ, CTX, [32, 33])
        for a in range(2):
            S.ops['sp'].append(([(f'D_yo{a}', S.cnt[f'D_yo{a}'])], None, None, 0))
        S.emit(st)
    return nc


def rope_tables():
    n_freq = 32
    inv = (10000.0 ** (-np.arange(n_freq, dtype=np.float32) / n_freq)).astype(np.float32)
    t = np.arange(SEQ)
    row = (t // 64).astype(np.float32)
    colp = (t % 64).astype(np.float32)
    ang = np.concatenate([row[:, None] * inv, colp[:, None] * inv], axis=-1).astype(np.float32)
    cos = np.cos(ang).astype(np.float32).T
    sin = np.sin(ang).astype(np.float32).T
    C = np.ascontiguousarray(np.concatenate([cos, cos], axis=0))
    Sg = np.ascontiguousarray(np.concatenate([-sin, sin], axis=0))
    P = np.zeros((128, 128), np.float32)
    for p in range(128):
        P[p, (p + 64) % 128] = 1.0
    return C, Sg, P


NEG = -30000.0


def build_NA():
    nc = bass.Bass("TRN2", target_bir_lowering=False)
    qT = nc.dram_tensor("qT", [4, 64, TK], BF16, kind="ExternalInput").ap()
    kT = nc.dram_tensor("kT", [4, 64, TK], BF16, kind="ExternalInput").ap()
    V = nc.dram_tensor("V", [4, TK, 64], BF16, kind="ExternalInput").ap()
    bias = nc.dram_tensor("bias", [128, 4, 8, 256], F32, kind="ExternalInput").ap()
    yc = nc.dram_tensor("yc", [4, 64, TK], BF16, kind="ExternalOutput").ap()
    with ExitStack() as st:
        sb = lambda name, shape, dt: st.enter_context(nc.sbuf_tensor(name, shape, dt))
        qs = sb("qs", [64, 4, TK], BF16)
        ks = sb("ks", [64, 4, TK], BF16)
        Ve = sb("Ve", [128, 4, 34, 64], BF16)
        Vo = sb("Vo", [128, 4, 31, 64], BF16)
        bs = sb("bs", [128, 4, 8, 256], F32)
        ones16 = sb("ones16", [128, 64], BF16)
        pT = sb("pT", [128, 2, 384], BF16)
        rden = sb("rden", [64, 512], F32)
        yo = sb("yo", [64, 2, 512], BF16)
        ps_s = [st.enter_context(nc.psum_tensor(f"ps_s{i}", [128, 512], F32)) for i in range(2)]
        ps_o = [st.enter_context(nc.psum_tensor(f"ps_o{i}", [64, 512], F32)) for i in range(2)]
        ps_d = [st.enter_context(nc.psum_tensor(f"ps_d{i}", [64, 512], F32)) for i in range(2)]
        S = Sched(nc)
        S.op('dve', lambda e: e.memset(ones16[:], 1.0), writes=['ones16'])
        S.op('sp', lambda e: e.dma_start(out=bs[:], in_=bias), writes=['bs'], dma='bs')
        for h in range(4):
            S.op('sp', lambda e, h=h: e.dma_start(out=qs[:, h, :], in_=qT[h]), writes=['qs'], dma='qs')
            S.op('sp', lambda e, h=h: e.dma_start(out=ks[:, h, :], in_=kT[h]), writes=['ks'], dma='ks')
            S.op('sp', lambda e, h=h: e.dma_start(out=Ve[:, h, :, :], in_=V[h].rearrange("(c p) d -> p c d", p=128)), writes=['Ve'], dma='Ve')
            S.op('sp', lambda e, h=h: e.dma_start(out=Vo[:, h, :, :], in_=V[h, 64:64 + 31 * 128, :].rearrange("(c p) d -> p c d", p=128)), writes=['Vo'], dma='Vo')

        rcnt = [0]

        def row(h, q0, chunks, variant, a, slot, first_in_grp):
            b = rcnt[0] % 2
            rcnt[0] += 1
            ncx = len(chunks)

            def mms(e):
                for c, (k0, _) in enumerate(chunks):
                    ins = e.matmul(ps_s[b][:, c * 64:(c + 1) * 64], lhsT=ks[:, h, k0:k0 + 128], rhs=qs[:, h, q0:q0 + 64], start=True, stop=True)
                return ins
            S.op('pe', mms, reads=['qs', 'ks'], writes=[f'ps_s{b}'])
            if variant is not None:
                S.op('dve', lambda e: e.tensor_tensor(out=ps_s[b][:, 0:256], in0=ps_s[b][:, 0:256], in1=bs[:, h, variant, :], op=ALU.add), reads=[f'ps_s{b}', 'bs'], writes=[f'ps_s{b}'])
            S.op('act', lambda e: e.activation(out=pT[:, b, :ncx * 64], in_=ps_s[b][:, :ncx * 64], func=AF.Exp), reads=[f'ps_s{b}'], writes=[f'pT{b}'])

            def mmo(e):
                for c, (_, vap) in enumerate(chunks):
                    ins = e.matmul(ps_o[a][:, slot * 64:(slot + 1) * 64], lhsT=vap, rhs=pT[:, b, c * 64:(c + 1) * 64], start=(c == 0), stop=(c == ncx - 1))
                for c in range(ncx):
                    ins = e.matmul(ps_d[a][:, slot * 64:(slot + 1) * 64], lhsT=ones16[:], rhs=pT[:, b, c * 64:(c + 1) * 64], start=(c == 0), stop=(c == ncx - 1))
                return ins
            S.op('pe', mmo, reads=[f'pT{b}', 'Ve', 'Vo', 'ones16'], writes=[f'ps_o{a}', f'ps_d{a}'])

        gcnt = [0]

        def finish(h, q0, nq, a):
            S.op('dve', lambda e: e.reciprocal(out=rden[:, :nq], in_=ps_d[a][:, :nq]), reads=[f'ps_d{a}'], writes=['rden'])
            S.op('dve', lambda e: e.tensor_tensor(out=yo[:, a, :nq], in0=ps_o[a][:, :nq], in1=rden[:, :nq], op=ALU.mult), reads=[f'ps_o{a}', 'rden'], writes=[f'yo{a}'])
            S.op('sp', lambda e: e.dma_start(out=yc[h, :, q0:q0 + nq], in_=yo[:, a, :nq]), reads=[f'yo{a}'], dma=f'yo{a}')

        def vtile(h, tok0):
            blk = tok0 // 64
            if blk % 2 == 0:
                return Ve[:, h, blk // 2, :]
            return Vo[:, h, (blk - 1) // 2, :]

        for h in range(4):
            for g in range(8):
                a = gcnt[0] % 2
                gcnt[0] += 1
                for r in range(8):
                    i = g * 8 + r
                    rs = min(max(i - 4, 0), 56)
                    variant = i if i < 4 else (4 if i <= 60 else i - 56)
                    chunks = [((rs + 2 * c) * 64, vtile(h, (rs + 2 * c) * 64)) for c in range(4)]
                    chunks += [(SEQ + cc * 128, Ve[:, h, 32 + cc, :]) for cc in range(2)]
                    row(h, i * 64, chunks, variant, a, r, r == 0)
                finish(h, g * 512, 512, a)
            a = gcnt[0] % 2
            gcnt[0] += 1
            for r in range(4):
                chunks = [(SEQ + cc * 128, Ve[:, h, 32 + cc, :]) for cc in range(2)]
                row(h, SEQ + r * 64, chunks, None, a, r, r == 0)
            finish(h, SEQ, 256, a)
        for a in range(2):
            S.ops['sp'].append(([(f'D_yo{a}', S.cnt[f'D_yo{a}'])], None, None, 0))
        S.emit(st)
    return nc


def na_bias(rpb4):
    out = np.full((128, 4, 8, 4, 64), NEG, np.float32)
    p = np.arange(128)
    kr_off = p // 64
    kc = p % 64
    j = np.arange(64)
    cs = np.clip(j - 8, 0, 48)
    inwin = (kc[:, None] >= cs[None, :]) & (kc[:, None] < cs[None, :] + 16)
    coff = kc[:, None] - j[None, :] + 15
    coff_c = np.clip(coff, 0, 30)
    for v in range(8):
        d = -v
        for c in range(4):
            roff = d + 2 * c + kr_off + 7
            for h in range(4):
                vals = rpb4[h][roff[:, None], coff_c]
                out[:, h, v, c, :] = np.where(inwin, vals, NEG)
    return np.ascontiguousarray(out.reshape(128, 4, 8, 256))


NCHK = TK // 64
LWC = -0.6065306597126334


def build_RS_v1():
    TT = 256
    NCT = TT // 64
    nc = bass.Bass("TRN2", target_bir_lowering=False)
    z0 = nc.dram_tensor("z0", [12, 128, TK], F32, kind="ExternalInput").ap()
    zL = nc.dram_tensor("zL", [12, 128, TK], F32, kind="ExternalInput").ap()
    zR = nc.dram_tensor("zR", [12, 128, TK], F32, kind="ExternalInput").ap()
    l0 = nc.dram_tensor("l0", [2, 96, TK], F32, kind="ExternalInput").ap()
    lL = nc.dram_tensor("lL", [2, 96, TK], F32, kind="ExternalInput").ap()
    lR = nc.dram_tensor("lR", [2, 96, TK], F32, kind="ExternalInput").ap()
    mu = nc.dram_tensor("mu", [128, 12, 2], F32, kind="ExternalInput").ap()
    mul = nc.dram_tensor("mul", [96, 2, 2], F32, kind="ExternalInput").ap()
    pvec = nc.dram_tensor("pvec", [128, 4, 5], F32, kind="ExternalInput").ap()
    wup = nc.dram_tensor("wup", [96, 512], F32, kind="ExternalInput").ap()
    aup = nc.dram_tensor("aup", [96, 512], F32, kind="ExternalInput").ap()
    cst = nc.dram_tensor("cst", [128, 4, 128], F32, kind="ExternalInput").ap()
    o = nc.dram_tensor("o", [TK, 512], F32, kind="ExternalOutput").ap()
    bv = nc.dram_tensor("bv", [4, 128, TK], F32, kind="ExternalOutput").ap()
    with ExitStack() as st:
        sb = lambda name, shape, dt: st.enter_context(nc.sbuf_tensor(name, shape, dt))
        zin = sb("zin", [128, 2, 3, TT], F32)
        zs = sb("zs", [128, 12, TT], F32)
        ls = sb("ls", [96, 2, TT], F32)
        mus = sb("mus", [128, 12, 2], F32)
        muls = sb("muls", [96, 2, 2], F32)
        pv = sb("pv", [128, 4, 5], F32)
        wups = sb("wups", [96, 512], F32)
        aups = sb("aups", [96, 512], F32)
        cs = sb("cs", [128, 4, 128], F32)
        zer = sb("zer", [128, 64], F32)
        d1 = sb("d1", [128, 2, TT], F32)
        sg = sb("sg", [128, TT], F32)
        av = sb("av", [128, TT], F32)
        kk = sb("kk", [128, TT], F32)
        kd = sb("kd", [128, TT], F32)
        t1 = sb("t1", [128, TT], F32)
        t2 = sb("t2", [128, TT], F32)
        lcw = sb("lcw", [128, TT], F32)
        cw = sb("cw", [128, 2, TT], F32)
        icw = sb("icw", [128, TT], F32)
        cwp = sb("cwp", [128, TT], F32)
        AR = sb("AR", [128, 2, NCT, 128], F32)
        BK = sb("BK", [128, 2, NCT, 128], F32)
        VV = sb("VV", [128, 2, NCT, 128], F32)
        bvs = sb("bvs", [128, 2, TT], F32)
        Sst = sb("Sst", [128, 4, 64], F32)
        BKt = sb("BKt", [128, 128], F32)
        UV = sb("UV", [128, 2, 64], F32)
        A4s = sb("A4s", [128, 2, 128], F32)
        Mb = sb("Mb", [64, 2, 2, 64], F32)
        Nb = sb("Nb", [64, 2, 2, 64], F32)
        Pb = sb("Pb", [64, 2, 2, 64], F32)
        Rs = sb("Rs", [64, 2, 64], F32)
        St = sb("St", [128, 64], F32)
        osb = sb("osb", [64, 2, NCT, 512], F32)
        pp = [st.enter_context(nc.psum_tensor(f"pp{i}", [128, TT], F32)) for i in range(8)]
        S = Sched(nc)
        S.alias = {'par': [f'par{i}' for i in range(6)]}
        S.psum_excl = True
        S.op('sp', lambda e: e.dma_start(out=mus[:], in_=mu), writes=['par0'], dma='par0')
        S.op('sp', lambda e: e.dma_start(out=muls[:], in_=mul), writes=['par1'], dma='par1')
        S.op('sp', lambda e: e.dma_start(out=pv[:], in_=pvec), writes=['par2'], dma='par2')
        S.op('sp', lambda e: e.dma_start(out=wups[:], in_=wup), writes=['par3'], dma='par3')
        S.op('sp', lambda e: e.dma_start(out=aups[:], in_=aup), writes=['par4'], dma='par4')
        S.op('sp', lambda e: e.dma_start(out=cs[:], in_=cst), writes=['par5'], dma='par5')
        S.op('dve', lambda e: e.memset(zer[:], 0.0), writes=['zer'])
        S.op('dve', lambda e: e.memset(Sst[:], 0.0), reads=['par', 'zer'], writes=[f'S{p}' for p in range(4)])
        S.op('dve', lambda e: e.memset(VV[:], 0.0), writes=['VV0', 'VV1'])
        blk1 = cs[:, 0, :]
        mask4 = cs[:, 1, :]
        ident = cs[:, 2, :]
        zcnt = [0]

        def shift(dst, n_part, src0, srcL, srcR, muL, muR, dkey):
            i = zcnt[0] % 2
            zcnt[0] += 1
            P = n_part
            S.op('sp', lambda e: e.dma_start(out=zin[:P, i, 0, :], in_=src0), writes=[f'zin{i}_0'], dma=f'zin{i}_0')
            S.op('sp', lambda e: e.dma_start(out=zin[:P, i, 1, :], in_=srcL), writes=[f'zin{i}_1'], dma=f'zin{i}_1')
            S.op('sp', lambda e: e.dma_start(out=zin[:P, i, 2, :], in_=srcR), writes=[f'zin{i}_2'], dma=f'zin{i}_2')
            S.op('pool', lambda e: e.tensor_tensor(out=d1[:P, 0, :], in0=zin[:P, i, 1, :], in1=zin[:P, i, 0, :], op=ALU.subtract), reads=[f'zin{i}_0', f'zin{i}_1'], writes=['d1a'])
            S.op('pool', lambda e: e.tensor_tensor(out=d1[:P, 1, :], in0=zin[:P, i, 2, :], in1=zin[:P, i, 0, :], op=ALU.subtract), reads=[f'zin{i}_0', f'zin{i}_2'], writes=['d1b'])
            S.op('dve', lambda e: e.scalar_tensor_tensor(out=d1[:P, 0, :], in0=d1[:P, 0, :], scalar=muL, in1=zin[:P, i, 0, :], op0=ALU.mult, op1=ALU.add), reads=['d1a', f'zin{i}_0', 'par'], writes=['d1a'])
            S.op('dve', lambda e: e.scalar_tensor_tensor(out=dst, in0=d1[:P, 1, :], scalar=muR, in1=d1[:P, 0, :], op0=ALU.mult, op1=ALU.add), reads=['d1a', 'd1b', 'par'], writes=[dkey])

        def tile_prep(ti):
            c0 = ti * TT
            for c in range(12):
                shift(zs[:, c, :], 128, z0[c, :, c0:c0 + TT], zL[c, :, c0:c0 + TT], zR[c, :, c0:c0 + TT], mus[:, c, 0:1], mus[:, c, 1:2], f'zs{c}')
            for j in range(2):
                shift(ls[:, j, :], 96, l0[j, :, c0:c0 + TT], lL[j, :, c0:c0 + TT], lR[j, :, c0:c0 + TT], muls[:, j, 0:1], muls[:, j, 1:2], f'ls{j}')
            S.op('act', lambda e: e.activation(out=ls[:, 0, :], in_=ls[:, 0, :], func=AF.Tanh), reads=['ls0'], writes=['ls0'])

        def pair_prep(ti, pr):
            q = pr % 2
            r_ = zs[:, pr, :]
            k_ = zs[:, 4 + pr, :]
            v_ = zs[:, 8 + pr, :]
            rk = f'zs{pr}'; kkey = f'zs{4 + pr}'; vkey = f'zs{8 + pr}'
            S.op('pe', lambda e: e.matmul(pp[0][:, :TT], lhsT=wups[:, pr * 128:(pr + 1) * 128], rhs=ls[:, 0, :], start=True, stop=True), reads=['ls0', 'par'], writes=['pp0'])
            S.op('act', lambda e: e.activation(out=sg[:], in_=pp[0][:, :TT], func=AF.Sigmoid, bias=pv[:, pr, 0:1], scale=1.0), reads=['pp0', 'par'], writes=['sg'])
            S.op('pe', lambda e: e.matmul(pp[1][:, :TT], lhsT=aups[:, pr * 128:(pr + 1) * 128], rhs=ls[:, 1, :], start=True, stop=True), reads=['ls1', 'par'], writes=['pp1'])
            S.op('act', lambda e: e.activation(out=av[:], in_=pp[1][:, :TT], func=AF.Sigmoid, bias=pv[:, pr, 1:2], scale=1.0), reads=['pp1', 'par'], writes=['av'])
            S.op('dve', lambda e: e.tensor_scalar(out=kk[:], in0=k_, scalar1=pv[:, pr, 2:3], scalar2=None, op0=ALU.mult), reads=[kkey, 'par'], writes=['kk'])
            S.op('act', lambda e: e.activation(out=t1[:], in_=kk[:], func=AF.Square), reads=['kk'], writes=['t1'])
            S.op('pe', lambda e: e.matmul(pp[0][:, :TT], lhsT=blk1, rhs=t1[:], start=True, stop=True), reads=['t1', 'par'], writes=['pp0'])
            S.op('act', lambda e: e.activation(out=t2[:], in_=pp[0][:, :TT], func=AF.Sqrt), reads=['pp0'], writes=['t2'])
            S.op('dve', lambda e: e.tensor_scalar(out=t2[:], in0=t2[:], scalar1=1e-12, scalar2=None, op0=ALU.max), reads=['t2'], writes=['t2'])
            S.op('dve', lambda e: e.reciprocal(out=t2[:], in_=t2[:]), reads=['t2'], writes=['t2'])
            S.op('dve', lambda e: e.tensor_tensor(out=kk[:], in0=kk[:], in1=t2[:], op=ALU.mult), reads=['kk', 't2'], writes=['kk'])
            S.op('dve', lambda e: e.tensor_scalar(out=t1[:], in0=av[:], scalar1=-1.0, scalar2=pv[:, pr, 3:4], op0=ALU.add, op1=ALU.mult), reads=['av', 'par', 't1'], writes=['t1'])
            S.op('dve', lambda e: e.scalar_tensor_tensor(out=kd[:], in0=t1[:], scalar=1.0, in1=k_, op0=ALU.add, op1=ALU.mult), reads=['t1', kkey], writes=['kd'])
            S.op('dve', lambda e: e.tensor_scalar(out=sg[:], in0=sg[:], scalar1=LWC, scalar2=None, op0=ALU.mult), reads=['sg'], writes=['sg'])
            for ch in range(NCT):
                S.op('dve', lambda e, ch=ch: e.tensor_tensor_scan(out=lcw[:, ch * 64:(ch + 1) * 64], data0=sg[:, ch * 64:(ch + 1) * 64], data1=zer[:], initial=0.0, op0=ALU.add, op1=ALU.add),
                     reads=['sg', 'zer'], writes=['lcw'])
            S.op('act', lambda e: e.activation(out=cw[:, q, :], in_=lcw[:], func=AF.Exp), reads=['lcw'], writes=[f'cw{q}'])
            S.op('act', lambda e: e.activation(out=icw[:], in_=lcw[:], func=AF.Exp, scale=-1.0), reads=['lcw'], writes=['icw'])
            S.op('dve', lambda e: e.tensor_tensor(out=cwp[:], in0=lcw[:], in1=sg[:], op=ALU.subtract), reads=['lcw', 'sg'], writes=['cwp'])
            S.op('act', lambda e: e.activation(out=cwp[:], in_=cwp[:], func=AF.Exp), reads=['cwp'], writes=['cwp'])
            v3 = lambda ap: ap.rearrange("p (c t) -> p c t", t=64)
            S.op('dve', lambda e: e.scalar_tensor_tensor(out=AR[:, q, :, 0:64], in0=v3(kk[:]), scalar=-1.0, in1=v3(cwp[:]), op0=ALU.mult, op1=ALU.mult), reads=['kk', 'cwp'], writes=[f'AR{q}'])
            S.op('dve', lambda e: e.tensor_tensor(out=AR[:, q, :, 64:128], in0=v3(r_), in1=v3(cw[:, q, :]), op=ALU.mult), reads=[rk, f'cw{q}'], writes=[f'AR{q}'])
            S.op('dve', lambda e: e.tensor_tensor(out=t1[:], in0=kk[:], in1=av[:], op=ALU.mult), reads=['kk', 'av', 't1'], writes=['t1'])
            S.op('dve', lambda e: e.tensor_tensor(out=BK[:, q, :, 0:64], in0=v3(t1[:]), in1=v3(icw[:]), op=ALU.mult), reads=['t1', 'icw'], writes=[f'BK{q}'])
            S.op('dve', lambda e: e.tensor_tensor(out=BK[:, q, :, 64:128], in0=v3(kd[:]), in1=v3(icw[:]), op=ALU.mult), reads=['kd', 'icw'], writes=[f'BK{q}'])
            S.op('pool', lambda e: e.tensor_copy(out=VV[:, q, :, 64:128], in_=v3(v_)), reads=[vkey], writes=[f'VV{q}'])
            S.op('dve', lambda e: e.scalar_tensor_tensor(out=t2[:], in0=r_, scalar=pv[:, pr, 4:5], in1=kd[:], op0=ALU.mult, op1=ALU.mult), reads=[rk, 'kd', 'par', 't2'], writes=['t2'])
            S.op('pe', lambda e: e.matmul(pp[1][:, :TT], lhsT=blk1, rhs=t2[:], start=True, stop=True), reads=['t2', 'par'], writes=['pp1'])
            S.op('dve', lambda e: e.tensor_tensor(out=bvs[:, q, :], in0=pp[1][:, :TT], in1=v_, op=ALU.mult), reads=['pp1', vkey], writes=[f'bvs{q}'])
            S.op('sp', lambda e: e.dma_start(out=bv[pr, :, ti * TT:(ti + 1) * TT], in_=bvs[:, q, :]), reads=[f'bvs{q}'], dma=f'bvs{q}')

        ecnt = [0]

        def evac(dst, src, rd, wr):
            ecnt[0] += 1
            if ecnt[0] % 2:
                S.op('act', lambda e: e.activation(out=dst, in_=src, func=AF.Copy), reads=rd, writes=wr)
            else:
                S.op('dve', lambda e: e.tensor_copy(out=dst, in_=src), reads=rd, writes=wr)

        def chunk(ti, pr, ch):
            q = pr % 2
            os_ = ti % 2
            S.op('pe', lambda e: e.transpose(pp[2][:, 0:128], BK[:, q, ch, :], ident), reads=[f'BK{q}', 'par'], writes=['pp2a'])
            evac(BKt[:], pp[2][:, 0:128], ['pp2a'], ['BKt'])
            S.op('pe', lambda e: e.transpose(pp[2][:, 128:256], VV[:, q, ch, :], ident), reads=[f'VV{q}', 'par'], writes=['pp2b'])
            evac(UV[64:128, :, :], pp[2][64:128, 128:256].rearrange("p (h v) -> p h v", h=2), ['pp2b'], ['UVv'])
            def head(h):
                hp = h * 64
                sl = slice(hp, hp + 64)
                S.op('pe', lambda e: e.matmul(pp[3][:, 0:128], lhsT=BK[sl, q, ch, :], rhs=AR[sl, q, ch, :], start=True, stop=True), reads=[f'BK{q}', f'AR{q}'], writes=['pp3a'])
                S.op('dve', lambda e: e.tensor_tensor(out=A4s[:, h, :], in0=pp[3][:, 0:128], in1=mask4, op=ALU.mult), reads=['pp3a', 'par'], writes=[f'A4s{h}'])
                S.op('pe', lambda e: e.matmul(pp[3][0:64, 128:192], lhsT=AR[sl, q, ch, 0:64], rhs=BK[sl, q, ch, 0:64], start=True, stop=True), reads=[f'BK{q}', f'AR{q}'], writes=['pp3b'])
                S.op('dve', lambda e: e.tensor_tensor(out=Mb[:, h, 0, :], in0=pp[3][0:64, 128:192], in1=cs[0:64, 3, 0:64], op=ALU.mult), reads=['pp3b', 'par'], writes=[f'M{h}_0'])
                S.op('pool', lambda e: e.tensor_copy(out=Nb[:, h, 0, :], in_=A4s[0:64, h, 0:64]), reads=[f'A4s{h}'], writes=[f'N{h}_0'])
                S.op('pool', lambda e: e.tensor_tensor(out=Pb[:, h, 0, :], in0=A4s[0:64, h, 0:64], in1=cs[0:64, 2, 0:64], op=ALU.add), reads=[f'A4s{h}', 'par'], writes=[f'P{h}_0'])
                cur = 0
                for lv in range(1, 6):
                    nx = 1 - cur
                    S.op('pe', lambda e, cur=cur: e.matmul(pp[4][0:64, 0:64], lhsT=Nb[:, h, cur, :], rhs=Mb[:, h, cur, :], start=True, stop=True), reads=[f'N{h}_{cur}', f'M{h}_{cur}'], writes=['pp4a'])
                    evac(Mb[:, h, nx, :], pp[4][0:64, 0:64], ['pp4a'], [f'M{h}_{nx}'])
                    if lv < 5:
                        S.op('pe', lambda e, cur=cur: e.matmul(pp[4][0:64, 64:128], lhsT=Mb[:, h, cur, :], rhs=Nb[:, h, cur, :], start=True, stop=True), reads=[f'N{h}_{cur}', f'M{h}_{cur}'], writes=['pp4b'])
                        evac(Nb[:, h, nx, :], pp[4][0:64, 64:128], ['pp4b'], [f'N{h}_{nx}'])
                    S.op('pe', lambda e, cur=cur, nx=nx: e.matmul(pp[5][0:64, 0:64], lhsT=Mb[:, h, nx, :], rhs=Pb[:, h, cur, :], start=True, stop=True), reads=[f'M{h}_{nx}', f'P{h}_{cur}'], writes=['pp5'])
                    S.op('dve', lambda e, cur=cur, nx=nx: e.tensor_tensor(out=Pb[:, h, nx, :], in0=pp[5][0:64, 0:64], in1=Pb[:, h, cur, :], op=ALU.add), reads=['pp5', f'P{h}_{cur}'], writes=[f'P{h}_{nx}'])
                    cur = nx
                Pf = Pb[:, h, cur, :]
                pkey = f'P{h}_{cur}'
                skey = f'S{pr}'

                S.op('pe', lambda e: e.matmul(pp[6][0:64, 0:64], lhsT=AR[sl, q, ch, 0:64], rhs=Sst[sl, pr, :], start=True, stop=False), reads=[f'AR{q}', skey], writes=['pp6a'])
                S.op('pe', lambda e: e.matmul(pp[6][0:64, 0:64], lhsT=A4s[64:128, h, 0:64], rhs=UV[64:128, h, :], start=False, stop=True), reads=[f'A4s{h}', 'UVv'], writes=['pp6a'], pe_sync=True)
                evac(Rs[:, h, :], pp[6][0:64, 0:64], ['pp6a'], [f'Rs{h}'])
                S.op('pe', lambda e: e.matmul(pp[6][0:64, 64:128], lhsT=Pf, rhs=Rs[:, h, :], start=True, stop=True), reads=[pkey, f'Rs{h}'], writes=['pp6b'])
                evac(UV[0:64, h, :], pp[6][0:64, 64:128], ['pp6b'], [f'UVu{h}'])

                def mm_o(e):
                    e.matmul(pp[6][0:64, 128:192], lhsT=AR[sl, q, ch, 64:128], rhs=Sst[sl, pr, :], start=True, stop=False)
                    return e.matmul(pp[6][0:64, 128:192], lhsT=A4s[:, h, 64:128], rhs=UV[:, h, :], start=False, stop=True)
                S.op('pe', mm_o, reads=[f'AR{q}', skey, f'A4s{h}', 'UVv', f'UVu{h}'], writes=['pp6c'])
                evac(osb[:, os_, ch, (2 * pr + h) * 64:(2 * pr + h + 1) * 64], pp[6][0:64, 128:192], ['pp6c'], [f'osb{os_}'])
                S.op('pe', lambda e: e.matmul(pp[7][:, 0:64], lhsT=BKt[:], rhs=UV[:, h, :], start=True, stop=True), reads=['BKt', 'UVv', f'UVu{h}'], writes=['pp7'])
                S.op('dve', lambda e: e.tensor_tensor(out=St[sl, :], in0=pp[7][sl, 0:64], in1=Sst[sl, pr, :], op=ALU.add), reads=['pp7', skey], writes=['St'])
                S.op('dve', lambda e: e.tensor_scalar(out=Sst[sl, pr, :], in0=St[sl, :], scalar1=cw[sl, q, ch * 64 + 63:ch * 64 + 64], scalar2=None, op0=ALU.mult), reads=['St', f'cw{q}'], writes=[skey])

            for h in range(2):
                head(h)

        for ti in range(TK // TT):
            c0 = ti * TT
            os_ = ti % 2
            tile_prep(ti)
            for pr in range(4):
                pair_prep(ti, pr)
                for ch in range(NCT):
                    chunk(ti, pr, ch)
            S.op('sp', lambda e, c0=c0, os_=os_: e.dma_start(out=o[c0:c0 + TT, :].rearrange("(c t) f -> t c f", t=64), in_=osb[:, os_, :, :]), reads=[f'osb{os_}'], dma=f'osb{os_}')
        for s in list(S.cnt):
            if s.startswith('D_osb') or s.startswith('D_bvs'):
                S.ops['sp'].append(([(s, S.cnt[s])], None, None, 0))
        S.emit(st)
    return nc


def build_RS_v2():
    TT = 256
    NCT = TT // 64
    nc = bass.Bass("TRN2", target_bir_lowering=False)
    z0 = nc.dram_tensor("z0", [12, 128, TK], F32, kind="ExternalInput").ap()
    zL = nc.dram_tensor("zL", [12, 128, TK], F32, kind="ExternalInput").ap()
    zR = nc.dram_tensor("zR", [12, 128, TK], F32, kind="ExternalInput").ap()
    l0 = nc.dram_tensor("l0", [2, 96, TK], F32, kind="ExternalInput").ap()
    lL = nc.dram_tensor("lL", [2, 96, TK], F32, kind="ExternalInput").ap()
    lR = nc.dram_tensor("lR", [2, 96, TK], F32, kind="ExternalInput").ap()
    mu = nc.dram_tensor("mu", [128, 12, 2], F32, kind="ExternalInput").ap()
    mul = nc.dram_tensor("mul", [96, 2, 2], F32, kind="ExternalInput").ap()
    pvec = nc.dram_tensor("pvec", [128, 4, 5], F32, kind="ExternalInput").ap()
    wup = nc.dram_tensor("wup", [96, 512], F32, kind="ExternalInput").ap()
    aup = nc.dram_tensor("aup", [96, 512], F32, kind="ExternalInput").ap()
    cst = nc.dram_tensor("cst", [128, 4, 128], F32, kind="ExternalInput").ap()
    o = nc.dram_tensor("o", [TK, 512], F32, kind="ExternalOutput").ap()
    bv = nc.dram_tensor("bv", [4, 128, TK], F32, kind="ExternalOutput").ap()
    with ExitStack() as st:
        sb = lambda name, shape, dt: st.enter_context(nc.sbuf_tensor(name, shape, dt))
        zin = sb("zin", [128, 2, 3, TT], F32)
        zs = sb("zs", [128, 12, TT], F32)
        ls = sb("ls", [96, 2, TT], F32)
        mus = sb("mus", [128, 12, 2], F32)
        muls = sb("muls", [96, 2, 2], F32)
        pv = sb("pv", [128, 4, 5], F32)
        wups = sb("wups", [96, 512], F32)
        aups = sb("aups", [96, 512], F32)
        cs = sb("cs", [128, 4, 128], F32)
        zer = sb("zer", [128, 64], F32)
        d1 = sb("d1", [128, 2, TT], F32)
        sg = sb("sg", [128, TT], F32)
        av = sb("av", [128, TT], F32)
        kk = sb("kk", [128, TT], F32)
        kd = sb("kd", [128, TT], F32)
        t1 = sb("t1", [128, TT], F32)
        t2 = sb("t2", [128, TT], F32)
        lcw = sb("lcw", [128, TT], F32)
        cw = sb("cw", [128, 4, TT], F32)
        icw = sb("icw", [128, TT], F32)
        cwp = sb("cwp", [128, TT], F32)
        AR = sb("AR", [128, 4, NCT, 128], F32)
        BK = sb("BK", [128, 4, NCT, 128], F32)
        VV = sb("VV", [128, 4, NCT, 128], F32)
        bvs = sb("bvs", [128, 2, TT], F32)
        Sst = sb("Sst", [128, 4, 64], F32)
        BKt = sb("BKt", [128, 4, 128], F32)
        UV = sb("UV", [128, 8, 64], F32)
        A4s = sb("A4s", [128, 8, 128], F32)
        Mb = sb("Mb", [64, 8, 2, 64], F32)
        Nb = sb("Nb", [64, 8, 2, 64], F32)
        Pb = sb("Pb", [64, 8, 2, 64], F32)
        Rs = sb("Rs", [64, 8, 64], F32)
        St = sb("St", [128, 4, 64], F32)
        osb = sb("osb", [64, 2, NCT, 512], F32)
        pp = [st.enter_context(nc.psum_tensor(f"pp{i}", [128, TT], F32)) for i in range(8)]
        S = Sched(nc)
        S.alias = {'par': [f'par{i}' for i in range(6)]}
        S.psum_excl = True
        S.op('sp', lambda e: e.dma_start(out=mus[:], in_=mu), writes=['par0'], dma='par0')
        S.op('sp', lambda e: e.dma_start(out=muls[:], in_=mul), writes=['par1'], dma='par1')
        S.op('sp', lambda e: e.dma_start(out=pv[:], in_=pvec), writes=['par2'], dma='par2')
        S.op('sp', lambda e: e.dma_start(out=wups[:], in_=wup), writes=['par3'], dma='par3')
        S.op('sp', lambda e: e.dma_start(out=aups[:], in_=aup), writes=['par4'], dma='par4')
        S.op('sp', lambda e: e.dma_start(out=cs[:], in_=cst), writes=['par5'], dma='par5')
        S.op('dve', lambda e: e.memset(zer[:], 0.0), writes=['zer'])
        S.op('dve', lambda e: e.memset(Sst[:], 0.0), reads=['par', 'zer'], writes=[f'S{p}' for p in range(8)])
        S.op('dve', lambda e: e.memset(VV[:], 0.0), writes=['VV0', 'VV1', 'VV2', 'VV3'])
        blk1 = cs[:, 0, :]
        mask4 = cs[:, 1, :]
        ident = cs[:, 2, :]
        zcnt = [0]

        def shift(dst, n_part, src0, srcL, srcR, muL, muR, dkey):
            i = zcnt[0] % 2
            zcnt[0] += 1
            P = n_part
            S.op('sp', lambda e: e.dma_start(out=zin[:P, i, 0, :], in_=src0), writes=[f'zin{i}_0'], dma=f'zin{i}_0')
            S.op('sp', lambda e: e.dma_start(out=zin[:P, i, 1, :], in_=srcL), writes=[f'zin{i}_1'], dma=f'zin{i}_1')
            S.op('sp', lambda e: e.dma_start(out=zin[:P, i, 2, :], in_=srcR), writes=[f'zin{i}_2'], dma=f'zin{i}_2')
            S.op('pool', lambda e: e.tensor_tensor(out=d1[:P, 0, :], in0=zin[:P, i, 1, :], in1=zin[:P, i, 0, :], op=ALU.subtract), reads=[f'zin{i}_0', f'zin{i}_1'], writes=['d1a'])
            S.op('pool', lambda e: e.tensor_tensor(out=d1[:P, 1, :], in0=zin[:P, i, 2, :], in1=zin[:P, i, 0, :], op=ALU.subtract), reads=[f'zin{i}_0', f'zin{i}_2'], writes=['d1b'])
            S.op('dve', lambda e: e.scalar_tensor_tensor(out=d1[:P, 0, :], in0=d1[:P, 0, :], scalar=muL, in1=zin[:P, i, 0, :], op0=ALU.mult, op1=ALU.add), reads=['d1a', f'zin{i}_0', 'par'], writes=['d1a'])
            S.op('dve', lambda e: e.scalar_tensor_tensor(out=dst, in0=d1[:P, 1, :], scalar=muR, in1=d1[:P, 0, :], op0=ALU.mult, op1=ALU.add), reads=['d1a', 'd1b', 'par'], writes=[dkey])

        def tile_prep(ti):
            c0 = ti * TT
            for c in range(12):
                shift(zs[:, c, :], 128, z0[c, :, c0:c0 + TT], zL[c, :, c0:c0 + TT], zR[c, :, c0:c0 + TT], mus[:, c, 0:1], mus[:, c, 1:2], f'zs{c}')
            for j in range(2):
                shift(ls[:, j, :], 96, l0[j, :, c0:c0 + TT], lL[j, :, c0:c0 + TT], lR[j, :, c0:c0 + TT], muls[:, j, 0:1], muls[:, j, 1:2], f'ls{j}')
            S.op('act', lambda e: e.activation(out=ls[:, 0, :], in_=ls[:, 0, :], func=AF.Tanh), reads=['ls0'], writes=['ls0'])

        def pair_prep(ti, pr):
            q = pr
            qb = pr % 2
            r_ = zs[:, pr, :]
            k_ = zs[:, 4 + pr, :]
            v_ = zs[:, 8 + pr, :]
            rk = f'zs{pr}'; kkey = f'zs{4 + pr}'; vkey = f'zs{8 + pr}'
            S.op('pe', lambda e: e.matmul(pp[0][:, :TT], lhsT=wups[:, pr * 128:(pr + 1) * 128], rhs=ls[:, 0, :], start=True, stop=True), reads=['ls0', 'par'], writes=['pp0'])
            S.op('act', lambda e: e.activation(out=sg[:], in_=pp[0][:, :TT], func=AF.Sigmoid, bias=pv[:, pr, 0:1], scale=1.0), reads=['pp0', 'par'], writes=['sg'])
            S.op('pe', lambda e: e.matmul(pp[1][:, :TT], lhsT=aups[:, pr * 128:(pr + 1) * 128], rhs=ls[:, 1, :], start=True, stop=True), reads=['ls1', 'par'], writes=['pp1'])
            S.op('act', lambda e: e.activation(out=av[:], in_=pp[1][:, :TT], func=AF.Sigmoid, bias=pv[:, pr, 1:2], scale=1.0), reads=['pp1', 'par'], writes=['av'])
            S.op('dve', lambda e: e.tensor_scalar(out=kk[:], in0=k_, scalar1=pv[:, pr, 2:3], scalar2=None, op0=ALU.mult), reads=[kkey, 'par'], writes=['kk'])
            S.op('act', lambda e: e.activation(out=t1[:], in_=kk[:], func=AF.Square), reads=['kk'], writes=['t1'])
            S.op('pe', lambda e: e.matmul(pp[0][:, :TT], lhsT=blk1, rhs=t1[:], start=True, stop=True), reads=['t1', 'par'], writes=['pp0'])
            S.op('act', lambda e: e.activation(out=t2[:], in_=pp[0][:, :TT], func=AF.Sqrt), reads=['pp0'], writes=['t2'])
            S.op('dve', lambda e: e.tensor_scalar(out=t2[:], in0=t2[:], scalar1=1e-12, scalar2=None, op0=ALU.max), reads=['t2'], writes=['t2'])
            S.op('dve', lambda e: e.reciprocal(out=t2[:], in_=t2[:]), reads=['t2'], writes=['t2'])
            S.op('dve', lambda e: e.tensor_tensor(out=kk[:], in0=kk[:], in1=t2[:], op=ALU.mult), reads=['kk', 't2'], writes=['kk'])
            S.op('dve', lambda e: e.tensor_scalar(out=t1[:], in0=av[:], scalar1=-1.0, scalar2=pv[:, pr, 3:4], op0=ALU.add, op1=ALU.mult), reads=['av', 'par', 't1'], writes=['t1'])
            S.op('dve', lambda e: e.scalar_tensor_tensor(out=kd[:], in0=t1[:], scalar=1.0, in1=k_, op0=ALU.add, op1=ALU.mult), reads=['t1', kkey], writes=['kd'])
            S.op('dve', lambda e: e.tensor_scalar(out=sg[:], in0=sg[:], scalar1=LWC, scalar2=None, op0=ALU.mult), reads=['sg'], writes=['sg'])
            for ch in range(NCT):
                S.op('dve', lambda e, ch=ch: e.tensor_tensor_scan(out=lcw[:, ch * 64:(ch + 1) * 64], data0=sg[:, ch * 64:(ch + 1) * 64], data1=zer[:], initial=0.0, op0=ALU.add, op1=ALU.add),
                     reads=['sg', 'zer'], writes=['lcw'])
            S.op('act', lambda e: e.activation(out=cw[:, q, :], in_=lcw[:], func=AF.Exp), reads=['lcw'], writes=[f'cw{q}'])
            S.op('act', lambda e: e.activation(out=icw[:], in_=lcw[:], func=AF.Exp, scale=-1.0), reads=['lcw'], writes=['icw'])
            S.op('dve', lambda e: e.tensor_tensor(out=cwp[:], in0=lcw[:], in1=sg[:], op=ALU.subtract), reads=['lcw', 'sg'], writes=['cwp'])
            S.op('act', lambda e: e.activation(out=cwp[:], in_=cwp[:], func=AF.Exp), reads=['cwp'], writes=['cwp'])
            v3 = lambda ap: ap.rearrange("p (c t) -> p c t", t=64)
            S.op('dve', lambda e: e.scalar_tensor_tensor(out=AR[:, q, :, 0:64], in0=v3(kk[:]), scalar=-1.0, in1=v3(cwp[:]), op0=ALU.mult, op1=ALU.mult), reads=['kk', 'cwp'], writes=[f'AR{q}'])
            S.op('dve', lambda e: e.tensor_tensor(out=AR[:, q, :, 64:128], in0=v3(r_), in1=v3(cw[:, q, :]), op=ALU.mult), reads=[rk, f'cw{q}'], writes=[f'AR{q}'])
            S.op('dve', lambda e: e.tensor_tensor(out=t1[:], in0=kk[:], in1=av[:], op=ALU.mult), reads=['kk', 'av', 't1'], writes=['t1'])
            S.op('dve', lambda e: e.tensor_tensor(out=BK[:, q, :, 0:64], in0=v3(t1[:]), in1=v3(icw[:]), op=ALU.mult), reads=['t1', 'icw'], writes=[f'BK{q}'])
            S.op('dve', lambda e: e.tensor_tensor(out=BK[:, q, :, 64:128], in0=v3(kd[:]), in1=v3(icw[:]), op=ALU.mult), reads=['kd', 'icw'], writes=[f'BK{q}'])
            S.op('pool', lambda e: e.tensor_copy(out=VV[:, q, :, 64:128], in_=v3(v_)), reads=[vkey], writes=[f'VV{q}'])
            S.op('dve', lambda e: e.scalar_tensor_tensor(out=t2[:], in0=r_, scalar=pv[:, pr, 4:5], in1=kd[:], op0=ALU.mult, op1=ALU.mult), reads=[rk, 'kd', 'par', 't2'], writes=['t2'])
            S.op('pe', lambda e: e.matmul(pp[1][:, :TT], lhsT=blk1, rhs=t2[:], start=True, stop=True), reads=['t2', 'par'], writes=['pp1'])
            S.op('dve', lambda e: e.tensor_tensor(out=bvs[:, qb, :], in0=pp[1][:, :TT], in1=v_, op=ALU.mult), reads=['pp1', vkey], writes=[f'bvs{qb}'])
            S.op('sp', lambda e: e.dma_start(out=bv[pr, :, ti * TT:(ti + 1) * TT], in_=bvs[:, qb, :]), reads=[f'bvs{qb}'], dma=f'bvs{qb}')

        ecnt = [0]

        def evac(dst, src, rd, wr):
            ecnt[0] += 1
            if ecnt[0] % 2:
                S.op('act', lambda e: e.activation(out=dst, in_=src, func=AF.Copy), reads=rd, writes=wr)
            else:
                S.op('dve', lambda e: e.tensor_copy(out=dst, in_=src), reads=rd, writes=wr)

        rot = [0]

        def bank():
            k = 2 + rot[0] % 6
            rot[0] += 1
            return pp[k], f'pp{k}'

        def chunk_all(ti, ch):
            os_ = ti % 2

            def stT(pr):
                p, pk_ = bank()
                S.op('pe', lambda e: e.transpose(p[:, 0:128], BK[:, pr, ch, :], ident), reads=[f'BK{pr}', 'par'], writes=[pk_])
                evac(BKt[:, pr, :], p[:, 0:128], [pk_], [f'BKt{pr}'])
                p2, pk2 = bank()
                S.op('pe', lambda e: e.transpose(p2[:, 0:128], VV[:, pr, ch, :], ident), reads=[f'VV{pr}', 'par'], writes=[pk2])
                evac(UV[64:128, 2 * pr:2 * pr + 2, :], p2[64:128, 0:128].rearrange("p (h v) -> p h v", h=2), [pk2], [f'UVv{pr}'])

            def stA(hh):
                pr, h = hh // 2, hh % 2
                sl = slice(h * 64, h * 64 + 64)
                p, pk_ = bank()
                S.op('pe', lambda e: e.matmul(p[:, 0:128], lhsT=BK[sl, pr, ch, :], rhs=AR[sl, pr, ch, :], start=True, stop=True), reads=[f'BK{pr}', f'AR{pr}'], writes=[pk_])
                S.op('dve', lambda e: e.tensor_tensor(out=A4s[:, hh, :], in0=p[:, 0:128], in1=mask4, op=ALU.mult), reads=[pk_, 'par'], writes=[f'A4s{hh}'])
                p2, pk2 = bank()
                S.op('pe', lambda e: e.matmul(p2[0:64, 0:64], lhsT=AR[sl, pr, ch, 0:64], rhs=BK[sl, pr, ch, 0:64], start=True, stop=True), reads=[f'BK{pr}', f'AR{pr}'], writes=[pk2])
                S.op('dve', lambda e: e.tensor_tensor(out=Mb[:, hh, 0, :], in0=p2[0:64, 0:64], in1=cs[0:64, 3, 0:64], op=ALU.mult), reads=[pk2, 'par'], writes=[f'M{hh}_0'])
                S.op('pool', lambda e: e.tensor_copy(out=Nb[:, hh, 0, :], in_=A4s[0:64, hh, 0:64]), reads=[f'A4s{hh}'], writes=[f'N{hh}_0'])
                S.op('pool', lambda e: e.tensor_tensor(out=Pb[:, hh, 0, :], in0=A4s[0:64, hh, 0:64], in1=cs[0:64, 2, 0:64], op=ALU.add), reads=[f'A4s{hh}', 'par'], writes=[f'P{hh}_0'])

            def stL(hh, lv, cur, nx):
                p, pk_ = bank()
                S.op('pe', lambda e: e.matmul(p[0:64, 0:64], lhsT=Nb[:, hh, cur, :], rhs=Mb[:, hh, cur, :], start=True, stop=True), reads=[f'N{hh}_{cur}', f'M{hh}_{cur}'], writes=[pk_])
                evac(Mb[:, hh, nx, :], p[0:64, 0:64], [pk_], [f'M{hh}_{nx}'])
                if lv < 5:
                    p2, pk2 = bank()
                    S.op('pe', lambda e: e.matmul(p2[0:64, 0:64], lhsT=Mb[:, hh, cur, :], rhs=Nb[:, hh, cur, :], start=True, stop=True), reads=[f'N{hh}_{cur}', f'M{hh}_{cur}'], writes=[pk2])
                    evac(Nb[:, hh, nx, :], p2[0:64, 0:64], [pk2], [f'N{hh}_{nx}'])
                p3, pk3 = bank()
                S.op('pe', lambda e: e.matmul(p3[0:64, 0:64], lhsT=Mb[:, hh, nx, :], rhs=Pb[:, hh, cur, :], start=True, stop=True), reads=[f'M{hh}_{nx}', f'P{hh}_{cur}'], writes=[pk3])
                S.op('dve', lambda e: e.tensor_tensor(out=Pb[:, hh, nx, :], in0=p3[0:64, 0:64], in1=Pb[:, hh, cur, :], op=ALU.add), reads=[pk3, f'P{hh}_{cur}'], writes=[f'P{hh}_{nx}'])

            def stR1(hh, fin):
                pr, h = hh // 2, hh % 2
                sl = slice(h * 64, h * 64 + 64)
                p, pk_ = bank()
                S.op('pe', lambda e: e.matmul(p[0:64, 0:64], lhsT=AR[sl, pr, ch, 0:64], rhs=Sst[sl, pr, :], start=True, stop=False), reads=[f'AR{pr}', f'S{hh}'], writes=[pk_])
                S.op('pe', lambda e: e.matmul(p[0:64, 0:64], lhsT=A4s[64:128, hh, 0:64], rhs=UV[64:128, hh, :], start=False, stop=True), reads=[f'A4s{hh}', f'UVv{pr}'], writes=[pk_], pe_sync=(h == 0))
                evac(Rs[:, hh, :], p[0:64, 0:64], [pk_], [f'Rs{hh}'])

            def stR2(hh, fin):
                p, pk_ = bank()
                S.op('pe', lambda e: e.matmul(p[0:64, 0:64], lhsT=Pb[:, hh, fin, :], rhs=Rs[:, hh, :], start=True, stop=True), reads=[f'P{hh}_{fin}', f'Rs{hh}'], writes=[pk_])
                evac(UV[0:64, hh, :], p[0:64, 0:64], [pk_], [f'UVu{hh}'])

            def stR3(hh, fin):
                pr, h = hh // 2, hh % 2
                sl = slice(h * 64, h * 64 + 64)
                p, pk_ = bank()

                def mm_o(e):
                    e.matmul(p[0:64, 0:64], lhsT=AR[sl, pr, ch, 64:128], rhs=Sst[sl, pr, :], start=True, stop=False)
                    return e.matmul(p[0:64, 0:64], lhsT=A4s[:, hh, 64:128], rhs=UV[:, hh, :], start=False, stop=True)
                S.op('pe', mm_o, reads=[f'AR{pr}', f'S{hh}', f'A4s{hh}', f'UVv{pr}', f'UVu{hh}'], writes=[pk_])
                evac(osb[:, os_, ch, hh * 64:(hh + 1) * 64], p[0:64, 0:64], [pk_], [f'osb{os_}_{hh}'])

            def stR4(hh, fin):
                pr, h = hh // 2, hh % 2
                sl = slice(h * 64, h * 64 + 64)
                p, pk_ = bank()
                S.op('pe', lambda e: e.matmul(p[:, 0:64], lhsT=BKt[:, pr, :], rhs=UV[:, hh, :], start=True, stop=True), reads=[f'BKt{pr}', f'UVv{pr}', f'UVu{hh}'], writes=[pk_])
                S.op('dve', lambda e: e.tensor_tensor(out=St[sl, pr, :], in0=p[sl, 0:64], in1=Sst[sl, pr, :], op=ALU.add), reads=[pk_, f'S{hh}'], writes=[f'St{hh}'])
                S.op('dve', lambda e: e.tensor_scalar(out=Sst[sl, pr, :], in0=St[sl, pr, :], scalar1=cw[sl, pr, ch * 64 + 63:ch * 64 + 64], scalar2=None, op0=ALU.mult), reads=[f'St{hh}', f'cw{pr}'], writes=[f'S{hh}'])

            for pr in range(4):
                stT(pr)
            for hh in range(8):
                stA(hh)
            cur = 0
            for lv in range(1, 6):
                nx = 1 - cur
                for hh in range(8):
                    stL(hh, lv, cur, nx)
                cur = nx
            for stg in (stR1, stR2, stR3, stR4):
                for hh in range(8):
                    stg(hh, cur)

        for ti in range(TK // TT):
            c0 = ti * TT
            os_ = ti % 2
            tile_prep(ti)
            for pr in range(4):
                pair_prep(ti, pr)
            for ch in range(NCT):
                chunk_all(ti, ch)
            S.op('sp', lambda e, c0=c0, os_=os_: e.dma_start(out=o[c0:c0 + TT, :].rearrange("(c t) f -> t c f", t=64), in_=osb[:, os_, :, :]), reads=[f'osb{os_}_{hh}' for hh in range(8)], dma=f'osb{os_}')
        for s in list(S.cnt):
            if s.startswith('D_osb') or s.startswith('D_bvs'):
                S.ops['sp'].append(([(s, S.cnt[s])], None, None, 0))
        S.emit(st)
    return nc


FP32R = False


def build_RS():
    TT = 256
    NCT = TT // 64
    nc = bass.Bass("TRN2", target_bir_lowering=False)
    z0 = nc.dram_tensor("z0", [12, 128, TK], F32, kind="ExternalInput").ap()
    zL = nc.dram_tensor("zL", [12, 128, TK], F32, kind="ExternalInput").ap()
    zR = nc.dram_tensor("zR", [12, 128, TK], F32, kind="ExternalInput").ap()
    l0 = nc.dram_tensor("l0", [2, 96, TK], F32, kind="ExternalInput").ap()
    lL = nc.dram_tensor("lL", [2, 96, TK], F32, kind="ExternalInput").ap()
    lR = nc.dram_tensor("lR", [2, 96, TK], F32, kind="ExternalInput").ap()
    mu = nc.dram_tensor("mu", [128, 12, 2], F32, kind="ExternalInput").ap()
    mul = nc.dram_tensor("mul", [96, 2, 2], F32, kind="ExternalInput").ap()
    pvec = nc.dram_tensor("pvec", [128, 4, 5], F32, kind="ExternalInput").ap()
    wup = nc.dram_tensor("wup", [96, 512], F32, kind="ExternalInput").ap()
    aup = nc.dram_tensor("aup", [96, 512], F32, kind="ExternalInput").ap()
    cst = nc.dram_tensor("cst", [128, 4, 128], F32, kind="ExternalInput").ap()
    o = nc.dram_tensor("o", [TK, 512], F32, kind="ExternalOutput").ap()
    bv = nc.dram_tensor("bv", [4, 128, TK], F32, kind="ExternalOutput").ap()
    with ExitStack() as st:
        sb = lambda name, shape, dt: st.enter_context(nc.sbuf_tensor(name, shape, dt))
        zin = sb("zin", [128, 2, 3, TT], F32)
        zs = sb("zs", [128, 12, TT], F32)
        ls = sb("ls", [96, 2, TT], F32)
        mus = sb("mus", [128, 12, 2], F32)
        muls = sb("muls", [96, 2, 2], F32)
        pv = sb("pv", [128, 4, 5], F32)
        wups = sb("wups", [96, 512], F32)
        aups = sb("aups", [96, 512], F32)
        cs = sb("cs", [128, 4, 128], F32)
        zer = sb("zer", [128, 64], F32)
        d1 = sb("d1", [128, 2, TT], F32)
        sg = sb("sg", [128, TT], F32)
        av = sb("av", [128, TT], F32)
        kk = sb("kk", [128, TT], F32)
        kd = sb("kd", [128, TT], F32)
        t1 = sb("t1", [128, TT], F32)
        t2 = sb("t2", [128, TT], F32)
        lcw = sb("lcw", [128, TT], F32)
        cw = sb("cw", [128, 4, TT], F32)
        icw = sb("icw", [128, TT], F32)
        cwp = sb("cwp", [128, TT], F32)
        AR = sb("AR", [128, 4, NCT, 128], F32)
        BK = sb("BK", [128, 4, NCT, 128], F32)
        VV = sb("VV", [128, 4, NCT, 128], F32)
        bvs = sb("bvs", [128, 2, TT], F32)
        Sst = sb("Sst", [128, 4, 64], F32)
        BKt = sb("BKt", [128, NCT * 4, 128], F32)
        UV = sb("UV", [128, NCT * 8, 64], F32)
        A4s = sb("A4s", [128, NCT * 8, 128], F32)
        Mb = sb("Mb", [64, NCT * 8, 2, 64], mybir.dt.float32r if FP32R else F32)
        Nb = sb("Nb", [64, NCT * 8, 2, 64], mybir.dt.float32r if FP32R else F32)
        Pb = sb("Pb", [64, NCT * 8, 2, 64], mybir.dt.float32r if FP32R else F32)
        Rs = sb("Rs", [64, 8, 64], mybir.dt.float32r if FP32R else F32)
        St = sb("St", [128, 4, 64], F32)
        osb = sb("osb", [64, 2, NCT, 512], F32)
        pp = [st.enter_context(nc.psum_tensor(f"pp{i}", [128, TT], F32)) for i in range(8)]
        S = Sched(nc)
        S.alias = {'par': [f'par{i}' for i in range(6)]}
        S.psum_excl = True
        S.op('sp', lambda e: e.dma_start(out=mus[:], in_=mu), writes=['par0'], dma='par0')
        S.op('sp', lambda e: e.dma_start(out=muls[:], in_=mul), writes=['par1'], dma='par1')
        S.op('sp', lambda e: e.dma_start(out=pv[:], in_=pvec), writes=['par2'], dma='par2')
        S.op('sp', lambda e: e.dma_start(out=wups[:], in_=wup), writes=['par3'], dma='par3')
        S.op('sp', lambda e: e.dma_start(out=aups[:], in_=aup), writes=['par4'], dma='par4')
        S.op('sp', lambda e: e.dma_start(out=cs[:], in_=cst), writes=['par5'], dma='par5')
        S.op('dve', lambda e: e.memset(zer[:], 0.0), writes=['zer'])
        S.op('dve', lambda e: e.memset(Sst[:], 0.0), reads=['par', 'zer'], writes=[f'S{p}' for p in range(8)])
        S.op('dve', lambda e: e.memset(VV[:], 0.0), writes=['VV0', 'VV1', 'VV2', 'VV3'])
        blk1 = cs[:, 0, :]
        mask4 = cs[:, 1, :]
        ident = cs[:, 2, :]
        zcnt = [0]

        def shift(dst, n_part, src0, srcL, srcR, muL, muR, dkey):
            i = zcnt[0] % 2
            zcnt[0] += 1
            P = n_part
            S.op('sp', lambda e: e.dma_start(out=zin[:P, i, 0, :], in_=src0), writes=[f'zin{i}_0'], dma=f'zin{i}_0')
            S.op('sp', lambda e: e.dma_start(out=zin[:P, i, 1, :], in_=srcL), writes=[f'zin{i}_1'], dma=f'zin{i}_1')
            S.op('sp', lambda e: e.dma_start(out=zin[:P, i, 2, :], in_=srcR), writes=[f'zin{i}_2'], dma=f'zin{i}_2')
            S.op('pool', lambda e: e.tensor_tensor(out=d1[:P, 0, :], in0=zin[:P, i, 1, :], in1=zin[:P, i, 0, :], op=ALU.subtract), reads=[f'zin{i}_0', f'zin{i}_1'], writes=['d1a'])
            S.op('pool', lambda e: e.tensor_tensor(out=d1[:P, 1, :], in0=zin[:P, i, 2, :], in1=zin[:P, i, 0, :], op=ALU.subtract), reads=[f'zin{i}_0', f'zin{i}_2'], writes=['d1b'])
            S.op('dve', lambda e: e.scalar_tensor_tensor(out=d1[:P, 0, :], in0=d1[:P, 0, :], scalar=muL, in1=zin[:P, i, 0, :], op0=ALU.mult, op1=ALU.add), reads=['d1a', f'zin{i}_0', 'par'], writes=['d1a'])
            S.op('dve', lambda e: e.scalar_tensor_tensor(out=dst, in0=d1[:P, 1, :], scalar=muR, in1=d1[:P, 0, :], op0=ALU.mult, op1=ALU.add), reads=['d1a', 'd1b', 'par'], writes=[dkey])

        def tile_prep(ti):
            c0 = ti * TT
            for c in range(12):
                shift(zs[:, c, :], 128, z0[c, :, c0:c0 + TT], zL[c, :, c0:c0 + TT], zR[c, :, c0:c0 + TT], mus[:, c, 0:1], mus[:, c, 1:2], f'zs{c}')
            for j in range(2):
                shift(ls[:, j, :], 96, l0[j, :, c0:c0 + TT], lL[j, :, c0:c0 + TT], lR[j, :, c0:c0 + TT], muls[:, j, 0:1], muls[:, j, 1:2], f'ls{j}')
            S.op('act', lambda e: e.activation(out=ls[:, 0, :], in_=ls[:, 0, :], func=AF.Tanh), reads=['ls0'], writes=['ls0'])

        def pair_prep(ti, pr):
            q = pr
            qb = pr % 2
            r_ = zs[:, pr, :]
            k_ = zs[:, 4 + pr, :]
            v_ = zs[:, 8 + pr, :]
            rk = f'zs{pr}'; kkey = f'zs{4 + pr}'; vkey = f'zs{8 + pr}'
            S.op('pe', lambda e: e.matmul(pp[0][:, :TT], lhsT=wups[:, pr * 128:(pr + 1) * 128], rhs=ls[:, 0, :], start=True, stop=True), reads=['ls0', 'par'], writes=['pp0'])
            S.op('act', lambda e: e.activation(out=sg[:], in_=pp[0][:, :TT], func=AF.Sigmoid, bias=pv[:, pr, 0:1], scale=1.0), reads=['pp0', 'par'], writes=['sg'])
            S.op('pe', lambda e: e.matmul(pp[1][:, :TT], lhsT=aups[:, pr * 128:(pr + 1) * 128], rhs=ls[:, 1, :], start=True, stop=True), reads=['ls1', 'par'], writes=['pp1'])
            S.op('act', lambda e: e.activation(out=av[:], in_=pp[1][:, :TT], func=AF.Sigmoid, bias=pv[:, pr, 1:2], scale=1.0), reads=['pp1', 'par'], writes=['av'])
            S.op('dve', lambda e: e.tensor_scalar(out=kk[:], in0=k_, scalar1=pv[:, pr, 2:3], scalar2=None, op0=ALU.mult), reads=[kkey, 'par'], writes=['kk'])
            S.op('act', lambda e: e.activation(out=t1[:], in_=kk[:], func=AF.Square), reads=['kk'], writes=['t1'])
            S.op('pe', lambda e: e.matmul(pp[0][:, :TT], lhsT=blk1, rhs=t1[:], start=True, stop=True), reads=['t1', 'par'], writes=['pp0'])
            S.op('act', lambda e: e.activation(out=t2[:], in_=pp[0][:, :TT], func=AF.Sqrt), reads=['pp0'], writes=['t2'])
            S.op('dve', lambda e: e.tensor_scalar(out=t2[:], in0=t2[:], scalar1=1e-12, scalar2=None, op0=ALU.max), reads=['t2'], writes=['t2'])
            S.op('dve', lambda e: e.reciprocal(out=t2[:], in_=t2[:]), reads=['t2'], writes=['t2'])
            S.op('dve', lambda e: e.tensor_tensor(out=kk[:], in0=kk[:], in1=t2[:], op=ALU.mult), reads=['kk', 't2'], writes=['kk'])
            S.op('dve', lambda e: e.tensor_scalar(out=t1[:], in0=av[:], scalar1=-1.0, scalar2=pv[:, pr, 3:4], op0=ALU.add, op1=ALU.mult), reads=['av', 'par', 't1'], writes=['t1'])
            S.op('dve', lambda e: e.scalar_tensor_tensor(out=kd[:], in0=t1[:], scalar=1.0, in1=k_, op0=ALU.add, op1=ALU.mult), reads=['t1', kkey], writes=['kd'])
            S.op('dve', lambda e: e.tensor_scalar(out=sg[:], in0=sg[:], scalar1=LWC, scalar2=None, op0=ALU.mult), reads=['sg'], writes=['sg'])
            for ch in range(NCT):
                S.op('dve', lambda e, ch=ch: e.tensor_tensor_scan(out=lcw[:, ch * 64:(ch + 1) * 64], data0=sg[:, ch * 64:(ch + 1) * 64], data1=zer[:], initial=0.0, op0=ALU.add, op1=ALU.add),
                     reads=['sg', 'zer'], writes=['lcw'])
            S.op('act', lambda e: e.activation(out=cw[:, q, :], in_=lcw[:], func=AF.Exp), reads=['lcw'], writes=[f'cw{q}'])
            S.op('act', lambda e: e.activation(out=icw[:], in_=lcw[:], func=AF.Exp, scale=-1.0), reads=['lcw'], writes=['icw'])
            S.op('dve', lambda e: e.tensor_tensor(out=cwp[:], in0=lcw[:], in1=sg[:], op=ALU.subtract), reads=['lcw', 'sg'], writes=['cwp'])
            S.op('act', lambda e: e.activation(out=cwp[:], in_=cwp[:], func=AF.Exp), reads=['cwp'], writes=['cwp'])
            v3 = lambda ap: ap.rearrange("p (c t) -> p c t", t=64)
            S.op('dve', lambda e: e.scalar_tensor_tensor(out=AR[:, q, :, 0:64], in0=v3(kk[:]), scalar=-1.0, in1=v3(cwp[:]), op0=ALU.mult, op1=ALU.mult), reads=['kk', 'cwp'], writes=[f'AR{q}'])
            S.op('dve', lambda e: e.tensor_tensor(out=AR[:, q, :, 64:128], in0=v3(r_), in1=v3(cw[:, q, :]), op=ALU.mult), reads=[rk, f'cw{q}'], writes=[f'AR{q}'])
            S.op('dve', lambda e: e.tensor_tensor(out=t1[:], in0=kk[:], in1=av[:], op=ALU.mult), reads=['kk', 'av', 't1'], writes=['t1'])
            S.op('dve', lambda e: e.tensor_tensor(out=BK[:, q, :, 0:64], in0=v3(t1[:]), in1=v3(icw[:]), op=ALU.mult), reads=['t1', 'icw'], writes=[f'BK{q}'])
            S.op('dve', lambda e: e.tensor_tensor(out=BK[:, q, :, 64:128], in0=v3(kd[:]), in1=v3(icw[:]), op=ALU.mult), reads=['kd', 'icw'], writes=[f'BK{q}'])
            S.op('pool', lambda e: e.tensor_copy(out=VV[:, q, :, 64:128], in_=v3(v_)), reads=[vkey], writes=[f'VV{q}'])
            S.op('dve', lambda e: e.scalar_tensor_tensor(out=t2[:], in0=r_, scalar=pv[:, pr, 4:5], in1=kd[:], op0=ALU.mult, op1=ALU.mult), reads=[rk, 'kd', 'par', 't2'], writes=['t2'])
            S.op('pe', lambda e: e.matmul(pp[1][:, :TT], lhsT=blk1, rhs=t2[:], start=True, stop=True), reads=['t2', 'par'], writes=['pp1'])
            S.op('dve', lambda e: e.tensor_tensor(out=bvs[:, qb, :], in0=pp[1][:, :TT], in1=v_, op=ALU.mult), reads=['pp1', vkey], writes=[f'bvs{qb}'])
            S.op('sp', lambda e: e.dma_start(out=bv[pr, :, ti * TT:(ti + 1) * TT], in_=bvs[:, qb, :]), reads=[f'bvs{qb}'], dma=f'bvs{qb}')

        ecnt = [0]

        def evac(dst, src, rd, wr):
            ecnt[0] += 1
            if ecnt[0] % 2:
                S.op('act', lambda e: e.activation(out=dst, in_=src, func=AF.Copy), reads=rd, writes=wr)
            else:
                S.op('dve', lambda e: e.tensor_copy(out=dst, in_=src), reads=rd, writes=wr)

        R32 = (lambda ap: ap)
        rot = [0]

        def bank():
            k = 2 + rot[0] % 6
            rot[0] += 1
            return pp[k], f'pp{k}'

        def stages(ti, ch):
            os_ = ti % 2
            co = ch * 8
            cp = ch * 4

            def stT(pr):
                p, pk_ = bank()
                S.op('pe', lambda e: e.transpose(p[:, 0:128], BK[:, pr, ch, :], ident), reads=[f'BK{pr}', 'par'], writes=[pk_])
                evac(BKt[:, cp + pr, :], p[:, 0:128], [pk_], [f'BKt{ch}_{pr}'])
                p2, pk2 = bank()
                S.op('pe', lambda e: e.transpose(p2[:, 0:128], VV[:, pr, ch, :], ident), reads=[f'VV{pr}', 'par'], writes=[pk2])
                evac(UV[64:128, co + 2 * pr:co + 2 * pr + 2, :], p2[64:128, 0:128].rearrange("p (h v) -> p h v", h=2), [pk2], [f'UVv{ch}_{pr}'])

            def stA(hh):
                pr, h = hh // 2, hh % 2
                sl = slice(h * 64, h * 64 + 64)
                p, pk_ = bank()
                S.op('pe', lambda e: e.matmul(p[:, 0:128], lhsT=BK[sl, pr, ch, :], rhs=AR[sl, pr, ch, :], start=True, stop=True), reads=[f'BK{pr}', f'AR{pr}'], writes=[pk_])
                S.op('dve', lambda e: e.tensor_tensor(out=A4s[:, co + hh, :], in0=p[:, 0:128], in1=mask4, op=ALU.mult), reads=[pk_, 'par'], writes=[f'A4s{ch}_{hh}'])
                p2, pk2 = bank()
                S.op('pe', lambda e: e.matmul(p2[0:64, 0:64], lhsT=AR[sl, pr, ch, 0:64], rhs=BK[sl, pr, ch, 0:64], start=True, stop=True), reads=[f'BK{pr}', f'AR{pr}'], writes=[pk2])
                S.op('dve', lambda e: e.tensor_tensor(out=Mb[:, co + hh, 0, :], in0=p2[0:64, 0:64], in1=cs[0:64, 3, 0:64], op=ALU.mult), reads=[pk2, 'par'], writes=[f'M{ch}_{hh}_0'])
                S.op('pool', lambda e: e.tensor_copy(out=Nb[:, co + hh, 0, :], in_=A4s[0:64, co + hh, 0:64]), reads=[f'A4s{ch}_{hh}'], writes=[f'N{ch}_{hh}_0'])
                S.op('pool', lambda e: e.tensor_tensor(out=Pb[:, co + hh, 0, :], in0=A4s[0:64, co + hh, 0:64], in1=cs[0:64, 2, 0:64], op=ALU.add), reads=[f'A4s{ch}_{hh}', 'par'], writes=[f'P{ch}_{hh}_0'])

            def stL(hh, lv, cur, nx):
                p, pk_ = bank()
                S.op('pe', lambda e: e.matmul(p[0:64, 0:64], lhsT=R32(Nb[:, co + hh, cur, :]), rhs=R32(Mb[:, co + hh, cur, :]), start=True, stop=True), reads=[f'N{ch}_{hh}_{cur}', f'M{ch}_{hh}_{cur}'], writes=[pk_])
                evac(Mb[:, co + hh, nx, :], p[0:64, 0:64], [pk_], [f'M{ch}_{hh}_{nx}'])
                if lv < 5:
                    p2, pk2 = bank()
                    S.op('pe', lambda e: e.matmul(p2[0:64, 0:64], lhsT=R32(Mb[:, co + hh, cur, :]), rhs=R32(Nb[:, co + hh, cur, :]), start=True, stop=True), reads=[f'N{ch}_{hh}_{cur}', f'M{ch}_{hh}_{cur}'], writes=[pk2])
                    evac(Nb[:, co + hh, nx, :], p2[0:64, 0:64], [pk2], [f'N{ch}_{hh}_{nx}'])
                p3, pk3 = bank()
                S.op('pe', lambda e: e.matmul(p3[0:64, 0:64], lhsT=R32(Mb[:, co + hh, nx, :]), rhs=R32(Pb[:, co + hh, cur, :]), start=True, stop=True), reads=[f'M{ch}_{hh}_{nx}', f'P{ch}_{hh}_{cur}'], writes=[pk3])
                S.op('dve', lambda e: e.tensor_tensor(out=Pb[:, co + hh, nx, :], in0=p3[0:64, 0:64], in1=Pb[:, co + hh, cur, :], op=ALU.add), reads=[pk3, f'P{ch}_{hh}_{cur}'], writes=[f'P{ch}_{hh}_{nx}'])

            def stR1(hh, fin):
                pr, h = hh // 2, hh % 2
                sl = slice(h * 64, h * 64 + 64)
                p, pk_ = bank()
                S.op('pe', lambda e: e.matmul(p[0:64, 0:64], lhsT=AR[sl, pr, ch, 0:64], rhs=Sst[sl, pr, :], start=True, stop=False), reads=[f'AR{pr}', f'S{hh}'], writes=[pk_])
                S.op('pe', lambda e: e.matmul(p[0:64, 0:64], lhsT=A4s[64:128, co + hh, 0:64], rhs=UV[64:128, co + hh, :], start=False, stop=True), reads=[f'A4s{ch}_{hh}', f'UVv{ch}_{pr}'], writes=[pk_], pe_sync=(h == 0))
                evac(Rs[:, hh, :], p[0:64, 0:64], [pk_], [f'Rs{hh}'])

            def stR2(hh, fin):
                p, pk_ = bank()
                S.op('pe', lambda e: e.matmul(p[0:64, 0:64], lhsT=Pb[:, co + hh, fin, :], rhs=Rs[:, hh, :], start=True, stop=True), reads=[f'P{ch}_{hh}_{fin}', f'Rs{hh}'], writes=[pk_])
                evac(UV[0:64, co + hh, :], p[0:64, 0:64], [pk_], [f'UVu{ch}_{hh}'])

            def stR3(hh, fin):
                pr, h = hh // 2, hh % 2
                sl = slice(h * 64, h * 64 + 64)
                p, pk_ = bank()

                def mm_o(e):
                    e.matmul(p[0:64, 0:64], lhsT=AR[sl, pr, ch, 64:128], rhs=Sst[sl, pr, :], start=True, stop=False)
                    return e.matmul(p[0:64, 0:64], lhsT=A4s[:, co + hh, 64:128], rhs=UV[:, co + hh, :], start=False, stop=True)
                S.op('pe', mm_o, reads=[f'AR{pr}', f'S{hh}', f'A4s{ch}_{hh}', f'UVv{ch}_{pr}', f'UVu{ch}_{hh}'], writes=[pk_])
                evac(osb[:, os_, ch, hh * 64:(hh + 1) * 64], p[0:64, 0:64], [pk_], [f'osb{os_}_{hh}'])

            def stR4(hh, fin):
                pr, h = hh // 2, hh % 2
                sl = slice(h * 64, h * 64 + 64)
                p, pk_ = bank()
                S.op('pe', lambda e: e.matmul(p[:, 0:64], lhsT=BKt[:, cp + pr, :], rhs=UV[:, co + hh, :], start=True, stop=True), reads=[f'BKt{ch}_{pr}', f'UVv{ch}_{pr}', f'UVu{ch}_{hh}'], writes=[pk_])
                S.op('dve', lambda e: e.tensor_tensor(out=St[sl, pr, :], in0=p[sl, 0:64], in1=Sst[sl, pr, :], op=ALU.add), reads=[pk_, f'S{hh}'], writes=[f'St{hh}'])
                S.op('dve', lambda e: e.tensor_scalar(out=Sst[sl, pr, :], in0=St[sl, pr, :], scalar1=cw[sl, pr, ch * 64 + 63:ch * 64 + 64], scalar2=None, op0=ALU.mult), reads=[f'St{hh}', f'cw{pr}'], writes=[f'S{hh}'])

            return stT, stA, stL, (stR1, stR2, stR3, stR4)

        for ti in range(TK // TT):
            c0 = ti * TT
            os_ = ti % 2
            tile_prep(ti)
            for pr in range(4):
                pair_prep(ti, pr)
            stg = [stages(ti, ch) for ch in range(NCT)]
            for ch in range(NCT):
                for pr in range(4):
                    stg[ch][0](pr)
            for ch in range(NCT):
                for hh in range(8):
                    stg[ch][1](hh)
            cur = 0
            for lv in range(1, 6):
                nx = 1 - cur
                for ch in range(NCT):
                    for hh in range(8):
                        stg[ch][2](hh, lv, cur, nx)
                cur = nx
            for ch in range(NCT):
                for sR in stg[ch][3]:
                    for hh in range(8):
                        sR(hh, cur)
            S.op('sp', lambda e, c0=c0, os_=os_: e.dma_start(out=o[c0:c0 + TT, :].rearrange("(c t) f -> t c f", t=64), in_=osb[:, os_, :, :]), reads=[f'osb{os_}_{hh}' for hh in range(8)], dma=f'osb{os_}')
        for s in list(S.cnt):
            if s.startswith('D_osb') or s.startswith('D_bvs'):
                S.ops['sp'].append(([(s, S.cnt[s])], None, None, 0))
        S.emit(st)
    return nc


def build_RS_v4():
    TT = 256
    NCT = TT // 64
    nc = bass.Bass("TRN2", target_bir_lowering=False)
    z0 = nc.dram_tensor("z0", [12, 128, TK], F32, kind="ExternalInput").ap()
    zL = nc.dram_tensor("zL", [12, 128, TK], F32, kind="ExternalInput").ap()
    zR = nc.dram_tensor("zR", [12, 128, TK], F32, kind="ExternalInput").ap()
    l0 = nc.dram_tensor("l0", [2, 96, TK], F32, kind="ExternalInput").ap()
    lL = nc.dram_tensor("lL", [2, 96, TK], F32, kind="ExternalInput").ap()
    lR = nc.dram_tensor("lR", [2, 96, TK], F32, kind="ExternalInput").ap()
    mu = nc.dram_tensor("mu", [128, 12, 2], F32, kind="ExternalInput").ap()
    mul = nc.dram_tensor("mul", [96, 2, 2], F32, kind="ExternalInput").ap()
    pvec = nc.dram_tensor("pvec", [128, 4, 5], F32, kind="ExternalInput").ap()
    wup = nc.dram_tensor("wup", [96, 512], F32, kind="ExternalInput").ap()
    aup = nc.dram_tensor("aup", [96, 512], F32, kind="ExternalInput").ap()
    cst = nc.dram_tensor("cst", [128, 4, 128], F32, kind="ExternalInput").ap()
    o = nc.dram_tensor("o", [TK, 512], F32, kind="ExternalOutput").ap()
    bv = nc.dram_tensor("bv", [4, 128, TK], F32, kind="ExternalOutput").ap()
    with ExitStack() as st:
        sb = lambda name, shape, dt: st.enter_context(nc.sbuf_tensor(name, shape, dt))
        zin = sb("zin", [128, 2, 3, TT], F32)
        zs = sb("zs", [128, 12, TT], F32)
        ls = sb("ls", [96, 2, TT], F32)
        mus = sb("mus", [128, 12, 2], F32)
        muls = sb("muls", [96, 2, 2], F32)
        pv = sb("pv", [128, 4, 5], F32)
        wups = sb("wups", [96, 512], F32)
        aups = sb("aups", [96, 512], F32)
        cs = sb("cs", [128, 4, 128], F32)
        zer = sb("zer", [128, 64], F32)
        d1 = sb("d1", [128, 2, TT], F32)
        sg = sb("sg", [128, TT], F32)
        av = sb("av", [128, TT], F32)
        kk = sb("kk", [128, TT], F32)
        kd = sb("kd", [128, TT], F32)
        t1 = sb("t1", [128, TT], F32)
        t2 = sb("t2", [128, TT], F32)
        lcw = sb("lcw", [128, TT], F32)
        cw = sb("cw", [128, 4, TT], F32)
        icw = sb("icw", [128, TT], F32)
        cwp = sb("cwp", [128, TT], F32)
        AR = sb("AR", [128, 4, NCT, 128], BF16)
        BK = sb("BK", [128, 4, NCT, 128], BF16)
        VV = sb("VV", [128, 4, NCT, 128], BF16)
        bvs = sb("bvs", [128, 2, TT], F32)
        Sst = sb("Sst", [128, 4, 64], F32)
        BKt = sb("BKt", [128, NCT * 4, 128], BF16)
        UV = sb("UV", [128, NCT * 8, 64], BF16)
        A4s = sb("A4s", [128, NCT * 8, 128], BF16)
        Mb = sb("Mb", [64, NCT * 8, 2, 64], mybir.dt.float32r if FP32R else F32)
        Nb = sb("Nb", [64, NCT * 8, 2, 64], mybir.dt.float32r if FP32R else F32)
        Pb = sb("Pb", [64, NCT * 8, 2, 64], mybir.dt.float32r if FP32R else F32)
        Rs = sb("Rs", [64, 8, 64], mybir.dt.float32r if FP32R else F32)
        St = sb("St", [128, 4, 64], F32)
        osb = sb("osb", [64, 2, NCT, 512], F32)
        S16 = sb("S16", [128, 4, 64], BF16)
        id16 = sb("id16", [128, 128], BF16)
        pp = [st.enter_context(nc.psum_tensor(f"pp{i}", [128, TT], F32)) for i in range(8)]
        S = Sched(nc)
        ptr = [pp[6][:].bitcast(BF16), pp[7][:].bitcast(BF16)]
        S.alias = {'par': [f'par{i}' for i in range(6)]}
        S.psum_excl = True
        S.op('sp', lambda e: e.dma_start(out=mus[:], in_=mu), writes=['par0'], dma='par0')
        S.op('sp', lambda e: e.dma_start(out=muls[:], in_=mul), writes=['par1'], dma='par1')
        S.op('sp', lambda e: e.dma_start(out=pv[:], in_=pvec), writes=['par2'], dma='par2')
        S.op('sp', lambda e: e.dma_start(out=wups[:], in_=wup), writes=['par3'], dma='par3')
        S.op('sp', lambda e: e.dma_start(out=aups[:], in_=aup), writes=['par4'], dma='par4')
        S.op('sp', lambda e: e.dma_start(out=cs[:], in_=cst), writes=['par5'], dma='par5')
        S.op('dve', lambda e: e.memset(zer[:], 0.0), writes=['zer'])
        S.op('dve', lambda e: e.memset(Sst[:], 0.0), reads=['par', 'zer'], writes=[f'S{p}' for p in range(8)])
        S.op('dve', lambda e: e.memset(VV[:], 0.0), writes=['VV0', 'VV1', 'VV2', 'VV3'])
        S.op('dve', lambda e: e.memset(S16[:], 0.0), writes=[f'S16_{p}' for p in range(8)])
        S.op('dve', lambda e: e.tensor_copy(out=id16[:], in_=cs[:, 2, :]), reads=['par'], writes=['id16'])
        blk1 = cs[:, 0, :]
        mask4 = cs[:, 1, :]
        ident = cs[:, 2, :]
        zcnt = [0]

        def shift(dst, n_part, src0, srcL, srcR, muL, muR, dkey):
            i = zcnt[0] % 2
            zcnt[0] += 1
            P = n_part
            S.op('sp', lambda e: e.dma_start(out=zin[:P, i, 0, :], in_=src0), writes=[f'zin{i}_0'], dma=f'zin{i}_0')
            S.op('sp', lambda e: e.dma_start(out=zin[:P, i, 1, :], in_=srcL), writes=[f'zin{i}_1'], dma=f'zin{i}_1')
            S.op('sp', lambda e: e.dma_start(out=zin[:P, i, 2, :], in_=srcR), writes=[f'zin{i}_2'], dma=f'zin{i}_2')
            S.op('pool', lambda e: e.tensor_tensor(out=d1[:P, 0, :], in0=zin[:P, i, 1, :], in1=zin[:P, i, 0, :], op=ALU.subtract), reads=[f'zin{i}_0', f'zin{i}_1'], writes=['d1a'])
            S.op('pool', lambda e: e.tensor_tensor(out=d1[:P, 1, :], in0=zin[:P, i, 2, :], in1=zin[:P, i, 0, :], op=ALU.subtract), reads=[f'zin{i}_0', f'zin{i}_2'], writes=['d1b'])
            S.op('dve', lambda e: e.scalar_tensor_tensor(out=d1[:P, 0, :], in0=d1[:P, 0, :], scalar=muL, in1=zin[:P, i, 0, :], op0=ALU.mult, op1=ALU.add), reads=['d1a', f'zin{i}_0', 'par'], writes=['d1a'])
            S.op('dve', lambda e: e.scalar_tensor_tensor(out=dst, in0=d1[:P, 1, :], scalar=muR, in1=d1[:P, 0, :], op0=ALU.mult, op1=ALU.add), reads=['d1a', 'd1b', 'par'], writes=[dkey])

        def tile_prep(ti):
            c0 = ti * TT
            for c in range(12):
                shift(zs[:, c, :], 128, z0[c, :, c0:c0 + TT], zL[c, :, c0:c0 + TT], zR[c, :, c0:c0 + TT], mus[:, c, 0:1], mus[:, c, 1:2], f'zs{c}')
            for j in range(2):
                shift(ls[:, j, :], 96, l0[j, :, c0:c0 + TT], lL[j, :, c0:c0 + TT], lR[j, :, c0:c0 + TT], muls[:, j, 0:1], muls[:, j, 1:2], f'ls{j}')
            S.op('act', lambda e: e.activation(out=ls[:, 0, :], in_=ls[:, 0, :], func=AF.Tanh), reads=['ls0'], writes=['ls0'])

        def pair_prep(ti, pr):
            q = pr
            qb = pr % 2
            r_ = zs[:, pr, :]
            k_ = zs[:, 4 + pr, :]
            v_ = zs[:, 8 + pr, :]
            rk = f'zs{pr}'; kkey = f'zs{4 + pr}'; vkey = f'zs{8 + pr}'
            S.op('pe', lambda e: e.matmul(pp[0][:, :TT], lhsT=wups[:, pr * 128:(pr + 1) * 128], rhs=ls[:, 0, :], start=True, stop=True), reads=['ls0', 'par'], writes=['pp0'])
            S.op('act', lambda e: e.activation(out=sg[:], in_=pp[0][:, :TT], func=AF.Sigmoid, bias=pv[:, pr, 0:1], scale=1.0), reads=['pp0', 'par'], writes=['sg'])
            S.op('pe', lambda e: e.matmul(pp[1][:, :TT], lhsT=aups[:, pr * 128:(pr + 1) * 128], rhs=ls[:, 1, :], start=True, stop=True), reads=['ls1', 'par'], writes=['pp1'])
            S.op('act', lambda e: e.activation(out=av[:], in_=pp[1][:, :TT], func=AF.Sigmoid, bias=pv[:, pr, 1:2], scale=1.0), reads=['pp1', 'par'], writes=['av'])
            S.op('dve', lambda e: e.tensor_scalar(out=kk[:], in0=k_, scalar1=pv[:, pr, 2:3], scalar2=None, op0=ALU.mult), reads=[kkey, 'par'], writes=['kk'])
            S.op('act', lambda e: e.activation(out=t1[:], in_=kk[:], func=AF.Square), reads=['kk'], writes=['t1'])
            S.op('pe', lambda e: e.matmul(pp[0][:, :TT], lhsT=blk1, rhs=t1[:], start=True, stop=True), reads=['t1', 'par'], writes=['pp0'])
            S.op('act', lambda e: e.activation(out=t2[:], in_=pp[0][:, :TT], func=AF.Sqrt), reads=['pp0'], writes=['t2'])
            S.op('dve', lambda e: e.tensor_scalar(out=t2[:], in0=t2[:], scalar1=1e-12, scalar2=None, op0=ALU.max), reads=['t2'], writes=['t2'])
            S.op('dve', lambda e: e.reciprocal(out=t2[:], in_=t2[:]), reads=['t2'], writes=['t2'])
            S.op('dve', lambda e: e.tensor_tensor(out=kk[:], in0=kk[:], in1=t2[:], op=ALU.mult), reads=['kk', 't2'], writes=['kk'])
            S.op('dve', lambda e: e.tensor_scalar(out=t1[:], in0=av[:], scalar1=-1.0, scalar2=pv[:, pr, 3:4], op0=ALU.add, op1=ALU.mult), reads=['av', 'par', 't1'], writes=['t1'])
            S.op('dve', lambda e: e.scalar_tensor_tensor(out=kd[:], in0=t1[:], scalar=1.0, in1=k_, op0=ALU.add, op1=ALU.mult), reads=['t1', kkey], writes=['kd'])
            S.op('dve', lambda e: e.tensor_scalar(out=sg[:], in0=sg[:], scalar1=LWC, scalar2=None, op0=ALU.mult), reads=['sg'], writes=['sg'])
            for ch in range(NCT):
                S.op('dve', lambda e, ch=ch: e.tensor_tensor_scan(out=lcw[:, ch * 64:(ch + 1) * 64], data0=sg[:, ch * 64:(ch + 1) * 64], data1=zer[:], initial=0.0, op0=ALU.add, op1=ALU.add),
                     reads=['sg', 'zer'], writes=['lcw'])
            S.op('act', lambda e: e.activation(out=cw[:, q, :], in_=lcw[:], func=AF.Exp), reads=['lcw'], writes=[f'cw{q}'])
            S.op('act', lambda e: e.activation(out=icw[:], in_=lcw[:], func=AF.Exp, scale=-1.0), reads=['lcw'], writes=['icw'])
            S.op('dve', lambda e: e.tensor_tensor(out=cwp[:], in0=lcw[:], in1=sg[:], op=ALU.subtract), reads=['lcw', 'sg'], writes=['cwp'])
            S.op('act', lambda e: e.activation(out=cwp[:], in_=cwp[:], func=AF.Exp), reads=['cwp'], writes=['cwp'])
            v3 = lambda ap: ap.rearrange("p (c t) -> p c t", t=64)
            S.op('dve', lambda e: e.scalar_tensor_tensor(out=AR[:, q, :, 0:64], in0=v3(kk[:]), scalar=-1.0, in1=v3(cwp[:]), op0=ALU.mult, op1=ALU.mult), reads=['kk', 'cwp'], writes=[f'AR{q}'])
            S.op('dve', lambda e: e.tensor_tensor(out=AR[:, q, :, 64:128], in0=v3(r_), in1=v3(cw[:, q, :]), op=ALU.mult), reads=[rk, f'cw{q}'], writes=[f'AR{q}'])
            S.op('dve', lambda e: e.tensor_tensor(out=t1[:], in0=kk[:], in1=av[:], op=ALU.mult), reads=['kk', 'av', 't1'], writes=['t1'])
            S.op('dve', lambda e: e.tensor_tensor(out=BK[:, q, :, 0:64], in0=v3(t1[:]), in1=v3(icw[:]), op=ALU.mult), reads=['t1', 'icw'], writes=[f'BK{q}'])
            S.op('dve', lambda e: e.tensor_tensor(out=BK[:, q, :, 64:128], in0=v3(kd[:]), in1=v3(icw[:]), op=ALU.mult), reads=['kd', 'icw'], writes=[f'BK{q}'])
            S.op('pool', lambda e: e.tensor_copy(out=VV[:, q, :, 64:128], in_=v3(v_)), reads=[vkey], writes=[f'VV{q}'])
            S.op('dve', lambda e: e.scalar_tensor_tensor(out=t2[:], in0=r_, scalar=pv[:, pr, 4:5], in1=kd[:], op0=ALU.mult, op1=ALU.mult), reads=[rk, 'kd', 'par', 't2'], writes=['t2'])
            S.op('pe', lambda e: e.matmul(pp[1][:, :TT], lhsT=blk1, rhs=t2[:], start=True, stop=True), reads=['t2', 'par'], writes=['pp1'])
            S.op('dve', lambda e: e.tensor_tensor(out=bvs[:, qb, :], in0=pp[1][:, :TT], in1=v_, op=ALU.mult), reads=['pp1', vkey], writes=[f'bvs{qb}'])
            S.op('sp', lambda e: e.dma_start(out=bv[pr, :, ti * TT:(ti + 1) * TT], in_=bvs[:, qb, :]), reads=[f'bvs{qb}'], dma=f'bvs{qb}')

        ecnt = [0]

        def evac(dst, src, rd, wr):
            ecnt[0] += 1
            if ecnt[0] % 2:
                S.op('act', lambda e: e.activation(out=dst, in_=src, func=AF.Copy), reads=rd, writes=wr)
            else:
                S.op('dve', lambda e: e.tensor_copy(out=dst, in_=src), reads=rd, writes=wr)

        R32 = (lambda ap: ap)
        rot = [0]

        def bank():
            k = 2 + rot[0] % 4
            rot[0] += 1
            return pp[k], f'pp{k}'

        def stages(ti, ch):
            os_ = ti % 2
            co = ch * 8
            cp = ch * 4

            def stT(pr):
                tb = (cp + pr) % 2
                p = ptr[tb]
                pk_ = f'pp{6 + tb}'
                S.op('pe', lambda e: e.transpose(p[:, 0:128], BK[:, pr, ch, :], id16[:]), reads=[f'BK{pr}', 'id16'], writes=[pk_])
                evac(BKt[:, cp + pr, :], p[:, 0:128], [pk_], [f'BKt{ch}_{pr}'])
                S.op('pe', lambda e: e.transpose(p[:, 128:256], VV[:, pr, ch, :], id16[:]), reads=[f'VV{pr}', 'id16'], writes=[pk_])
                evac(UV[64:128, co + 2 * pr:co + 2 * pr + 2, :], p[64:128, 128:256].rearrange("p (h v) -> p h v", h=2), [pk_], [f'UVv{ch}_{pr}'])

            def stA(hh):
                pr, h = hh // 2, hh % 2
                sl = slice(h * 64, h * 64 + 64)
                p, pk_ = bank()
                S.op('pe', lambda e: e.matmul(p[:, 0:128], lhsT=BK[sl, pr, ch, :], rhs=AR[sl, pr, ch, :], start=True, stop=True), reads=[f'BK{pr}', f'AR{pr}'], writes=[pk_])
                S.op('dve', lambda e: e.tensor_tensor(out=A4s[:, co + hh, :], in0=p[:, 0:128], in1=mask4, op=ALU.mult), reads=[pk_, 'par'], writes=[f'A4s{ch}_{hh}'])
                p2, pk2 = bank()
                S.op('pe', lambda e: e.matmul(p2[0:64, 0:64], lhsT=AR[sl, pr, ch, 0:64], rhs=BK[sl, pr, ch, 0:64], start=True, stop=True), reads=[f'BK{pr}', f'AR{pr}'], writes=[pk2])
                S.op('dve', lambda e: e.tensor_tensor(out=Mb[:, co + hh, 0, :], in0=p2[0:64, 0:64], in1=cs[0:64, 3, 0:64], op=ALU.mult), reads=[pk2, 'par'], writes=[f'M{ch}_{hh}_0'])
                S.op('dve', lambda e: e.tensor_tensor(out=Nb[:, co + hh, 0, :], in0=p[0:64, 0:64], in1=cs[0:64, 1, 0:64], op=ALU.mult), reads=[pk_, 'par'], writes=[f'N{ch}_{hh}_0'])
                S.op('pool', lambda e: e.tensor_tensor(out=Pb[:, co + hh, 0, :], in0=Nb[:, co + hh, 0, :], in1=cs[0:64, 2, 0:64], op=ALU.add), reads=[f'N{ch}_{hh}_0', 'par'], writes=[f'P{ch}_{hh}_0'])

            def stL(hh, lv, cur, nx):
                p, pk_ = bank()
                S.op('pe', lambda e: e.matmul(p[0:64, 0:64], lhsT=R32(Nb[:, co + hh, cur, :]), rhs=R32(Mb[:, co + hh, cur, :]), start=True, stop=True), reads=[f'N{ch}_{hh}_{cur}', f'M{ch}_{hh}_{cur}'], writes=[pk_])
                evac(Mb[:, co + hh, nx, :], p[0:64, 0:64], [pk_], [f'M{ch}_{hh}_{nx}'])
                if lv < 5:
                    p2, pk2 = bank()
                    S.op('pe', lambda e: e.matmul(p2[0:64, 0:64], lhsT=R32(Mb[:, co + hh, cur, :]), rhs=R32(Nb[:, co + hh, cur, :]), start=True, stop=True), reads=[f'N{ch}_{hh}_{cur}', f'M{ch}_{hh}_{cur}'], writes=[pk2])
                    evac(Nb[:, co + hh, nx, :], p2[0:64, 0:64], [pk2], [f'N{ch}_{hh}_{nx}'])
                p3, pk3 = bank()
                S.op('pe', lambda e: e.matmul(p3[0:64, 0:64], lhsT=R32(Mb[:, co + hh, nx, :]), rhs=R32(Pb[:, co + hh, cur, :]), start=True, stop=True), reads=[f'M{ch}_{hh}_{nx}', f'P{ch}_{hh}_{cur}'], writes=[pk3])
                S.op('dve', lambda e: e.tensor_tensor(out=Pb[:, co + hh, nx, :], in0=p3[0:64, 0:64], in1=Pb[:, co + hh, cur, :], op=ALU.add), reads=[pk3, f'P{ch}_{hh}_{cur}'], writes=[f'P{ch}_{hh}_{nx}'])

            def stR1(hh, fin):
                pr, h = hh // 2, hh % 2
                sl = slice(h * 64, h * 64 + 64)
                p, pk_ = bank()
                S.op('pe', lambda e: e.matmul(p[0:64, 0:64], lhsT=AR[sl, pr, ch, 0:64], rhs=S16[sl, pr, :], start=True, stop=False), reads=[f'AR{pr}', f'S16_{hh}'], writes=[pk_])
                S.op('pe', lambda e: e.matmul(p[0:64, 0:64], lhsT=A4s[64:128, co + hh, 0:64], rhs=UV[64:128, co + hh, :], start=False, stop=True), reads=[f'A4s{ch}_{hh}', f'UVv{ch}_{pr}'], writes=[pk_], pe_sync=(h == 0))
                evac(Rs[:, hh, :], p[0:64, 0:64], [pk_], [f'Rs{hh}'])

            def stR2(hh, fin):
                p, pk_ = bank()
                S.op('pe', lambda e: e.matmul(p[0:64, 0:64], lhsT=Pb[:, co + hh, fin, :], rhs=Rs[:, hh, :], start=True, stop=True), reads=[f'P{ch}_{hh}_{fin}', f'Rs{hh}'], writes=[pk_])
                evac(UV[0:64, co + hh, :], p[0:64, 0:64], [pk_], [f'UVu{ch}_{hh}'])

            def stR3(hh, fin):
                pr, h = hh // 2, hh % 2
                sl = slice(h * 64, h * 64 + 64)
                p, pk_ = bank()

                def mm_o(e):
                    e.matmul(p[0:64, 0:64], lhsT=AR[sl, pr, ch, 64:128], rhs=S16[sl, pr, :], start=True, stop=False)
                    return e.matmul(p[0:64, 0:64], lhsT=A4s[:, co + hh, 64:128], rhs=UV[:, co + hh, :], start=False, stop=True)
                S.op('pe', mm_o, reads=[f'AR{pr}', f'S16_{hh}', f'A4s{ch}_{hh}', f'UVv{ch}_{pr}', f'UVu{ch}_{hh}'], writes=[pk_])
                evac(osb[:, os_, ch, hh * 64:(hh + 1) * 64], p[0:64, 0:64], [pk_], [f'osb{os_}_{hh}'])

            def stR4(hh, fin):
                pr, h = hh // 2, hh % 2
                sl = slice(h * 64, h * 64 + 64)
                p, pk_ = bank()
                S.op('pe', lambda e: e.matmul(p[:, 0:64], lhsT=BKt[:, cp + pr, :], rhs=UV[:, co + hh, :], start=True, stop=True), reads=[f'BKt{ch}_{pr}', f'UVv{ch}_{pr}', f'UVu{ch}_{hh}'], writes=[pk_])
                S.op('dve', lambda e: e.tensor_tensor(out=St[sl, pr, :], in0=p[sl, 0:64], in1=Sst[sl, pr, :], op=ALU.add), reads=[pk_, f'S{hh}'], writes=[f'St{hh}'])
                S.op('dve', lambda e: e.tensor_scalar(out=Sst[sl, pr, :], in0=St[sl, pr, :], scalar1=cw[sl, pr, ch * 64 + 63:ch * 64 + 64], scalar2=None, op0=ALU.mult), reads=[f'St{hh}', f'cw{pr}'], writes=[f'S{hh}'])
                S.op('pool', lambda e: e.tensor_scalar(out=S16[sl, pr, :], in0=St[sl, pr, :], scalar1=cw[sl, pr, ch * 64 + 63:ch * 64 + 64], scalar2=None, op0=ALU.mult), reads=[f'St{hh}', f'cw{pr}'], writes=[f'S16_{hh}'])

            return stT, stA, stL, (stR1, stR2, stR3, stR4)

        for ti in range(TK // TT):
            c0 = ti * TT
            os_ = ti % 2
            tile_prep(ti)
            for pr in range(4):
                pair_prep(ti, pr)
            stg = [stages(ti, ch) for ch in range(NCT)]
            for ch in range(NCT):
                for pr in range(4):
                    stg[ch][0](pr)
            for ch in range(NCT):
                for hh in range(8):
                    stg[ch][1](hh)
            cur = 0
            for lv in range(1, 6):
                nx = 1 - cur
                for ch in range(NCT):
                    for hh in range(8):
                        stg[ch][2](hh, lv, cur, nx)
                cur = nx
            for ch in range(NCT):
                for sR in stg[ch][3]:
                    for hh in range(8):
                        sR(hh, cur)
            S.op('sp', lambda e, c0=c0, os_=os_: e.dma_start(out=o[c0:c0 + TT, :].rearrange("(c t) f -> t c f", t=64), in_=osb[:, os_, :, :]), reads=[f'osb{os_}_{hh}' for hh in range(8)], dma=f'osb{os_}')
        for s in list(S.cnt):
            if s.startswith('D_osb') or s.startswith('D_bvs'):
                S.ops['sp'].append(([(s, S.cnt[s])], None, None, 0))
        S.emit(st)
    return nc


def rs_consts():
    c = np.zeros((128, 4, 128), np.float32)
    p = np.arange(128)[:, None]
    q = np.arange(128)[None, :]
    c[:, 0, :] = (p // 64 == q // 64)
    j = p % 64
    t = q % 64
    c[:, 1, :] = np.where(q < 64, j < t, j <= t)
    c[:, 2, :] = (p == q)
    c[:, 3, :] = (j > t)
    return c


def prep_RS(l, I, zlat, zctx, d):
    def seg(a):
        a = a[::-1] if d == 1 else a
        zl = np.concatenate([np.zeros_like(a[:1]), a[:-1]], axis=0)
        zr = np.concatenate([a[1:], np.zeros_like(a[:1])], axis=0)
        return a, zl, zr
    c0, cl, cr = seg(zctx)
    a0, al, ar = seg(zlat)
    s0 = np.concatenate([c0, a0], axis=0)
    sL = np.concatenate([cl, al], axis=0)
    sR = np.concatenate([cr, ar], axis=0)
    mp, mn = I['rw_mu_prev'][l], I['rw_mu_next'][l]
    muL, muR = (mp, mn) if d == 0 else (mn, mp)
    wl0, al0 = 1536 + d * 96, 1536 + 192 + d * 96

    def rk(s):
        return np.ascontiguousarray(s[:, 0:1536].T).reshape(12, 128, TK)

    def lo(s):
        return np.ascontiguousarray(np.stack([s[:, wl0:wl0 + 96].T, s[:, al0:al0 + 96].T]))
    mu = np.ascontiguousarray(np.stack([muL[0:1536].reshape(12, 128).T, muR[0:1536].reshape(12, 128).T], axis=2))
    mul = np.ascontiguousarray(np.stack([np.stack([muL[wl0:wl0 + 96], muL[al0:al0 + 96]], axis=1), np.stack([muR[wl0:wl0 + 96], muR[al0:al0 + 96]], axis=1)], axis=2))
    pv = np.stack([I['rw_w0'][l][d], I['rw_a0'][l][d], I['rw_k_k'][l], I['rw_k_a'][l], I['rw_r_k'][l].reshape(512)], axis=1)
    pvec = np.ascontiguousarray(pv.reshape(4, 128, 5).transpose(1, 0, 2))
    return {"z0": rk(s0), "zL": rk(sL), "zR": rk(sR), "l0": lo(s0), "lL": lo(sL), "lR": lo(sR), "mu": mu, "mul": mul, "pvec": pvec,
            "wup": np.ascontiguousarray(I['rw_w_up'][l][d]), "aup": np.ascontiguousarray(I['rw_a_up'][l][d]), "cst": rs_consts()}


def unrev(a, d, axis):
    if d == 0:
        return a
    c, t = np.split(a, [CTX], axis=axis)
    return np.concatenate([np.flip(c, axis), np.flip(t, axis)], axis=axis)


RW_GN_EPS = 64e-5


def build_RO():
    nc = bass.Bass("TRN2", target_bir_lowering=False)
    T = TPC
    of = nc.dram_tensor("of", [4, 128, T], F32, kind="ExternalInput").ap()
    ob = nc.dram_tensor("ob", [4, 128, T], F32, kind="ExternalInput").ap()
    bf_ = nc.dram_tensor("bvf", [4, 128, T], F32, kind="ExternalInput").ap()
    bb_ = nc.dram_tensor("bvb", [4, 128, T], F32, kind="ExternalInput").ap()
    g0 = nc.dram_tensor("g0", [2, 128, T], F32, kind="ExternalInput").ap()
    gL = nc.dram_tensor("gL", [2, 128, T], F32, kind="ExternalInput").ap()
    gR = nc.dram_tensor("gR", [2, 128, T], F32, kind="ExternalInput").ap()
    gmu = nc.dram_tensor("gmu", [128, 2, 2], F32, kind="ExternalInput").ap()
    gup = nc.dram_tensor("gup", [128, 2, 512], F32, kind="ExternalInput").ap()
    lnp = nc.dram_tensor("lnp", [128, 4, 2], F32, kind="ExternalInput").ap()
    blk = nc.dram_tensor("blk", [128, 128], F32, kind="ExternalInput").ap()
    ya = nc.dram_tensor("ya", [4, 128, T], BF16, kind="ExternalOutput").ap()
    with ExitStack() as st:
        sb = lambda name, shape, dt: st.enter_context(nc.sbuf_tensor(name, shape, dt))
        a_ = sb("a_", [128, 2, 512], F32)
        b_ = sb("b_", [128, 2, 512], F32)
        c_ = sb("c_", [128, 2, 512], F32)
        d_ = sb("d_", [128, 2, 512], F32)
        gz = sb("gz", [128, 3, 2, 512], F32)
        sgd = sb("sgd", [128, 2, 512], F32)
        xc = sb("xc", [128, 512], F32)
        sq = sb("sq", [128, 512], F32)
        rs_ = sb("rs_", [128, 512], F32)
        yo = sb("yo", [128, 2, 512], BF16)
        gmus = sb("gmus", [128, 2, 2], F32)
        gups = sb("gups", [128, 2, 512], F32)
        lns = sb("lns", [128, 4, 2], F32)
        blks = sb("blks", [128, 128], F32)
        eps_t = sb("eps", [128, 1], F32)
        p0 = st.enter_context(nc.psum_tensor("p0", [128, 512], F32))
        p1 = st.enter_context(nc.psum_tensor("p1", [128, 512], F32))
        p2 = st.enter_context(nc.psum_tensor("p2", [128, 512], F32))
        S = Sched(nc)
        S.alias = {'par': [f'par{i}' for i in range(4)]}
        S.op('sp', lambda e: e.dma_start(out=gmus[:], in_=gmu), writes=['par0'], dma='par0')
        S.op('sp', lambda e: e.dma_start(out=gups[:], in_=gup), writes=['par1'], dma='par1')
        S.op('sp', lambda e: e.dma_start(out=lns[:], in_=lnp), writes=['par2'], dma='par2')
        S.op('sp', lambda e: e.dma_start(out=blks[:], in_=blk), writes=['par3'], dma='par3')
        S.op('dve', lambda e: e.memset(eps_t[:], RW_GN_EPS), writes=['eps'])
        cnt = [0]

        def tile(t0, n):
            for j in range(2):
                for w, src in enumerate((g0, gL, gR)):
                    S.op('sp', lambda e, j=j, w=w, src=src: e.dma_start(out=gz[:, w, j, :n], in_=src[j, :, t0:t0 + n]), writes=[f'gz{w}{j}'], dma=f'gz{w}{j}')
                S.op('pool', lambda e, j=j: e.tensor_tensor(out=gz[:, 1, j, :n], in0=gz[:, 1, j, :n], in1=gz[:, 0, j, :n], op=ALU.subtract), reads=[f'gz0{j}', f'gz1{j}'], writes=[f'gz1{j}'])
                S.op('pool', lambda e, j=j: e.tensor_tensor(out=gz[:, 2, j, :n], in0=gz[:, 2, j, :n], in1=gz[:, 0, j, :n], op=ALU.subtract), reads=[f'gz0{j}', f'gz2{j}'], writes=[f'gz2{j}'])
                S.op('dve', lambda e, j=j: e.scalar_tensor_tensor(out=gz[:, 1, j, :n], in0=gz[:, 1, j, :n], scalar=gmus[:, j, 0:1], in1=gz[:, 0, j, :n], op0=ALU.mult, op1=ALU.add), reads=[f'gz0{j}', f'gz1{j}', 'par'], writes=[f'gz1{j}'])
                S.op('dve', lambda e, j=j: e.scalar_tensor_tensor(out=gz[:, 2, j, :n], in0=gz[:, 2, j, :n], scalar=gmus[:, j, 1:2], in1=gz[:, 1, j, :n], op0=ALU.mult, op1=ALU.add), reads=[f'gz1{j}', f'gz2{j}', 'par'], writes=[f'gz2{j}'])
                S.op('act', lambda e, j=j: e.activation(out=sgd[:, j, :n], in_=gz[:, 2, j, :n], func=AF.Sigmoid), reads=[f'gz2{j}'], writes=[f'sgd{j}'])

            def pair(pr):
                i = cnt[0] % 2
                cnt[0] += 1
                for buf, src, nm in ((a_, of, 'a'), (b_, ob, 'b'), (c_, bf_, 'c'), (d_, bb_, 'd')):
                    S.op('sp', lambda e, buf=buf, src=src: e.dma_start(out=buf[:, i, :n], in_=src[pr, :, t0:t0 + n]), writes=[f'{nm}{i}'], dma=f'{nm}{i}')
                S.op('dve', lambda e: e.tensor_tensor(out=a_[:, i, :n], in0=a_[:, i, :n], in1=b_[:, i, :n], op=ALU.add), reads=[f'a{i}', f'b{i}'], writes=[f'a{i}'])
                S.op('pe', lambda e: e.matmul(p0[:, :n], lhsT=blks[:], rhs=a_[:, i, :n], start=True, stop=True), reads=[f'a{i}', 'par'], writes=['p0'])
                S.op('dve', lambda e: e.tensor_tensor(out=xc[:, :n], in0=a_[:, i, :n], in1=p0[:, :n], op=ALU.subtract), reads=[f'a{i}', 'p0'], writes=['xc'])
                S.op('act', lambda e: e.activation(out=sq[:, :n], in_=xc[:, :n], func=AF.Square), reads=['xc'], writes=['sq'])
                S.op('pe', lambda e: e.matmul(p1[:, :n], lhsT=blks[:], rhs=sq[:, :n], start=True, stop=True), reads=['sq', 'par'], writes=['p1'])
                S.op('act', lambda e: e.activation(out=rs_[:, :n], in_=p1[:, :n], func=AF.Sqrt, bias=eps_t[:], scale=1.0), reads=['p1', 'eps'], writes=['rs'])
                S.op('dve', lambda e: e.reciprocal(out=rs_[:, :n], in_=rs_[:, :n]), reads=['rs'], writes=['rs'])
                S.op('dve', lambda e: e.tensor_tensor(out=xc[:, :n], in0=xc[:, :n], in1=rs_[:, :n], op=ALU.mult), reads=['xc', 'rs'], writes=['xc'])
                S.op('act', lambda e: e.activation(out=xc[:, :n], in_=xc[:, :n], func=AF.Identity, bias=lns[:, pr, 1:2], scale=lns[:, pr, 0:1]), reads=['xc', 'par'], writes=['xc'])
                S.op('pool', lambda e: e.tensor_tensor(out=c_[:, i, :n], in0=c_[:, i, :n], in1=d_[:, i, :n], op=ALU.add), reads=[f'c{i}', f'd{i}'], writes=[f'c{i}'])
                S.op('dve', lambda e: e.tensor_tensor(out=xc[:, :n], in0=xc[:, :n], in1=c_[:, i, :n], op=ALU.add), reads=['xc', f'c{i}'], writes=['xc'])

                def mmg(e):
                    e.matmul(p2[:, :n], lhsT=gups[:, 0, pr * 128:(pr + 1) * 128], rhs=sgd[:, 0, :n], start=True, stop=False)
                    return e.matmul(p2[:, :n], lhsT=gups[:, 1, pr * 128:(pr + 1) * 128], rhs=sgd[:, 1, :n], start=False, stop=True)
                S.op('pe', mmg, reads=['sgd0', 'sgd1', 'par'], writes=['p2'])
                S.op('dve', lambda e: e.tensor_tensor(out=yo[:, i, :n], in0=xc[:, :n], in1=p2[:, :n], op=ALU.mult), reads=['xc', 'p2'], writes=[f'yo{i}'])
                S.op('sp', lambda e: e.dma_start(out=ya[pr, :, t0:t0 + n], in_=yo[:, i, :n]), reads=[f'yo{i}'], dma=f'yo{i}')
            for pr in range(4):
                pair(pr)
        for (t0, n) in token_tiles(T, 512):
            tile(t0, n)
        for i in range(2):
            S.ops['sp'].append(([(f'D_yo{i}', S.cnt[f'D_yo{i}'])], None, None, 0))
        S.emit(st)
    return nc


_PROGS = {}


def _prog(name, fn):
    if name not in _PROGS:
        _PROGS[name] = fn()
    return _PROGS[name]


def _run(nc, ins):
    res = run_bass_kernel_spmd(nc, ins, core_ids=list(range(len(ins))))
    return res.results


def _core_tok(s):
    return np.concatenate([np.arange(s * 2048, (s + 1) * 2048), SEQ + np.arange(s * 128, (s + 1) * 128)])


def _nbrs(a):
    def seg(x):
        z = np.zeros_like(x[:, :1])
        return np.concatenate([z, x[:, :-1]], axis=1), np.concatenate([x[:, 1:], z], axis=1)
    ll, lr = seg(a[:, :SEQ])
    cl, cr = seg(a[:, SEQ:])
    return np.concatenate([ll, cl], axis=1), np.concatenate([lr, cr], axis=1)


def kernel(**I):
    I = {k: np.asarray(v) for k, v in I.items()}
    x, ctx = I['x'], I['ctx']
    xT = []
    for c in range(8):
        b, s = c // 2, c % 2
        xT.append(np.ascontiguousarray(np.concatenate([x[b, s * 2048:(s + 1) * 2048].T, ctx[b, s * 128:(s + 1) * 128].T], axis=1)))
    C, Sg, Pm = rope_tables()
    pm = _perm_heads(np.arange(128), 128)
    blk64 = np.ascontiguousarray(rs_consts()[:, 0, :] / 64.0).astype(np.float32)
    for l in range(DEPTH):
        shP = prep_P_shared(l, I)
        res = _run(_prog('P', build_P), [dict(shP, xT=xT[c], cT=cT_of(I, c // 2)) for c in range(8)])
        ZA, ZB = [], []
        for b in range(NB):
            for nm, lst in (('zA', ZA), ('zB', ZB)):
                r0, r1 = res[2 * b][nm], res[2 * b + 1][nm]
                lst.append(np.concatenate([r0[:, :2048], r1[:, :2048], r0[:, 2048:], r1[:, 2048:]], axis=1))
        del res
        gains = np.ascontiguousarray(np.stack([I['gq_q_norm'][l][pm], I['gq_k_norm'][l][pm]], axis=1))
        ins = []
        for c in range(8):
            b, g = c // 2, c % 2
            q0 = RWC + 4 * g * 128
            ins.append({"qT": np.ascontiguousarray(ZA[b][q0:q0 + 512].reshape(4, 128, TK)),
                        "kT": np.ascontiguousarray(ZA[b][RWC + 1024 + g * 128:RWC + 1024 + (g + 1) * 128]),
                        "V": np.ascontiguousarray(ZB[b][g * 128:(g + 1) * 128].T),
                        "ropeC": C, "ropeS": Sg, "gains": gains, "pswap": Pm})
        rGQ = _run(_prog('GQ', build_GQ), ins)
        ins = []
        for c in range(8):
            b, hg = c // 2, c % 2
            o_ = 256 + hg * 256
            ins.append({"qT": np.ascontiguousarray(ZB[b][o_:o_ + 256].reshape(4, 64, TK)),
                        "kT": np.ascontiguousarray(ZB[b][o_ + 512:o_ + 768].reshape(4, 64, TK)),
                        "V": np.ascontiguousarray(ZB[b][o_ + 1024:o_ + 1280].reshape(4, 64, TK).transpose(0, 2, 1)),
                        "bias": na_bias(I['na_rpb'][l][4 * hg:4 * hg + 4])})
        rNA = _run(_prog('NA', build_NA), ins)
        ins = []
        for c in range(8):
            b, d = c // 2, c % 2
            zt = np.ascontiguousarray(ZA[b][0:RWC].T)
            ins.append(prep_RS(l, I, zt[:SEQ], zt[SEQ:], d))
        rRS = _run(_prog('RS', build_RS), ins)
        del ins
        mp, mn = I['rw_mu_prev'][l], I['rw_mu_next'][l]
        gmu = np.ascontiguousarray(np.stack([mp[1920:2176].reshape(2, 128).T, mn[1920:2176].reshape(2, 128).T], axis=2))
        gup = np.ascontiguousarray(I['rw_g_up'][l].reshape(2, 128, 512).transpose(1, 0, 2))
        lnp = np.ascontiguousarray(np.stack([I['rw_ln_w'][l].reshape(4, 128).T, I['rw_ln_b'][l].reshape(4, 128).T], axis=2))
        per_b = []
        for b in range(NB):
            d_ = {}
            for d, nm in ((0, 'f'), (1, 'b')):
                o = unrev(rRS[2 * b + d]["o"], d, 0)
                o = np.concatenate([o[CTX:], o[:CTX]], axis=0)
                d_['o' + nm] = np.ascontiguousarray(o.T).reshape(4, 128, TK)
                v_ = unrev(rRS[2 * b + d]["bv"], d, 2)
                d_['bv' + nm] = np.concatenate([v_[:, :, CTX:], v_[:, :, :CTX]], axis=2)
            g0 = ZA[b][1920:2176]
            gl, gr = _nbrs(g0)
            d_['g0'], d_['gL'], d_['gR'] = g0, gl, gr
            per_b.append(d_)
        del rRS
        ins = []
        for c in range(8):
            b, s = c // 2, c % 2
            tk = _core_tok(s)
            pb = per_b[b]
            ins.append({"of": np.ascontiguousarray(pb['of'][:, :, tk]), "ob": np.ascontiguousarray(pb['ob'][:, :, tk]),
                        "bvf": np.ascontiguousarray(pb['bvf'][:, :, tk]), "bvb": np.ascontiguousarray(pb['bvb'][:, :, tk]),
                        "g0": np.ascontiguousarray(pb['g0'][:, tk]).reshape(2, 128, TPC), "gL": np.ascontiguousarray(pb['gL'][:, tk]).reshape(2, 128, TPC),
                        "gR": np.ascontiguousarray(pb['gR'][:, tk]).reshape(2, 128, TPC), "gmu": gmu, "gup": gup, "lnp": lnp, "blk": blk64})
        rRO = _run(_prog('RO', build_RO), ins)
        del per_b, ins
        final = (l == DEPTH - 1)
        shM = prep_MF_shared(l, I, final)
        ins = []
        for c in range(8):
            b, s = c // 2, c % 2
            tk = _core_tok(s)
            ybT = np.concatenate([rGQ[2 * b]["yb"].reshape(512, TK), rGQ[2 * b + 1]["yb"].reshape(512, TK)], axis=0)[:, tk]
            ycT = np.concatenate([rNA[2 * b]["yc"].reshape(256, TK), rNA[2 * b + 1]["yc"].reshape(256, TK)], axis=0)[:, tk]
            yT = np.ascontiguousarray(np.concatenate([rRO[c]["ya"].reshape(512, TPC), ybT, ycT], axis=0))
            ins.append(dict(shM, xT=xT[c], yT=yT, cT=cT_of(I, b)))
        res = _run(_prog('MF1' if final else 'MF0', lambda: build_MF2(final)), ins)
        xT = [r["xo"] for r in res]
        del res, ins, shM, shP, ZA, ZB, rGQ, rNA, rRO
    out = np.empty((NB, SEQ, D), np.float32)
    for c in range(8):
        b, s = c // 2, c % 2
        out[b, s * 2048:(s + 1) * 2048] = xT[c][:, :2048].T
    return out
```
